# Optimizing a Trainium2 kernel written in Bass

```python
import math
import jax, jax.numpy as jnp
from jax import lax
import numpy as np

D_MODEL = 1024
BATCH = 16
SEQ = 4096
DEPTH = 2
DEC_BATCH = 2
DEC_SEQ = 8192
PAST_LEN = 128

N_MIXERS = 2
SSM_EXPAND = 2
D_INNER = SSM_EXPAND * D_MODEL
SSM_HEAD_DIM = 64
SSM_HEADS = D_INNER // SSM_HEAD_DIM
SSM_GROUPS = 8
SSM_STATE = 128
SSM_CONV = 3
SSM_CHUNK = 128
SSM_CONV_DIM = D_INNER + 2 * SSM_GROUPS * SSM_STATE
SSM_IN_DIM = D_INNER + SSM_CONV_DIM + 2 * SSM_HEADS
CF_DIM = D_MODEL
CF_CONV = 31
FFN_DIM = 2816
FFN_CONV = 3
EPS = 1e-6

kernel_name = "hybrid_bidir_ssd_conformer_convffn"


def _rmsnorm(x, w):
    xf = x.astype(jnp.float32)
    y = xf * lax.rsqrt(jnp.mean(xf * xf, axis=-1, keepdims=True) + EPS)
    return (y * w.astype(jnp.float32)).astype(x.dtype)


def _layernorm(x, w, b):
    xf = x.astype(jnp.float32)
    mu = jnp.mean(xf, axis=-1, keepdims=True)
    xc = xf - mu
    y = xc * lax.rsqrt(jnp.mean(xc * xc, axis=-1, keepdims=True) + EPS)
    return (y * w.astype(jnp.float32) + b.astype(jnp.float32)).astype(x.dtype)


def _dwconv(x, w, b):
    y = lax.conv_general_dilated(
        x, w[:, None, :].astype(x.dtype), window_strides=(1,), padding='SAME',
        dimension_numbers=('NWC', 'WIO', 'NWC'), feature_group_count=x.shape[-1])
    return y + b.astype(x.dtype)


def _ssd_scan(x, dt, A, B, C):
    b, L, H, P = x.shape
    G, N = B.shape[-2:]
    R = H // G
    Q = SSM_CHUNK
    nc = L // Q
    xdt = (x * dt[..., None]).reshape(b, nc, Q, G, R, P)
    dA = (dt * A).reshape(b, nc, Q, G, R)
    Bc = B.reshape(b, nc, Q, G, N)
    Cc = C.reshape(b, nc, Q, G, N)
    a_cs = jnp.cumsum(dA, axis=2)
    seg = a_cs[:, :, :, None] - a_cs[:, :, None, :]
    mask = jnp.tril(jnp.ones((Q, Q), dtype=bool))[:, :, None, None]
    decay = jnp.exp(jnp.where(mask, seg, -jnp.inf))
    cb = jnp.einsum('bctgn,bcsgn->bctsg', Cc, Bc)
    y_diag = jnp.einsum('bctsgr,bcsgrp->bctgrp', cb[..., None] * decay, xdt)
    decay_to_end = jnp.exp(a_cs[:, :, -1:] - a_cs)
    states = jnp.einsum('bcsgn,bcsgrp->bcgrpn', Bc, xdt * decay_to_end[..., None])
    chunk_decay = jnp.exp(a_cs[:, :, -1])

    def step(h, inp):
        s, d = inp
        return d[..., None, None] * h + s, h

    h0 = jnp.zeros((b, G, R, P, N), dtype=x.dtype)
    _, h_in = lax.scan(step, h0, (jnp.moveaxis(states, 1, 0), jnp.moveaxis(chunk_decay, 1, 0)))
    h_in = jnp.moveaxis(h_in, 0, 1)
    y_off = jnp.einsum('bctgn,bcgrpn->bctgrp', Cc, h_in) * jnp.exp(a_cs)[..., None]
    return (y_diag + y_off).reshape(b, L, H, P)


def _ssd_mixer(x, in_w, conv_w, conv_b, dt_bias, a_log, d_skip, norm_w, out_w):
    b, L, _ = x.shape
    G, N, H, P = SSM_GROUPS, SSM_STATE, SSM_HEADS, SSM_HEAD_DIM
    R = H // G
    proj = x @ in_w
    z = proj[..., :D_INNER]
    xbc = proj[..., D_INNER:D_INNER + SSM_CONV_DIM]
    dt_raw = proj[..., D_INNER + SSM_CONV_DIM:]
    xbc = jax.nn.silu(_dwconv(xbc, conv_w, conv_b)).astype(jnp.float32)
    xs = xbc[..., :D_INNER].reshape(b, L, H, P)
    Bm = xbc[..., D_INNER:D_INNER + G * N].reshape(b, L, G, N)
    Cm = xbc[..., D_INNER + G * N:].reshape(b, L, G, N)
    dt = jax.nn.softplus(dt_raw.astype(jnp.float32).reshape(b, L, 2, H) + dt_bias.astype(jnp.float32))
    A = -jnp.exp(a_log.astype(jnp.float32))
    dt_f, dt_b = dt[:, :, 0], dt[:, :, 1]
    y_f = _ssd_scan(xs, dt_f, A[0], Bm, Cm)
    y_b = jnp.flip(_ssd_scan(jnp.flip(xs, 1), jnp.flip(dt_b, 1), A[1],
                             jnp.flip(Bm, 1), jnp.flip(Cm, 1)), 1)
    cb_self = jnp.einsum('blgn,blgn->blg', Cm, Bm)
    diag = (cb_self[..., None, None] * (xs * dt_b[..., None]).reshape(b, L, G, R, P)).reshape(b, L, H, P)
    y = y_f + y_b - diag + d_skip.astype(jnp.float32)[:, None] * xs
    y = y.reshape(b, L, D_INNER) * jax.nn.silu(z.astype(jnp.float32))
    yg = y.reshape(b, L, G, D_INNER // G)
    yg = yg * lax.rsqrt(jnp.mean(yg * yg, axis=-1, keepdims=True) + EPS)
    y = (yg.reshape(b, L, D_INNER) * norm_w.astype(jnp.float32)).astype(x.dtype)
    return y @ out_w


def _conformer_conv(x, pw1_w, pw1_b, dw_w, dw_b, ln_w, ln_b, pw2_w, pw2_b):
    h = x @ pw1_w + pw1_b
    h = h[..., :CF_DIM] * jax.nn.sigmoid(h[..., CF_DIM:])
    h = _dwconv(h, dw_w, dw_b)
    h = jax.nn.silu(_layernorm(h, ln_w, ln_b))
    return h @ pw2_w + pw2_b


def _conv_ffn(x, up_w, conv_w, conv_b, down_w):
    u = _dwconv(x @ up_w, conv_w, conv_b)
    h = jax.nn.silu(u[..., :FFN_DIM]) * u[..., FFN_DIM:]
    return h @ down_w


def setup_inputs(seed: int = 0) -> dict:
    key = jax.random.key(seed)
    k = jax.random.split(key, 26)
    ns = (DEPTH + 1) // 2
    nc = DEPTH // 2

    def nrm(kk, shape, scale):
        return jax.random.normal(kk, shape, jnp.float32) * scale

    dt0 = jnp.exp(jax.random.uniform(k[5], (ns, 2, SSM_HEADS), jnp.float32,
                                     minval=math.log(1e-3), maxval=math.log(1e-1)))
    return {
        "x_prompt": nrm(k[0], (BATCH, SEQ, D_MODEL), 1.0),
        "x_sample": nrm(k[1], (DEC_BATCH, DEC_SEQ, D_MODEL), 1.0),
        "ssm_in_w": nrm(k[2], (ns, D_MODEL, SSM_IN_DIM), D_MODEL ** -0.5),
        "ssm_conv_w": nrm(k[3], (ns, SSM_CONV, SSM_CONV_DIM), SSM_CONV ** -0.5),
        "ssm_conv_b": nrm(k[4], (ns, SSM_CONV_DIM), 0.02),
        "ssm_dt_bias": dt0 + jnp.log(-jnp.expm1(-dt0)),
        "ssm_a_log": jnp.log(jax.random.uniform(k[6], (ns, 2, SSM_HEADS), jnp.float32, minval=1.0, maxval=16.0)),
        "ssm_d": 1.0 + nrm(k[7], (ns, SSM_HEADS), 0.1),
        "ssm_norm_w": 1.0 + nrm(k[8], (ns, D_INNER), 0.1),
        "ssm_out_w": nrm(k[9], (ns, D_INNER, D_MODEL), D_INNER ** -0.5),
        "cf_pw1_w": nrm(k[10], (nc, D_MODEL, 2 * CF_DIM), D_MODEL ** -0.5),
        "cf_pw1_b": nrm(k[11], (nc, 2 * CF_DIM), 0.02),
        "cf_dw_w": nrm(k[12], (nc, CF_CONV, CF_DIM), CF_CONV ** -0.5),
        "cf_dw_b": nrm(k[13], (nc, CF_DIM), 0.02),
        "cf_ln_w": 1.0 + nrm(k[14], (nc, CF_DIM), 0.1),
        "cf_ln_b": nrm(k[15], (nc, CF_DIM), 0.02),
        "cf_pw2_w": nrm(k[16], (nc, CF_DIM, D_MODEL), CF_DIM ** -0.5),
        "cf_pw2_b": nrm(k[17], (nc, D_MODEL), 0.02),
        "ffn_up_w": nrm(k[18], (DEPTH, D_MODEL, 2 * FFN_DIM), D_MODEL ** -0.5),
        "ffn_conv_w": nrm(k[19], (DEPTH, FFN_CONV, 2 * FFN_DIM), FFN_CONV ** -0.5),
        "ffn_conv_b": nrm(k[20], (DEPTH, 2 * FFN_DIM), 0.02),
        "ffn_down_w": nrm(k[21], (DEPTH, FFN_DIM, D_MODEL), FFN_DIM ** -0.5),
        "norm_pre_mix": 1.0 + nrm(k[22], (DEPTH, D_MODEL), 0.1),
        "norm_post_mix": 1.0 + nrm(k[23], (DEPTH, D_MODEL), 0.1),
        "norm_pre_ffn": 1.0 + nrm(k[24], (DEPTH, D_MODEL), 0.1),
        "norm_post_ffn": 1.0 + nrm(k[25], (DEPTH, D_MODEL), 0.1),
    }


def reference(x_prompt, x_sample, ssm_in_w, ssm_conv_w, ssm_conv_b, ssm_dt_bias, ssm_a_log,
              ssm_d, ssm_norm_w, ssm_out_w, cf_pw1_w, cf_pw1_b, cf_dw_w, cf_dw_b, cf_ln_w,
              cf_ln_b, cf_pw2_w, cf_pw2_b, ffn_up_w, ffn_conv_w, ffn_conv_b, ffn_down_w,
              norm_pre_mix, norm_post_mix, norm_pre_ffn, norm_post_ffn):
    def layer_stack(x):
        for i in range(DEPTH):
            j = i // N_MIXERS
            h = _rmsnorm(x, norm_pre_mix[i])
            if i % N_MIXERS == 0:
                h = _ssd_mixer(h, ssm_in_w[j], ssm_conv_w[j], ssm_conv_b[j], ssm_dt_bias[j],
                               ssm_a_log[j], ssm_d[j], ssm_norm_w[j], ssm_out_w[j])
            else:
                h = _conformer_conv(h, cf_pw1_w[j], cf_pw1_b[j], cf_dw_w[j], cf_dw_b[j],
                                    cf_ln_w[j], cf_ln_b[j], cf_pw2_w[j], cf_pw2_b[j])
            x = x + _rmsnorm(h, norm_post_mix[i])
            h = _conv_ffn(_rmsnorm(x, norm_pre_ffn[i]), ffn_up_w[i], ffn_conv_w[i],
                          ffn_conv_b[i], ffn_down_w[i])
            x = x + _rmsnorm(h, norm_post_ffn[i])
        return x

    y_prompt = layer_stack(x_prompt)
    y_sample = layer_stack(x_sample)
    return (y_prompt, y_sample)
```

```python
from contextlib import ExitStack
import numpy as np
import concourse.bass as bass
import concourse.mybir as mybir
from concourse.bass_utils import run_bass_kernel_spmd

F32 = mybir.dt.float32
BF16 = mybir.dt.bfloat16
AF = mybir.ActivationFunctionType
ALU = mybir.AluOpType
AX = mybir.AxisListType

ENGS = ("pe", "act", "dve", "pool", "sp")
D = 1024; DI = 2048; NH = 32; NG = 8; HP = 64; NS = 128; CD = 4096; FF = 2816
EPS = 1e-6
NCORES = 8


class Buf:
    __slots__ = ("t", "name", "lw", "rd", "dsem", "dcount")

    def __init__(self, t, name):
        self.t = t; self.name = name; self.lw = None; self.rd = []; self.dsem = None; self.dcount = 0

    def __getitem__(self, k):
        return self.t[k]


class View:
    __slots__ = ("t", "root")

    def __init__(self, ap, root):
        self.t = ap; self.root = root

    def __getitem__(self, k):
        return self.t[k]


def _roots(bufs):
    return [getattr(b, "root", b) for b in bufs]


class Op:
    __slots__ = ("eng", "fn", "deps", "semval", "needed", "dma")

    def __init__(self, eng, fn, deps, dma=None):
        self.eng = eng; self.fn = fn; self.deps = deps; self.semval = None; self.needed = False; self.dma = dma


class Sched:
    def __init__(self, nc):
        self.nc = nc
        self.ops = {e: [] for e in ENGS}
        self.all = []
        self.semfinal = {}
        self.sem_pool = {"hw": [], "sw": []}
        self.last = {e: None for e in ENGS}

    def _deps(self, eng, reads, writes):
        reads = _roots(reads); writes = _roots(writes)
        deps = []
        for b in reads:
            if b.lw is not None:
                deps.append((b.lw, "raw"))
        for b in writes:
            if b.lw is not None:
                deps.append((b.lw, "waw"))
            deps.extend((r, "war") for r in b.rd)
        out = []
        for d, kind in deps:
            if d[0] == "op" and d[1].eng == eng:
                if eng == "pe":
                    continue
                if kind != "raw":
                    continue
            out.append(d)
        return out

    def op(self, eng, fn, reads=(), writes=()):
        reads = _roots(reads); writes = _roots(writes)
        o = Op(eng, fn, self._deps(eng, reads, writes))
        self.all.append(o); self.ops[eng].append(o); self.last[eng] = o
        tok = ("op", o)
        for b in reads:
            b.rd.append(tok)
        for b in writes:
            b.lw = tok; b.rd = []
        return o

    def _dsem(self, b, q):
        qc = "sw" if q == "pool" else "hw"
        if b.dsem is None:
            b.dsem = {}
        if qc not in b.dsem:
            if self.sem_pool[qc]:
                b.dsem[qc] = list(self.sem_pool[qc].pop())
            else:
                b.dsem[qc] = [self.nc.alloc_semaphore("d%s_%s" % (qc, b.name)), 0]
        return b.dsem[qc]

    def release(self, bufs):
        for b in bufs:
            if b.dsem:
                for qc, (sm, cnt) in b.dsem.items():
                    self.sem_pool[qc].append((sm, cnt))
                b.dsem = None

    def dma_load(self, q, dst, fn, n=1):
        deps = self._deps(q, [], [dst])
        ent = self._dsem(dst, q)
        ent[1] += 16 * n
        sem, cnt = ent
        self.semfinal[sem.num] = (sem, cnt)
        o = Op(q, fn, deps, dma=sem)
        self.all.append(o); self.ops[q].append(o)
        dst.lw = ("dma", sem, cnt); dst.rd = []
        return o

    def dma_store(self, q, src, fn, n=1):
        deps = self._deps(q, [src], [])
        ent = self._dsem(src, q)
        ent[1] += 16 * n
        sem, cnt = ent
        self.semfinal[sem.num] = (sem, cnt)
        o = Op(q, fn, deps, dma=sem)
        self.all.append(o); self.ops[q].append(o)
        src.rd.append(("dma", sem, cnt))
        return o

    def barrier(self):
        toks = [("op", self.last[e]) for e in ENGS if self.last[e] is not None and self.last[e].dma is None]
        toks += [("dma", sm, v) for (sm, v) in self.semfinal.values()]
        for e in ENGS:
            o = Op(e, None, list(toks))
            self.all.append(o); self.ops[e].append(o)

    def emit(self):
        nc = self.nc
        for o in self.all:
            for d in o.deps:
                if d[0] == "op":
                    d[1].needed = True
        esem = {e: nc.alloc_semaphore("e_" + e) for e in ENGS}
        for e in ENGS:
            c = 0
            for o in self.ops[e]:
                if o.needed:
                    c += 1; o.semval = c
        finals = list(self.semfinal.values())

        def run(e, engine):
            waited = {}
            for o in self.ops[e]:
                need = {}
                for d in o.deps:
                    if d[0] == "op":
                        if d[1].eng == e and e == "pe":
                            continue
                        s, v = esem[d[1].eng], d[1].semval
                    else:
                        s, v = d[1], d[2]
                    if need.get(s.num, (None, 0))[1] < v:
                        need[s.num] = (s, v)
                for k, (s, v) in need.items():
                    if waited.get(k, 0) < v:
                        engine.wait_ge(s, v); waited[k] = v
                if o.fn is None:
                    continue
                if o.dma is not None:
                    o.fn(engine, o.dma)
                else:
                    ins = o.fn(engine)
                    if o.needed:
                        ins.then_inc(esem[e], 1)
            if e == "sp":
                for (s, v) in finals:
                    if waited.get(s.num, 0) < v:
                        engine.wait_ge(s, v)

        with nc.Block() as block:
            block.tensor(lambda eng: run("pe", eng))
            block.scalar(lambda eng: run("act", eng))
            block.vector(lambda eng: run("dve", eng))
            block.gpsimd(lambda eng: run("pool", eng))
            block.sync(lambda eng: run("sp", eng))


class Rot:
    def __init__(self, bufs):
        self.b = bufs; self.i = 0

    def next(self):
        r = self.b[self.i % len(self.b)]; self.i += 1
        return r


class Builder:
    def __init__(self, NB, dbg=()):
        self.NB = NB; self.T = NB * 512; self.NCH = self.T // 128
        self.dbg = dbg
        self.nc = bass.Bass("TRN2", target_bir_lowering=False)
        self.S = Sched(self.nc)
        self.ein = {}
        self.cnt = 0
        self.cur_bufs = []

    def end_phase(self):
        self.S.barrier()
        self.S.release(self.cur_bufs)
        self.cur_bufs = []

    def din(self, name, shape, dt=F32):
        self.ein[name] = self.nc.dram_tensor(name, list(shape), dt, kind="ExternalInput").ap()
        return self.ein[name]

    def dscr(self, name, shape, dt):
        kind = "ExternalOutput" if name in self.dbg else "Internal"
        return self.nc.dram_tensor(name, list(shape), dt, kind=kind).ap()

    def sb(self, es, name, shape, dt):
        self.cnt += 1
        t = es.enter_context(self.nc.sbuf_tensor("%s_%d" % (name, self.cnt), list(shape), dt))
        b = Buf(t, "%s_%d" % (name, self.cnt))
        self.cur_bufs.append(b)
        return b

    def sb_split(self, es, name, ncols, dt, nparts):
        whole = self.sb(es, name, [128, ncols], dt)
        w = ncols // nparts
        parts = [Buf(whole.t[:, i * w:(i + 1) * w], "%s_p%d" % (whole.name, i)) for i in range(nparts)]
        self.cur_bufs.extend(parts)
        return whole.t, parts

    def ps(self, es, name, shape, dt=F32):
        self.cnt += 1
        t = es.enter_context(self.nc.psum_tensor("%s_%d" % (name, self.cnt), list(shape), dt))
        return Buf(t, "%s_%d" % (name, self.cnt))

    def mm(self, ob, o, lb, l, rb, r, start, stop):
        self.S.op("pe", lambda e: e.matmul(o, l, r, start=start, stop=stop), reads=[lb, rb], writes=[ob])

    def tr(self, ob, o, ib, i, idb, idn):
        self.S.op("pe", lambda e: e.transpose(o, i, idn), reads=[ib, idb], writes=[ob])

    def act(self, ob, o, ib, i, func, bias=None, scale=1.0, accum=None, extra=(), eng="act"):
        kw = {}
        if bias is not None:
            kw["bias"] = bias
        if accum is not None:
            kw["accum_out"] = accum[1]
        wr = [ob] + ([accum[0]] if accum is not None else [])
        self.S.op("act", lambda e: e.activation(o, i, func, scale=scale, **kw), reads=[ib] + list(extra), writes=wr)

    def copy(self, eng, ob, o, ib, i):
        if eng == "act":
            self.S.op("act", lambda e: e.copy(o, i), reads=[ib], writes=[ob])
        else:
            self.S.op(eng, lambda e: e.tensor_copy(o, i), reads=[ib], writes=[ob])

    def tt(self, eng, ob, o, ab, a, bb, b_, op):
        self.S.op(eng, lambda e: e.tensor_tensor(o, a, b_, op), reads=[ab, bb], writes=[ob])

    def ts(self, ob, o, ib, i, s1, s2, op0, op1=None, extra=()):
        if op1 is None:
            self.S.op("dve", lambda e: e.tensor_scalar(o, i, s1, None, op0), reads=[ib] + list(extra), writes=[ob])
        else:
            self.S.op("dve", lambda e: e.tensor_scalar(o, i, s1, s2, op0, op1), reads=[ib] + list(extra), writes=[ob])

    def stt(self, ob, o, ab, a, sb_, s, bb, b_, op0, op1):
        rd = [ab, bb] + ([sb_] if sb_ is not None else [])
        self.S.op("dve", lambda e: e.scalar_tensor_tensor(o, a, s, b_, op0, op1), reads=rd, writes=[ob])

    def load(self, dst, o, src, q="sp"):
        self.S.dma_load(q, dst, lambda e, s: e.dma_start(out=o, in_=src).then_inc(s, 16))

    def store(self, srcb, i, dst, q="sp"):
        self.S.dma_store(q, srcb, lambda e, s: e.dma_start(out=dst, in_=i).then_inc(s, 16))

    def setup_consts(self, es):
        c = self.din("consts", [128, 6 * 128])
        self.cf = self.sb(es, "cf", [128, 6, 128], F32)
        self.cb = self.sb(es, "cbf", [128, 6, 128], BF16)
        self.load(self.cf, self.cf[:], c.rearrange("p (a b) -> p a b", a=6))
        self.copy("dve", self.cb, self.cb[:], self.cf, self.cf[:])
        self.fl = self.sb(es, "fl", [128, self.NB * 2], F32)
        self.load(self.fl, self.fl[:], self.din("flags", [128, self.NB * 2]))
        self.epsb = self.sb(es, "epsb", [128, 1], F32)
        self.S.op("pool", lambda e: e.memset(self.epsb[:], EPS), writes=[self.epsb])

    def load_weight(self, es, dst, dcol0, src, K, N, stage, rowscale=None):
        engs = ["act", "dve", "pool"] if rowscale is None else ["act", "dve"]
        i = 0
        for k in range(K // 128):
            for c0 in range(0, N, 1024):
                n = min(1024, N - c0)
                st = stage.next()
                self.load(st, st[:, 0:n], src[k * 128:(k + 1) * 128, c0:c0 + n])
                o = dst[:, k, dcol0 + c0:dcol0 + c0 + n]
                e = engs[i % len(engs)]; i += 1
                if rowscale is None:
                    self.copy(e, dst, o, st, st[:, 0:n])
                else:
                    rb, r = rowscale
                    self.ts(dst, o, st, st[:, 0:n], r[:, k:k + 1], None, ALU.mult, extra=[rb])

    def make_diag(self, es, diag, cw, n):
        for i in range(n):
            if i % 2 == 0:
                self.ts(diag, diag[:, i, :], self.cb, self.cb[:, 0, :], cw[:, i:i + 1], None, ALU.mult, extra=[cw])
            else:
                self.S.op("act", lambda e, i=i: e.activation(diag[:, i, :], self.cb[:, 0, :], AF.Copy, scale=cw[:, i:i + 1]),
                          reads=[self.cb, cw], writes=[diag])

    def rms_tile(self, xb, npart, wbc, hn_tok, small, junk):
        ss = small.next(); lg = small.next(); rs = small.next()
        self.act(junk, junk[0:npart, 0:1024], xb, xb[0:npart, :], AF.Square, accum=(ss, ss[0:npart, 0:1]))
        self.act(lg, lg[0:npart, :], ss, ss[0:npart, :], AF.Ln, bias=self.epsb[0:npart, 0:1], scale=1.0 / D, extra=[self.epsb])
        self.act(rs, rs[0:npart, :], lg, lg[0:npart, :], AF.Exp, scale=-0.5)
        self.stt(hn_tok, hn_tok[0:npart, :], xb, xb[0:npart, :], rs, rs[0:npart, 0:1], wbc, wbc[0:npart, :], ALU.mult, ALU.mult)

    def phase1(self, x_d, w_in, cw_d, cb_d, dtb_d, nw_d, scr):
        S = self.S; NB = self.NB; T = self.T
        with ExitStack() as es:
            W = self.sb(es, "W1", [128, 8, 4160], BF16)
            stage = Rot([self.sb(es, "stg", [128, 1024], F32) for _ in range(3)])
            self.load_weight(es, W, 0, w_in[:, 2048:6208], 1024, 4160, stage)
            cw = self.sb(es, "cw", [128, 96], F32); self.load(cw, cw[:], cw_d)
            cbias = self.sb(es, "cbias", [128, 32], F32); self.load(cbias, cbias[:], cb_d)
            dtb = self.sb(es, "dtb", [128, 1], F32); self.load(dtb, dtb[0:64, :], dtb_d)
            wbc = self.sb(es, "wbc", [128, 1024], F32); self.load(wbc, wbc[:], nw_d.partition_broadcast(128))
            diag = self.sb(es, "diag", [128, 96, 128], BF16)
            self.make_diag(es, diag, cw, 96)
            xt = Rot([self.sb(es, "xt", [128, 1024], F32) for _ in range(6)])
            hnt = Rot([self.sb(es, "hnt", [128, 1024], BF16) for _ in range(5)])
            junk = self.sb(es, "junk", [128, 1024], BF16)
            small = Rot([self.sb(es, "sm", [128, 1], F32) for _ in range(12)])
            hnr = Rot([self.sb(es, "hn", [128, 8, 514], BF16) for _ in range(2)])
            ur = Rot([self.sb(es, "u", [128, 514], BF16) for _ in range(3)])
            xgr = Rot([self.sb(es, "xg", [128, 8, 512], BF16) for _ in range(2)])
            tokr = Rot([self.sb(es, "tok", [128, 4, 1024], BF16) for _ in range(2)])
            dte = self.sb(es, "dte", [128, 512], F32)
            dtT = self.sb(es, "dtT", [128, 512], F32)
            dtk = Rot([self.sb(es, "dtk", [128, 4, 64], F32) for _ in range(2)])
            pTr = Rot([self.ps(es, "pT", [128, 8, 128], BF16) for _ in range(2)])
            pmr = Rot([self.ps(es, "pm", [128, 512], F32) for _ in range(2)])
            pcr = Rot([self.ps(es, "pc", [128, 512], F32) for _ in range(2)])
            phbs = [self.ps(es, "ph", [128, 512], F32) for _ in range(2)]
            phr = Rot([View(pb[:, 0:2], pb) for pb in phbs])
            pdr = Rot([View(pb[:, 0:64], pb) for pb in pmr.b])
            ev = 0
            hn_next = hnr.next()
            self.norm_block(x_d, 0, 1, wbc, hn_next, xt, hnt, small, junk, pTr)
            for b in range(NB):
                t0 = b * 512
                hn = hn_next
                nxt = None
                us = {}
                xgs = {}

                def projA(m):
                    pm = pmr.next(); ph = phr.next()
                    for k in range(8):
                        self.mm(pm, pm[:], W, W[:, k, 128 * m:128 * (m + 1)], hn, hn[:, k, 1:513], k == 0, k == 7)
                    for k in range(8):
                        self.mm(ph, ph[:], W, W[:, k, 128 * m:128 * (m + 1)], hn, hn[:, k, 0:514:513], k == 0, k == 7)
                    u = ur.next()
                    self.copy("act" if m % 2 else "dve", u, u[:, 1:513], pm, pm[:])
                    self.tt("dve", u, u[:, 0:514:513], ph, ph[:], self.fl, self.fl[:, 2 * b:2 * b + 2], ALU.mult)
                    us[m] = u

                def convB(m):
                    G, i = divmod(m, 8)
                    if i == 0:
                        xgs[G] = xgr.next()
                    xg = xgs[G]; u = us.pop(m)
                    pc = pcr.next()
                    for k in range(3):
                        self.mm(pc, pc[:], diag, diag[:, k * 32 + m, :], u, u[:, k:k + 512], k == 0, k == 2)
                    self.act(xg, xg[:, i, :], pc, pc[:], AF.Silu, bias=cbias[:, m:m + 1], extra=[cbias])
                    if i < 7:
                        return
                    if G < 3:
                        tok = tokr.next()
                        for j in range(4):
                            pT = pTr.next()
                            for ii in range(8):
                                self.tr(pT, pT[:, ii, :], xg, xg[:, ii, 128 * j:128 * (j + 1)], self.cb, self.cb[:, 0, :])
                            self.copy("act" if j % 2 else "dve", tok, tok[:, j, :], pT, pT[:].rearrange("p a b -> p (a b)"))
                        if G < 2:
                            dst = scr["xs"][t0:t0 + 512, G * 1024:(G + 1) * 1024]
                        else:
                            dst = scr["Bt"][t0:t0 + 512, :]
                        self.store(tok, tok[:], dst.rearrange("(j p) c -> p j c", p=128), q="pool")
                    if G == 2:
                        self.store(xg, xg[:], scr["BT"][:, :, t0:t0 + 512], q="pool")
                    if G == 3:
                        self.store(xg, xg[:], scr["CT"][:, :, t0:t0 + 512], q="pool")

                projA(0)
                for m in range(1, 32):
                    projA(m)
                    convB(m - 1)
                    if m == 24 and b + 1 < NB:
                        nxt = self.norm_a(x_d, b + 1, 1, wbc, xt, hnt, small, junk)
                convB(31)
                pm = pmr.next()
                for k in range(8):
                    self.mm(pm, pm[0:64, :], W, W[:, k, 4096:4160], hn, hn[:, k, 1:513], k == 0, k == 7)
                self.act(dte, dte[0:64, :], pm, pm[0:64, :], AF.Exp, bias=dtb[0:64, 0:1], extra=[dtb])
                self.act(dtT, dtT[0:64, :], dte, dte[0:64, :], AF.Ln, bias=1.0)
                dk = dtk.next()
                for j in range(4):
                    pd = pdr.next()
                    self.tr(pd, pd[:], dtT, dtT[0:64, 128 * j:128 * (j + 1)], self.cf, self.cf[0:64, 0, 0:64])
                    self.copy("dve", dk, dk[:, j, :], pd, pd[:])
                self.store(dk, dk[:], scr["dt"][t0:t0 + 512, :].rearrange("(j p) c -> p j c", p=128), q="pool")
                if nxt is not None:
                    hn_next = hnr.next()
                    self.norm_b(nxt, 1, hn_next, pTr)
        self.end_phase()

    def sweep(self, final, scr, alog_d, x_d=None, xout_d=None, w_in=None, w_out=None, d_d=None, gnw_d=None,
              npre_d=None, npost_d=None):
        S = self.S; NB = self.NB; T = self.T; NCH = self.NCH
        d0 = 0 if final else 32
        qi, ei = (1, 2) if final else (3, 4)
        mk_u = 1 if final else 3
        mk_l = 2 if final else 4
        mk_c = 1 if final else 2
        with ExitStack() as es:
            if final:
                Wo = self.sb(es, "Wo", [128, 16, 1024], BF16)
                gnw = self.sb(es, "gnw", [128, 16], F32); self.load(gnw, gnw[:], gnw_d)
                with ExitStack() as es2:
                    stage = Rot([self.sb(es2, "stg", [128, 1024], F32) for _ in range(3)])
                    self.load_weight(es2, Wo, 0, w_out, 2048, 1024, stage, rowscale=(gnw, gnw))
                    S.barrier()
                wpo = self.sb(es, "wpo", [128, 1024], F32); self.load(wpo, wpo[:], npost_d.partition_broadcast(128))
            else:
                Dbc = self.sb(es, "Dbc", [128, 32], F32); self.load(Dbc, Dbc[:], d_d.partition_broadcast(128))
                dx = self.sb(es, "dx", [128, 2048], F32)
                Wz = self.sb(es, "Wz", [128, 8, 2048], BF16)
                with ExitStack() as es2:
                    stage = Rot([self.sb(es2, "stg", [128, 1024], F32) for _ in range(3)])
                    self.load_weight(es2, Wz, 0, w_in[:, 0:2048], 1024, 2048, stage)
                    S.barrier()
                wbc = self.sb(es, "wbc", [128, 1024], F32); self.load(wbc, wbc[:], npre_d.partition_broadcast(128))
            Abc = self.sb(es, "Abc", [128, 32], F32)
            self.load(Abc, Abc[:], alog_d.partition_broadcast(128))
            self.act(Abc, Abc[:], Abc, Abc[:], AF.Exp)
            self.ts(Abc, Abc[:], Abc, Abc[:], -1.0, None, ALU.mult)
            nl = 2
            xsr = Rot([self.sb(es, "xs", [128, 2048], BF16) for _ in range(nl)])
            btr = Rot([self.sb(es, "bt", [128, 1024], BF16) for _ in range(nl)])
            BTr = Rot([self.sb(es, "BT", [128, 8, 128], BF16) for _ in range(nl)])
            CTr = Rot([self.sb(es, "CT", [128, 8, 128], BF16) for _ in range(nl)])
            dtr = Rot([self.sb(es, "dt", [128, 64], F32) for _ in range(nl)])
            x0r = Rot([self.sb(es, "x0", [128, 1024], F32) for _ in range(nl)])
            sm32 = Rot([self.sb(es, "s32", [128, 32], F32) for _ in range(9)])
            scr_ = Rot([self.sb(es, "sc", [128, 96], F32) for _ in range(3)])
            rhsUr = Rot([self.sb_split(es, "rhsU", 4096, BF16, 2) for _ in range(1)])
            MTar = Rot([self.sb_split(es, "MTall", 4096, BF16, 8) for _ in range(2)])
            ydr = Rot([self.sb_split(es, "yd", 2048, F32, 4) for _ in range(2)])
            cbmr = Rot([self.sb_split(es, "cbm", 1024, BF16, 2) for _ in range(2)])
            xdtr = Rot([self.sb(es, "xdt", [128, 2048], BF16) for _ in range(2)])
            xdter = Rot([self.sb(es, "xdte", [128, 2048], BF16) for _ in range(2)])
            Er = Rot([self.sb(es, "E", [128, 512], BF16) for _ in range(2)])
            h, hP = self.sb_split(es, "h", 2048, F32, 4)
            hbf, hbP = self.sb_split(es, "hbf", 2048, BF16, 4)
            S.op("pool", lambda e: e.memset(h[:], 0.0), writes=hP)
            S.op("pool", lambda e: e.memset(hbf[:], 0.0), writes=hbP)
            tmpr = Rot([self.sb(es, "tmp", [128, 512], F32) for _ in range(2)])
            br = Rot([self.sb(es, "b", [128, 512], F32) for _ in range(2)])
            yaccr = Rot([self.sb_split(es, "yacc", 2048, F32, 4) for _ in range(1 if final else 2)])
            junk = self.sb(es, "junk", [128, 1024], BF16)
            small = Rot([self.sb(es, "sm", [128, 1], F32) for _ in range(12)])
            if final:
                ybr = Rot([self.sb(es, "yb", [128, 2048], BF16) for _ in range(nl)])
                zsr = Rot([self.sb(es, "zs", [128, 2048], BF16) for _ in range(nl)])
                yn = self.sb(es, "yn", [128, 2048], BF16)
                ynT = self.sb(es, "ynT", [128, 16, 128], BF16)
                ho = self.sb(es, "ho", [128, 1024], F32)
                xo = Rot([self.sb(es, "xo", [128, 1024], F32) for _ in range(2)])
                sm8 = Rot([self.sb(es, "s8", [128, 8], F32) for _ in range(6)])
            else:
                ybsr = Rot([self.sb(es, "ybs", [128, 2048], BF16) for _ in range(2)])
                hnt = self.sb(es, "hnt", [128, 1024], BF16)
                hnT = self.sb(es, "hnT", [128, 8, 128], BF16)
                zsr = Rot([self.sb(es, "zso", [128, 2048], BF16) for _ in range(2)])
            pg = Rot([self.ps(es, "pg", [128, 512], F32) for _ in range(6)])
            pT = self.ps(es, "pT", [128, 8, 128], BF16)
            psm = self.ps(es, "psm", [128, 512], F32)
            psr = Rot([View(psm[:, 96 * i:96 * (i + 1)], psm) for i in range(4)])
            ones1 = self.cf[:, 5, 0:1]
            order = list(range(NCH)) if final else list(range(NCH - 1, -1, -1))
            v3 = lambda ap: ap.rearrange("p (h d) -> p h d", h=8)

            def prep(c, P):
                b = c // 4; tk = c * 128
                xs = xsr.next(); bt = btr.next(); BT = BTr.next(); CT = CTr.next(); dt = dtr.next(); x0 = x0r.next()
                self.load(xs, xs[:], scr["xs"][tk:tk + 128, :])
                self.load(bt, bt[:], scr["Bt"][tk:tk + 128, :])
                self.load(BT, BT[:], scr["BT"][:, :, tk:tk + 128])
                self.load(CT, CT[:], scr["CT"][:, :, tk:tk + 128])
                self.load(dt, dt[:], scr["dt"][tk:tk + 128, :])
                self.load(x0, x0[:], x_d[tk:tk + 128, :])
                if final:
                    yb = ybr.next(); self.load(yb, yb[:], scr["yb"][tk:tk + 128, :])
                    zs = zsr.next(); self.load(zs, zs[:], scr["zs"][tk:tk + 128, :])
                    P["yb"] = yb; P["zs"] = zs
                    flag = self.fl[:, 2 * b + 1:2 * b + 2] if c % 4 == 3 else ones1
                else:
                    flag = self.fl[:, 2 * b:2 * b + 1] if c % 4 == 0 else ones1
                yield
                dtd = dt[:, d0:d0 + 32]
                dA = sm32.next()
                self.tt("dve", dA, dA[:], dt, dtd, Abc, Abc[:], ALU.mult)
                ps_ = psr.next()
                self.mm(ps_, ps_[:, 0:32], self.cf, self.cf[:, qi, :], dA, dA[:], True, True)
                self.mm(ps_, ps_[:, 32:64], self.cf, self.cf[:, ei, :], dA, dA[:], True, True)
                self.mm(ps_, ps_[:, 64:96], self.cf, self.cf[:, 5, :], dA, dA[:], True, True)
                sc = scr_.next()
                self.act(sc, sc[:], ps_, ps_[:], AF.Exp)
                w2 = sm32.next(); cdf = sm32.next()
                self.tt("dve", w2, w2[:], dt, dtd, sc, sc[:, 32:64], ALU.mult)
                self.ts(cdf, cdf[:], sc, sc[:, 64:96], flag, None, ALU.mult, extra=[self.fl, self.cf])
                yield
                xdt = xdtr.next(); xdte = xdter.next(); rhsU, rhsUP = rhsUr.next(); cbm, cbmP = cbmr.next()
                xs3 = xs[:].rearrange("p (h d) -> p h d", h=32)
                self.tt("pool", xdt, xdt[:].rearrange("p (h d) -> p h d", h=32), xs, xs3, dt,
                        dtd.unsqueeze(2).broadcast_to([128, 32, 64]), ALU.mult)
                self.tt("dve", xdte, xdte[:].rearrange("p (h d) -> p h d", h=32), xs, xs3, w2,
                        w2[:].unsqueeze(2).broadcast_to([128, 32, 64]), ALU.mult)
                yield
                for hf in range(2):
                    self.tt("pool", rhsUP[hf], rhsU[:, hf * 2048:(hf + 1) * 2048].rearrange("p (h t) -> p h t", h=16),
                            self.cf, self.cf[:, mk_u, :].unsqueeze(1).broadcast_to([128, 16, 128]),
                            dA, dA[:, hf * 16:(hf + 1) * 16].unsqueeze(2).broadcast_to([128, 16, 128]), ALU.mult)
                yield
                for hf in range(2):
                    pcb = pg.next()
                    for gg in range(4):
                        g = hf * 4 + gg
                        self.mm(pcb, pcb[:, gg * 128:(gg + 1) * 128], BT, BT[:, g, :], CT, CT[:, g, :], True, True)
                    self.tt("dve", cbmP[hf], cbm[:, hf * 512:(hf + 1) * 512].rearrange("p (g t) -> p g t", g=4),
                            pcb, pcb[:].rearrange("p (g t) -> p g t", g=4),
                            self.cb, self.cb[:, mk_c, :].unsqueeze(1).broadcast_to([128, 4, 128]), ALU.mult)
                yield
                MTa, MTP = MTar.next(); yd, ydP = ydr.next()

                def segA(g):
                    pseg = pg.next()
                    self.mm(pseg, pseg[:], self.cb, self.cb[:, mk_l, :], rhsUP[g // 4], rhsU[:, g * 512:(g + 1) * 512], True, True)
                    E = Er.next()
                    self.act(E, E[:], pseg, pseg[:], AF.Exp)
                    self.tt("dve", MTP[g], MTa[:, g * 512:(g + 1) * 512].rearrange("p (h t) -> p h t", h=4), E, E[:].rearrange("p (h t) -> p h t", h=4),
                            cbmP[g // 4], cbm[:, g * 128:(g + 1) * 128].unsqueeze(1).broadcast_to([128, 4, 128]), ALU.mult)

                pys = {}

                def ydB(g):
                    pr, gg = divmod(g, 2)
                    if gg == 0:
                        pys[pr] = pg.next()
                    py = pys[pr]
                    for hh in range(4):
                        hd = 4 * g + hh
                        self.mm(py, py[:, (gg * 4 + hh) * 64:(gg * 4 + hh + 1) * 64], MTP[g], MTa[:, hd * 128:(hd + 1) * 128],
                                xdt, xdt[:, hd * 64:(hd + 1) * 64], True, True)
                    if gg == 0:
                        return
                    c0 = pr * 512; h0 = pr * 8
                    if final:
                        self.tt("dve", ydP[pr], yd[:, c0:c0 + 512], py, py[:], yb, yb[:, c0:c0 + 512], ALU.add)
                    else:
                        self.copy("act", ydP[pr], yd[:, c0:c0 + 512], py, py[:])

                segA(0); segA(1)
                yield
                for g in range(2, 8):
                    segA(g)
                    ydB(g - 2)
                    yield
                ydB(6); ydB(7)
                P.update(xs=xs, bt=bt, CT=CT, sc=sc, cdf=cdf, flag=flag, xdte=xdte, yd=yd, ydP=ydP, x0=x0, tk=tk)

            def main(c, P):
                bt = P["bt"]; CT = P["CT"]; sc = P["sc"]; cdf = P["cdf"]; flag = P["flag"]
                xdte = P["xdte"]; yd = P["yd"]; ydP = P["ydP"]; x0 = P["x0"]; tk = P["tk"]
                yacc, yaP = yaccr.next()
                for pr in range(4):
                    pyo = pg.next(); pS = pg.next()
                    for gg in range(2):
                        g = 2 * pr + gg
                        self.mm(pyo, pyo[:, gg * 256:(gg + 1) * 256], CT, CT[:, g, :], hbP[pr], hbf[:, g * 256:(g + 1) * 256], True, True)
                    for gg in range(2):
                        g = 2 * pr + gg
                        self.mm(pS, pS[:, gg * 256:(gg + 1) * 256], bt, bt[:, g * 128:(g + 1) * 128],
                                xdte, xdte[:, g * 256:(g + 1) * 256], True, True)
                    c0 = pr * 512; h0 = pr * 8
                    tmp = tmpr.next()
                    self.tt("dve", tmp, v3(tmp[:]), hP[pr], v3(h[:, c0:c0 + 512]), cdf, cdf[:, h0:h0 + 8].unsqueeze(2).broadcast_to([128, 8, 64]), ALU.mult)
                    self.stt(hP[pr], h[:, c0:c0 + 512], pS, pS[:], None, flag, tmp, tmp[:], ALU.mult, ALU.add)
                    S.ops["dve"][-1].deps.extend(self.S._deps("dve", [self.fl, self.cf], []))
                    self.copy("act", hbP[pr], hbf[:, c0:c0 + 512], hP[pr], h[:, c0:c0 + 512])
                    bq = br.next()
                    self.tt("dve", bq, v3(bq[:]), pyo, v3(pyo[:]), sc, sc[:, h0:h0 + 8].unsqueeze(2).broadcast_to([128, 8, 64]), ALU.mult)
                    self.tt("pool", yaP[pr], yacc[:, c0:c0 + 512], ydP[pr], yd[:, c0:c0 + 512], bq, bq[:], ALU.add)
                    yield
                if not final:
                    ybs = ybsr.next(); xs = P["xs"]
                    S.op("pool", lambda e, xs=xs: e.tensor_tensor(dx[:].rearrange("p (h d) -> p h d", h=32), xs[:].rearrange("p (h d) -> p h d", h=32),
                         Dbc[:].unsqueeze(2).broadcast_to([128, 32, 64]), ALU.mult), reads=[xs, Dbc], writes=[dx])
                    S.op("pool", lambda e, ybs=ybs, yacc=yacc: e.tensor_tensor(ybs[:], yacc[:], dx[:], ALU.add), reads=yaP + [dx], writes=[ybs])
                    self.store(ybs, ybs[:], scr["yb"][tk:tk + 128, :], q="pool")
                    yield
                    self.rms_tile(x0, 128, wbc, hnt, small, junk)
                    for k in range(8):
                        self.tr(pT, pT[:, k, :], hnt, hnt[:, k * 128:(k + 1) * 128], self.cb, self.cb[:, 0, :])
                    self.copy("dve", hnT, hnT[:], pT, pT[:])
                    yield
                    zs = zsr.next()
                    for q in range(4):
                        pz = pg.next()
                        for k in range(8):
                            self.mm(pz, pz[:], hnT, hnT[:, k, :], Wz, Wz[:, k, 512 * q:512 * (q + 1)], k == 0, k == 7)
                        self.act(zs, zs[:, 512 * q:512 * (q + 1)], pz, pz[:], AF.Silu)
                        yield
                    self.store(zs, zs[:], scr["zs"][tk:tk + 128, :], q="pool")
                    return
                zs = P["zs"]
                S.op("dve", lambda e, yacc=yacc, zs=zs: e.tensor_tensor(yacc[:], yacc[:], zs[:], ALU.mult), reads=yaP + [zs], writes=yaP)
                yield
                ssg = sm8.next(); lg = sm8.next(); rg = sm8.next()
                for g in range(8):
                    self.act(yn, yn[:, g * 256:(g + 1) * 256], yaP[g // 2], yacc[:, g * 256:(g + 1) * 256], AF.Square,
                             accum=(ssg, ssg[:, g:g + 1]))
                self.act(lg, lg[:], ssg, ssg[:], AF.Ln, bias=self.epsb[:, 0:1], scale=1.0 / 256, extra=[self.epsb])
                self.act(rg, rg[:], lg, lg[:], AF.Exp, scale=-0.5)
                yield
                S.op("dve", lambda e, yacc=yacc, rg=rg: e.tensor_tensor(yn[:].rearrange("p (g d) -> p g d", g=8), yacc[:].rearrange("p (g d) -> p g d", g=8),
                     rg[:].unsqueeze(2).broadcast_to([128, 8, 256]), ALU.mult), reads=yaP + [rg], writes=[yn])
                for hf in range(2):
                    for k in range(8):
                        kk = hf * 8 + k
                        self.tr(pT, pT[:, k, :], yn, yn[:, kk * 128:(kk + 1) * 128], self.cb, self.cb[:, 0, :])
                    self.copy("act" if hf else "dve", ynT, ynT[:, hf * 8:(hf + 1) * 8, :], pT, pT[:])
                    yield
                for q in range(2):
                    po = pg.next()
                    for kk in range(16):
                        self.mm(po, po[:], ynT, ynT[:, kk, :], Wo, Wo[:, kk, 512 * q:512 * (q + 1)], kk == 0, kk == 15)
                    self.copy("act" if q else "dve", ho, ho[:, 512 * q:512 * (q + 1)], po, po[:])
                    yield
                self.post_norm_residual(ho, x0, wpo, xo.next(), junk, small, xout_d[tk:tk + 128, :])

            P = {}
            for _ in prep(order[0], P):
                pass
            for idx, c in enumerate(order):
                Pn = {}
                gens = [main(c, P)]
                if idx + 1 < len(order):
                    gens.append(prep(order[idx + 1], Pn))
                while gens:
                    for gnr in list(gens):
                        try:
                            next(gnr)
                        except StopIteration:
                            gens.remove(gnr)
                P = Pn
        self.end_phase()

    def post_norm_residual(self, hb, x0, wpo, xo, junk, small, dst):
        ss = small.next(); lg = small.next(); rs = small.next()
        self.act(junk, junk[:, 0:1024], hb, hb[:], AF.Square, accum=(ss, ss[:, 0:1]))
        self.act(lg, lg[:], ss, ss[:], AF.Ln, bias=self.epsb[:, 0:1], scale=1.0 / D, extra=[self.epsb])
        self.act(rs, rs[:], lg, lg[:], AF.Exp, scale=-0.5)
        self.stt(xo, xo[:], hb, hb[:], rs, rs[:, 0:1], wpo, wpo[:], ALU.mult, ALU.mult)
        self.tt("pool", xo, xo[:], xo, xo[:], x0, x0[:], ALU.add)
        self.store(xo, xo[:], dst, q="pool")

    def norm_a(self, x_d, b, halo, wbc, xt, hnt, small, junk):
        S = self.S; T = self.T; t0 = b * 512
        tiles = []
        for j in range(5):
            x = xt.next()
            if j < 4:
                self.load(x, x[:], x_d[t0 + 128 * j:t0 + 128 * (j + 1), :]); npart = 128
            else:
                lo = t0 - halo if t0 - halo >= 0 else 0
                hi = t0 + 512 if t0 + 512 + halo <= T else T - halo
                S.dma_load("sp", x, lambda e, s, x=x, lo=lo, hi=hi: (
                    e.dma_start(out=x[0:halo, :], in_=x_d[lo:lo + halo, :]).then_inc(s, 16),
                    e.dma_start(out=x[halo:2 * halo, :], in_=x_d[hi:hi + halo, :]).then_inc(s, 16)), n=2)
                npart = 2 * halo
            ht = hnt.next()
            self.rms_tile(x, npart, wbc, ht, small, junk)
            tiles.append((ht, npart))
        return tiles

    def norm_b(self, tiles, halo, hn, pTr):
        for j, (ht, npart) in enumerate(tiles):
            pT = pTr.next()
            if j < 4:
                for k in range(8):
                    self.tr(pT, pT[:, k, :], ht, ht[:, k * 128:(k + 1) * 128], self.cb, self.cb[:, 0, :])
                self.copy("act" if j % 2 else "dve", hn, hn[:, :, halo + 128 * j:halo + 128 * (j + 1)], pT, pT[:])
            else:
                for k in range(8):
                    self.tr(pT, pT[:, k, 0:npart], ht, ht[0:npart, k * 128:(k + 1) * 128], self.cb, self.cb[0:npart, 0, 0:npart])
                self.copy("dve", hn, hn[:, :, 0:halo], pT, pT[:, :, 0:halo])
                self.copy("dve", hn, hn[:, :, 512 + halo:512 + 2 * halo], pT, pT[:, :, halo:2 * halo])

    def norm_block(self, x_d, b, halo, wbc, hn, xt, hnt, small, junk, pTr):
        self.norm_b(self.norm_a(x_d, b, halo, wbc, xt, hnt, small, junk), halo, hn, pTr)

    def ffn(self, xin_d, xout_d, up_w, cw_d, cb_d, down_w, npre_d, npost_d, wdbf):
        S = self.S; NB = self.NB
        with ExitStack() as es:
            Wu = self.sb(es, "Wu", [128, 8, 5632], BF16)
            cw = self.sb(es, "cw", [128, 132], F32); self.load(cw, cw[:], cw_d)
            cbias = self.sb(es, "cbias", [128, 44], F32); self.load(cbias, cbias[:], cb_d)
            wbc = self.sb(es, "wbc", [128, 1024], F32); self.load(wbc, wbc[:], npre_d.partition_broadcast(128))
            wpo = self.sb(es, "wpo", [128, 1024], F32); self.load(wpo, wpo[:], npost_d.partition_broadcast(128))
            with ExitStack() as es2:
                stage = Rot([self.sb(es2, "stg", [128, 1024], F32) for _ in range(3)])
                stb = Rot([self.sb(es2, "stb", [128, 1024], BF16) for _ in range(2)])
                self.load_weight(es2, Wu, 0, up_w, 1024, 5632, stage)
                for kk in range(22):
                    st = stage.next(); sb_ = stb.next()
                    self.load(st, st[:], down_w[kk * 128:(kk + 1) * 128, :])
                    self.copy(("act", "dve")[kk % 2], sb_, sb_[:], st, st[:])
                    self.store(sb_, sb_[:], wdbf[kk], q="pool")
                S.barrier()
            xt = Rot([self.sb(es, "xt", [128, 1024], F32) for _ in range(3)])
            xrr = Rot([self.sb(es, "xr", [128, 1024], F32) for _ in range(2)])
            hnt = Rot([self.sb(es, "hnt", [128, 1024], BF16) for _ in range(5)])
            junk = self.sb(es, "junk", [128, 1024], BF16)
            small = Rot([self.sb(es, "sm", [128, 1], F32) for _ in range(12)])
            hnr = Rot([self.sb(es, "hn", [128, 8, 514], BF16) for _ in range(2)])
            ur = Rot([self.sb(es, "u", [128, 514], BF16) for _ in range(3)])
            ar = Rot([self.sb(es, "a", [128, 512], BF16) for _ in range(2)])
            dgr = Rot([self.sb(es, "dg", [128, 128], BF16) for _ in range(12)])
            wdr = Rot([self.sb(es, "wd", [128, 512], BF16) for _ in range(4)])
            hact = self.sb(es, "hact", [128, 22, 512], BF16)
            hos = [self.sb(es, "ho", [128, 1024], F32) for _ in range(4)]
            xo = Rot([self.sb(es, "xo", [128, 1024], F32) for _ in range(1)])
            pg = Rot([self.ps(es, "pg", [128, 512], F32) for _ in range(5)])
            pTr = Rot([self.ps(es, "pT", [128, 8, 128], BF16)])
            phbs = [self.ps(es, "ph", [128, 512], F32) for _ in range(2)]
            phr = Rot([View(pb[:, 0:2], pb) for pb in phbs])
            hn_next = hnr.next()
            self.norm_block(xin_d, 0, 1, wbc, hn_next, xt, hnt, small, junk, pTr)
            for b in range(NB):
                t0 = b * 512
                hn = hn_next
                us = {}
                av = {}

                def projA(kk, part):
                    m = kk + 22 * part
                    pm = pg.next(); ph = phr.next()
                    for k in range(8):
                        self.mm(pm, pm[:], Wu, Wu[:, k, 128 * m:128 * (m + 1)], hn, hn[:, k, 1:513], k == 0, k == 7)
                    for k in range(8):
                        self.mm(ph, ph[:], Wu, Wu[:, k, 128 * m:128 * (m + 1)], hn, hn[:, k, 0:514:513], k == 0, k == 7)
                    u = ur.next()
                    self.copy("act" if part else "dve", u, u[:, 1:513], pm, pm[:])
                    self.tt("dve", u, u[:, 0:514:513], ph, ph[:], self.fl, self.fl[:, 2 * b:2 * b + 2], ALU.mult)
                    us[(kk, part)] = u

                def convB(kk, part):
                    m = kk + 22 * part
                    u = us.pop((kk, part))
                    pc = pg.next()
                    for k in range(3):
                        dg = dgr.next()
                        self.tt("pool", dg, dg[:], self.cb, self.cb[:, 0, :], cw, cw[:, k * 44 + m:k * 44 + m + 1].broadcast_to([128, 128]), ALU.mult)
                        self.mm(pc, pc[:], dg, dg[:], u, u[:, k:k + 512], k == 0, k == 2)
                    if part == 0:
                        a = ar.next(); av[kk] = a
                        self.act(a, a[:], pc, pc[:], AF.Silu, bias=cbias[:, m:m + 1], extra=[cbias])
                    else:
                        a = av.pop(kk)
                        self.stt(hact, hact[:, kk, :], pc, pc[:], cbias, cbias[:, m:m + 1], a, a[:], ALU.add, ALU.mult)

                tiles = [(kk, part) for kk in range(22) for part in range(2)]
                projA(*tiles[0])
                for ti in range(1, len(tiles)):
                    projA(*tiles[ti])
                    convB(*tiles[ti - 1])
                convB(*tiles[-1])
                nxt = self.norm_a(xin_d, b + 1, 1, wbc, xt, hnt, small, junk) if b + 1 < NB else None
                for q in range(2):
                    pos = [pg.next() for _ in range(4)]
                    for kk in range(22):
                        wd = wdr.next()
                        self.load(wd, wd[:], wdbf[kk][:, 512 * q:512 * (q + 1)])
                        for j in range(4):
                            self.mm(pos[j], pos[j][:], hact, hact[:, kk, 128 * j:128 * (j + 1)], wd, wd[:], kk == 0, kk == 21)
                    for j in range(4):
                        self.copy("act" if j % 2 else "dve", hos[j], hos[j][:, 512 * q:512 * (q + 1)], pos[j], pos[j][:])
                if nxt is not None:
                    hn_next = hnr.next()
                    self.norm_b(nxt, 1, hn_next, pTr)
                for j in range(4):
                    x0 = xrr.next()
                    self.load(x0, x0[:], xin_d[t0 + 128 * j:t0 + 128 * (j + 1), :])
                    self.post_norm_residual(hos[j], x0, wpo, xo.next(), junk, small, xout_d[t0 + 128 * j:t0 + 128 * (j + 1), :])
        self.end_phase()

    def conformer(self, xin_d, xout_d, pw1_w, pw1b_d, dw_d, dwb_d, lnw_d, lnb_d, pw2_w, pw2b_d, npre_d, npost_d):
        S = self.S; NB = self.NB; H = 15
        with ExitStack() as es:
            W1 = self.sb(es, "W1c", [128, 8, 2048], BF16)
            W2 = self.sb(es, "W2c", [128, 8, 1024], BF16)
            with ExitStack() as es2:
                stage = Rot([self.sb(es2, "stg", [128, 1024], F32) for _ in range(3)])
                self.load_weight(es2, W1, 0, pw1_w, 1024, 2048, stage)
                self.load_weight(es2, W2, 0, pw2_w, 1024, 1024, stage)
                S.barrier()
            b1 = self.sb(es, "b1", [128, 16], F32); self.load(b1, b1[:], pw1b_d)
            dw = self.sb(es, "dw", [128, 248], F32); self.load(dw, dw[:], dw_d)
            dwb = self.sb(es, "dwb", [128, 8], F32); self.load(dwb, dwb[:], dwb_d)
            lnw = self.sb(es, "lnw", [128, 8], F32); self.load(lnw, lnw[:], lnw_d)
            lnb = self.sb(es, "lnb", [128, 8], F32); self.load(lnb, lnb[:], lnb_d)
            wbc = self.sb(es, "wbc", [128, 1024], F32); self.load(wbc, wbc[:], npre_d.partition_broadcast(128))
            wpo = self.sb(es, "wpo", [128, 1024], F32); self.load(wpo, wpo[:], npost_d.partition_broadcast(128))
            b2bc = self.sb(es, "b2bc", [128, 1024], F32); self.load(b2bc, b2bc[:], pw2b_d.partition_broadcast(128))
            onesN = self.sb(es, "onesN", [128, 128], BF16)
            self.ts(onesN, onesN[:], self.cf, self.cf[:, 5, :], 1.0 / 1024, None, ALU.mult)
            xt = Rot([self.sb(es, "xt", [128, 1024], F32) for _ in range(3)])
            xrr = Rot([self.sb(es, "xr", [128, 1024], F32) for _ in range(2)])
            hnt = Rot([self.sb(es, "hnt", [128, 1024], BF16) for _ in range(5)])
            junk = self.sb(es, "junk", [128, 1024], BF16)
            small = Rot([self.sb(es, "sm", [128, 1], F32) for _ in range(12)])
            hnr = Rot([self.sb(es, "hn", [128, 8, 542], BF16) for _ in range(2)])
            glu = self.sb(es, "glu", [128, 8, 542], BF16)
            sgr = Rot([self.sb(es, "sg", [128, 512], F32) for _ in range(2)])
            s30 = Rot([self.sb(es, "s30", [128, 30], F32) for _ in range(4)])
            dgr = Rot([self.sb(es, "dg", [128, 128], BF16) for _ in range(16)])
            vf = self.sb(es, "vf", [128, 8, 512], F32)
            vb = self.sb(es, "vb", [128, 8, 512], BF16)
            vsq = self.sb(es, "vsq", [128, 8, 512], BF16)
            mean = self.sb(es, "mean", [128, 512], F32)
            var = self.sb(es, "var", [128, 512], F32)
            t1r = Rot([self.sb(es, "t1", [128, 512], F32) for _ in range(2)])
            sact = self.sb(es, "sact", [128, 8, 512], BF16)
            hos = [self.sb(es, "ho", [128, 1024], F32) for _ in range(4)]
            xo = Rot([self.sb(es, "xo", [128, 1024], F32) for _ in range(2)])
            pg = Rot([self.ps(es, "pg", [128, 512], F32) for _ in range(5)])
            pTr = Rot([self.ps(es, "pT", [128, 8, 128], BF16)])
            phbs = [self.ps(es, "ph", [128, 512], F32) for _ in range(2)]
            phr = Rot([View(pb[:, 32 * i:32 * i + 30], pb) for pb in phbs for i in range(2)])
            for b in range(NB):
                t0 = b * 512
                hn = hnr.next()
                self.norm_block(xin_d, b, H, wbc, hn, xt, hnt, small, junk, pTr)
                for m in range(8):
                    pa = pg.next(); pgt = pg.next(); pha = phr.next(); phg = phr.next()
                    for (pmain, phal, c0) in ((pa, pha, 128 * m), (pgt, phg, 1024 + 128 * m)):
                        for k in range(8):
                            self.mm(pmain, pmain[:], W1, W1[:, k, c0:c0 + 128], hn, hn[:, k, H:H + 512], k == 0, k == 7)
                        for k in range(8):
                            self.mm(phal, phal[:, 0:H], W1, W1[:, k, c0:c0 + 128], hn, hn[:, k, 0:H], k == 0, k == 7)
                        for k in range(8):
                            self.mm(phal, phal[:, H:2 * H], W1, W1[:, k, c0:c0 + 128], hn, hn[:, k, 512 + H:512 + 2 * H], k == 0, k == 7)
                    sg = sgr.next()
                    self.act(sg, sg[:], pgt, pgt[:], AF.Sigmoid, bias=b1[:, 8 + m:9 + m], extra=[b1])
                    self.stt(glu, glu[:, m, H:H + 512], pa, pa[:], b1, b1[:, m:m + 1], sg, sg[:], ALU.add, ALU.mult)
                    sh = s30.next(); th = s30.next()
                    self.act(sh, sh[:], phg, phg[:], AF.Sigmoid, bias=b1[:, 8 + m:9 + m], extra=[b1])
                    self.stt(th, th[:], pha, pha[:], b1, b1[:, m:m + 1], sh, sh[:], ALU.add, ALU.mult)
                    self.tt("dve", glu, glu[:, m, 0:H], th, th[:, 0:H], self.fl, self.fl[:, 2 * b:2 * b + 1].broadcast_to([128, H]), ALU.mult)
                    self.tt("dve", glu, glu[:, m, 512 + H:512 + 2 * H], th, th[:, H:2 * H], self.fl,
                            self.fl[:, 2 * b + 1:2 * b + 2].broadcast_to([128, H]), ALU.mult)
                for m in range(8):
                    pc = pg.next()
                    for k in range(31):
                        dg = dgr.next()
                        col = dw[:, k * 8 + m:k * 8 + m + 1]
                        if k % 3 == 0:
                            self.tt("pool", dg, dg[:], self.cb, self.cb[:, 0, :], dw, col.broadcast_to([128, 128]), ALU.mult)
                        elif k % 3 == 1:
                            self.ts(dg, dg[:], self.cb, self.cb[:, 0, :], col, None, ALU.mult, extra=[dw])
                        else:
                            self.S.op("act", lambda e, dg=dg, col=col: e.activation(dg[:], self.cb[:, 0, :], AF.Copy, scale=col),
                                      reads=[self.cb, dw], writes=[dg])
                        self.mm(pc, pc[:], dg, dg[:], glu, glu[:, m, k:k + 512], k == 0, k == 30)
                    self.act(vf, vf[:, m, :], pc, pc[:], AF.Identity, bias=dwb[:, m:m + 1], extra=[dwb])
                    self.act(vsq, vsq[:, m, :], pc, pc[:], AF.Square, bias=dwb[:, m:m + 1], extra=[dwb])
                    self.copy("dve", vb, vb[:, m, :], vf, vf[:, m, :])
                pmean = pg.next(); pmsq = pg.next()
                for m in range(8):
                    self.mm(pmean, pmean[:], onesN, onesN[:], vb, vb[:, m, :], m == 0, m == 7)
                for m in range(8):
                    self.mm(pmsq, pmsq[:], onesN, onesN[:], vsq, vsq[:, m, :], m == 0, m == 7)
                self.copy("dve", mean, mean[:], pmean, pmean[:])
                self.tt("pool", var, var[:], mean, mean[:], mean, mean[:], ALU.mult)
                self.tt("dve", var, var[:], pmsq, pmsq[:], var, var[:], ALU.subtract)
                self.ts(var, var[:], var, var[:], 0.0, None, ALU.max)
                self.act(var, var[:], var, var[:], AF.Ln, bias=self.epsb[:, 0:1], extra=[self.epsb])
                self.act(var, var[:], var, var[:], AF.Exp, scale=-0.5)
                for m in range(8):
                    t1 = t1r.next()
                    self.tt("pool", t1, t1[:], vf, vf[:, m, :], mean, mean[:], ALU.subtract)
                    self.tt("dve", t1, t1[:], t1, t1[:], var, var[:], ALU.mult)
                    self.S.op("act", lambda e, t1=t1, m=m: e.activation(sact[:, m, :], t1[:], AF.Silu, bias=lnb[:, m:m + 1], scale=lnw[:, m:m + 1]),
                              reads=[t1, lnb, lnw], writes=[sact])
                for j in range(4):
                    for q in range(2):
                        po = pg.next()
                        for m in range(8):
                            self.mm(po, po[:], sact, sact[:, m, 128 * j:128 * (j + 1)], W2, W2[:, m, 512 * q:512 * (q + 1)], m == 0, m == 7)
                        self.tt("dve", hos[j], hos[j][:, 512 * q:512 * (q + 1)], po, po[:], b2bc, b2bc[:, 512 * q:512 * (q + 1)], ALU.add)
                    x0 = xrr.next()
                    self.load(x0, x0[:], xin_d[t0 + 128 * j:t0 + 128 * (j + 1), :])
                    self.post_norm_residual(hos[j], x0, wpo, xo.next(), junk, small, xout_d[t0 + 128 * j:t0 + 128 * (j + 1), :])
        self.end_phase()


def build(NB, dbg=(), upto=6):
    B = Builder(NB, dbg)
    T = B.T
    x_d = B.din("x", [T, D])
    w_in = B.din("ssm_in_w", [D, 6208])
    cw1 = B.din("ssm_cw", [128, 96]); cb1 = B.din("ssm_cb", [128, 32]); dtb = B.din("ssm_dtb", [64, 1])
    alog = B.din("ssm_a_log", [2, 32]); dsk = B.din("ssm_d", [1, 32]); gnw = B.din("ssm_gnw", [128, 16])
    w_out = B.din("ssm_out_w", [DI, D])
    npm = B.din("norm_pre_mix", [2, D]); npo = B.din("norm_post_mix", [2, D])
    npf = B.din("norm_pre_ffn", [2, D]); npof = B.din("norm_post_ffn", [2, D])
    up_w = B.din("ffn_up_w", [2, D, 2 * FF]); dn_w = B.din("ffn_down_w", [2, FF, D])
    fcw = B.din("ffn_cw", [2, 128, 132]); fcb = B.din("ffn_cb", [2, 128, 44])
    pw1 = B.din("cf_pw1_w", [D, 2 * D]); pw1b = B.din("cf_pw1_b", [128, 16])
    dww = B.din("cf_dw", [128, 248]); dwb = B.din("cf_dwb", [128, 8])
    lnw = B.din("cf_lnw", [128, 8]); lnb = B.din("cf_lnb", [128, 8])
    pw2 = B.din("cf_pw2_w", [D, D]); pw2b = B.din("cf_pw2_b", [1, D])
    scr = {
        "xs": B.dscr("xs", [T, DI], BF16), "Bt": B.dscr("Bt", [T, 1024], BF16),
        "BT": B.dscr("BT", [128, 8, T], BF16), "CT": B.dscr("CT", [128, 8, T], BF16),
        "dt": B.dscr("dt", [T, 64], F32), "yb": B.dscr("yb", [T, DI], BF16), "zs": B.dscr("zs", [T, DI], BF16),
    }
    x1 = B.dscr("x1", [T, D], F32); x2 = B.dscr("x2", [T, D], F32); x3 = B.dscr("x3", [T, D], F32)
    wdbf = B.dscr("wdbf", [22, 128, D], BF16)
    y_d = B.nc.dram_tensor("y", [T, D], F32, kind="ExternalOutput").ap()
    with ExitStack() as es:
        B.setup_consts(es)
        B.cur_bufs = []
        B.phase1(x_d, w_in, cw1, cb1, dtb, npm[0:1, :], scr)
        if upto >= 2:
            B.sweep(False, scr, alog[1:2, :], x_d=x_d, w_in=w_in, npre_d=npm[0:1, :], d_d=dsk)
        if upto >= 3:
            B.sweep(True, scr, alog[0:1, :], x_d=x_d, xout_d=x1, w_in=w_in, w_out=w_out, d_d=dsk, gnw_d=gnw,
                    npre_d=npm[0:1, :], npost_d=npo[0:1, :])
        if upto >= 4:
            B.ffn(x1, x2, up_w[0], fcw[0], fcb[0], dn_w[0], npf[0:1, :], npof[0:1, :], wdbf)
        if upto >= 5:
            B.conformer(x2, x3, pw1, pw1b, dww, dwb, lnw, lnb, pw2, pw2b, npm[1:2, :], npo[1:2, :])
        if upto >= 6:
            B.ffn(x3, y_d, up_w[1], fcw[1], fcb[1], dn_w[1], npf[1:2, :], npof[1:2, :], wdbf)
        B.S.emit()
    return B


def make_consts():
    r = np.arange(128)
    ident = np.eye(128, dtype=np.float32)
    LE = (r[:, None] <= r[None, :]).astype(np.float32)
    GT = (r[:, None] > r[None, :]).astype(np.float32)
    GE = (r[:, None] >= r[None, :]).astype(np.float32)
    LT = (r[:, None] < r[None, :]).astype(np.float32)
    ones = np.ones((128, 128), np.float32)
    return np.ascontiguousarray(np.concatenate([ident, LE, GT, GE, LT, ones], axis=1))


def colmajor(v, ncol):
    return np.ascontiguousarray(np.asarray(v, np.float32).reshape(ncol, 128).T)


SLOT = 4096
NB_FULL = 24
_CACHE = {}


def _layout_weights(w):
    f = lambda a: np.ascontiguousarray(np.asarray(a, np.float32))
    out = {
        "ssm_in_w": f(w["ssm_in_w"][0]),
        "ssm_cw": colmajor(f(w["ssm_conv_w"][0]).reshape(-1), 96),
        "ssm_cb": colmajor(w["ssm_conv_b"][0], 32),
        "ssm_dtb": f(w["ssm_dt_bias"][0]).reshape(64, 1),
        "ssm_a_log": f(w["ssm_a_log"][0]),
        "ssm_d": f(w["ssm_d"][0]).reshape(1, 32),
        "ssm_gnw": colmajor(w["ssm_norm_w"][0], 16),
        "ssm_out_w": f(w["ssm_out_w"][0]),
        "norm_pre_mix": f(w["norm_pre_mix"]), "norm_post_mix": f(w["norm_post_mix"]),
        "norm_pre_ffn": f(w["norm_pre_ffn"]), "norm_post_ffn": f(w["norm_post_ffn"]),
        "ffn_up_w": f(w["ffn_up_w"]), "ffn_down_w": f(w["ffn_down_w"]),
        "ffn_cw": np.stack([colmajor(f(w["ffn_conv_w"][l]).reshape(-1), 132) for l in range(2)]),
        "ffn_cb": np.stack([colmajor(w["ffn_conv_b"][l], 44) for l in range(2)]),
        "cf_pw1_w": f(w["cf_pw1_w"][0]), "cf_pw1_b": colmajor(w["cf_pw1_b"][0], 16),
        "cf_dw": colmajor(f(w["cf_dw_w"][0]).reshape(-1), 248),
        "cf_dwb": colmajor(w["cf_dw_b"][0], 8), "cf_lnw": colmajor(w["cf_ln_w"][0], 8), "cf_lnb": colmajor(w["cf_ln_b"][0], 8),
        "cf_pw2_w": f(w["cf_pw2_w"][0]), "cf_pw2_b": f(w["cf_pw2_b"][0]).reshape(1, D),
    }
    return out


def kernel(x_prompt, x_sample, **w):
    x_prompt = np.asarray(x_prompt, np.float32); x_sample = np.asarray(x_sample, np.float32)
    plan = []
    for c in range(4):
        plan.append([("p", 3 * c), ("p", 3 * c + 1), ("p", 3 * c + 2)])
    for c in range(2):
        plan.append([("p", 12 + 2 * c), ("p", 13 + 2 * c), None])
    for c in range(2):
        plan.append([("s", c, 0), ("s", c, 1), None])
    wl = _layout_weights(w)
    consts = make_consts()
    in_maps = []
    for c in range(NCORES):
        xs = np.zeros((3 * SLOT, D), np.float32)
        fl = np.ones((NB_FULL, 2), np.float32)
        for si, ent in enumerate(plan[c]):
            b0 = si * 8
            fl[b0, 0] = 0.0; fl[b0 + 7, 1] = 0.0
            if ent is None:
                continue
            if ent[0] == "p":
                xs[si * SLOT:(si + 1) * SLOT] = x_prompt[ent[1]]
            else:
                xs[si * SLOT:(si + 1) * SLOT] = x_sample[ent[1], ent[2] * SLOT:(ent[2] + 1) * SLOT]
                if ent[2] == 0:
                    fl[b0 + 7, 1] = 1.0
                else:
                    fl[b0, 0] = 1.0
        m = dict(wl)
        m["x"] = xs
        m["flags"] = np.ascontiguousarray(np.broadcast_to(fl.reshape(1, -1), (128, NB_FULL * 2)))
        m["consts"] = consts
        in_maps.append(m)
    if "prog" not in _CACHE:
        _CACHE["prog"] = build(NB_FULL)
    res = run_bass_kernel_spmd(_CACHE["prog"].nc, in_maps, core_ids=list(range(NCORES)))
    y_prompt = np.empty_like(x_prompt); y_sample = np.empty_like(x_sample)
    for c in range(NCORES):
        y = np.asarray(res.results[c]["y"], np.float32)
        for si, ent in enumerate(plan[c]):
            if ent is None:
                continue
            if ent[0] == "p":
                y_prompt[ent[1]] = y[si * SLOT:(si + 1) * SLOT]
            else:
                y_sample[ent[1], ent[2] * SLOT:(ent[2] + 1) * SLOT] = y[si * SLOT:(si + 1) * SLOT]
    return (y_prompt, y_sample)
```

```python
from contextlib import ExitStack
import numpy as np
import concourse.bass as bass
import concourse.mybir as mybir
from concourse.bass_utils import run_bass_kernel_spmd

F32 = mybir.dt.float32
BF16 = mybir.dt.bfloat16
AF = mybir.ActivationFunctionType
ALU = mybir.AluOpType
AX = mybir.AxisListType

ENGS = ("pe", "act", "dve", "pool", "sp")
D = 1024; DI = 2048; NH = 32; NG = 8; HP = 64; NS = 128; CD = 4096; FF = 2816
EPS = 1e-6
NCORES = 8


class Buf:
    __slots__ = ("t", "name", "lw", "rd", "dsem", "dcount")

    def __init__(self, t, name):
        self.t = t; self.name = name; self.lw = None; self.rd = []; self.dsem = None; self.dcount = 0

    def __getitem__(self, k):
        return self.t[k]


class View:
    __slots__ = ("t", "root")

    def __init__(self, ap, root):
        self.t = ap; self.root = root

    def __getitem__(self, k):
        return self.t[k]


def _roots(bufs):
    return [getattr(b, "root", b) for b in bufs]


class Op:
    __slots__ = ("eng", "fn", "deps", "semval", "needed", "dma")

    def __init__(self, eng, fn, deps, dma=None):
        self.eng = eng; self.fn = fn; self.deps = deps; self.semval = None; self.needed = False; self.dma = dma


class Sched:
    def __init__(self, nc):
        self.nc = nc
        self.ops = {e: [] for e in ENGS}
        self.all = []
        self.semfinal = {}
        self.sem_pool = {"hw": [], "sw": []}
        self.last = {e: None for e in ENGS}

    def _deps(self, eng, reads, writes):
        reads = _roots(reads); writes = _roots(writes)
        deps = []
        for b in reads:
            if b.lw is not None:
                deps.append((b.lw, "raw"))
        for b in writes:
            if b.lw is not None:
                deps.append((b.lw, "waw"))
            deps.extend((r, "war") for r in b.rd)
        out = []
        for d, kind in deps:
            if d[0] == "op" and d[1].eng == eng:
                if eng == "pe":
                    continue
                if kind != "raw":
                    continue
            out.append(d)
        return out

    def op(self, eng, fn, reads=(), writes=()):
        reads = _roots(reads); writes = _roots(writes)
        o = Op(eng, fn, self._deps(eng, reads, writes))
        self.all.append(o); self.ops[eng].append(o); self.last[eng] = o
        tok = ("op", o)
        for b in reads:
            b.rd.append(tok)
        for b in writes:
            b.lw = tok; b.rd = []
        return o

    def _dsem(self, b, q):
        qc = "sw" if q == "pool" else "hw"
        if b.dsem is None:
            b.dsem = {}
        if qc not in b.dsem:
            if self.sem_pool[qc]:
                b.dsem[qc] = list(self.sem_pool[qc].pop())
            else:
                b.dsem[qc] = [self.nc.alloc_semaphore("d%s_%s" % (qc, b.name)), 0]
        return b.dsem[qc]

    def release(self, bufs):
        for b in bufs:
            if b.dsem:
                for qc, (sm, cnt) in b.dsem.items():
                    self.sem_pool[qc].append((sm, cnt))
                b.dsem = None

    def dma_load(self, q, dst, fn, n=1):
        deps = self._deps(q, [], [dst])
        ent = self._dsem(dst, q)
        ent[1] += 16 * n
        sem, cnt = ent
        self.semfinal[sem.num] = (sem, cnt)
        o = Op(q, fn, deps, dma=sem)
        self.all.append(o); self.ops[q].append(o)
        dst.lw = ("dma", sem, cnt); dst.rd = []
        return o

    def dma_store(self, q, src, fn, n=1):
        deps = self._deps(q, [src], [])
        ent = self._dsem(src, q)
        ent[1] += 16 * n
        sem, cnt = ent
        self.semfinal[sem.num] = (sem, cnt)
        o = Op(q, fn, deps, dma=sem)
        self.all.append(o); self.ops[q].append(o)
        src.rd.append(("dma", sem, cnt))
        return o

    def barrier(self):
        toks = [("op", self.last[e]) for e in ENGS if self.last[e] is not None and self.last[e].dma is None]
        toks += [("dma", sm, v) for (sm, v) in self.semfinal.values()]
        for e in ENGS:
            o = Op(e, None, list(toks))
            self.all.append(o); self.ops[e].append(o)

    def emit(self):
        nc = self.nc
        for o in self.all:
            for d in o.deps:
                if d[0] == "op":
                    d[1].needed = True
        esem = {e: nc.alloc_semaphore("e_" + e) for e in ENGS}
        for e in ENGS:
            c = 0
            for o in self.ops[e]:
                if o.needed:
                    c += 1; o.semval = c
        finals = list(self.semfinal.values())

        def run(e, engine):
            waited = {}
            for o in self.ops[e]:
                need = {}
                for d in o.deps:
                    if d[0] == "op":
                        if d[1].eng == e and e == "pe":
                            continue
                        s, v = esem[d[1].eng], d[1].semval
                    else:
                        s, v = d[1], d[2]
                    if need.get(s.num, (None, 0))[1] < v:
                        need[s.num] = (s, v)
                for k, (s, v) in need.items():
                    if waited.get(k, 0) < v:
                        engine.wait_ge(s, v); waited[k] = v
                if o.fn is None:
                    continue
                if o.dma is not None:
                    o.fn(engine, o.dma)
                else:
                    ins = o.fn(engine)
                    if o.needed:
                        ins.then_inc(esem[e], 1)
            if e == "sp":
                for (s, v) in finals:
                    if waited.get(s.num, 0) < v:
                        engine.wait_ge(s, v)

        with nc.Block() as block:
            block.tensor(lambda eng: run("pe", eng))
            block.scalar(lambda eng: run("act", eng))
            block.vector(lambda eng: run("dve", eng))
            block.gpsimd(lambda eng: run("pool", eng))
            block.sync(lambda eng: run("sp", eng))


class Rot:
    def __init__(self, bufs):
        self.b = bufs; self.i = 0

    def next(self):
        r = self.b[self.i % len(self.b)]; self.i += 1
        return r


class Builder:
    def __init__(self, NB, dbg=()):
        self.NB = NB; self.T = NB * 512; self.NCH = self.T // 128
        self.dbg = dbg
        self.nc = bass.Bass("TRN2", target_bir_lowering=False)
        self.S = Sched(self.nc)
        self.ein = {}
        self.cnt = 0
        self.cur_bufs = []

    def end_phase(self):
        self.S.barrier()
        self.S.release(self.cur_bufs)
        self.cur_bufs = []

    def din(self, name, shape, dt=F32):
        self.ein[name] = self.nc.dram_tensor(name, list(shape), dt, kind="ExternalInput").ap()
        return self.ein[name]

    def dscr(self, name, shape, dt):
        kind = "ExternalOutput" if name in self.dbg else "Internal"
        return self.nc.dram_tensor(name, list(shape), dt, kind=kind).ap()

    def sb(self, es, name, shape, dt):
        self.cnt += 1
        t = es.enter_context(self.nc.sbuf_tensor("%s_%d" % (name, self.cnt), list(shape), dt))
        b = Buf(t, "%s_%d" % (name, self.cnt))
        self.cur_bufs.append(b)
        return b

    def sb_split(self, es, name, ncols, dt, nparts):
        whole = self.sb(es, name, [128, ncols], dt)
        w = ncols // nparts
        parts = [Buf(whole.t[:, i * w:(i + 1) * w], "%s_p%d" % (whole.name, i)) for i in range(nparts)]
        self.cur_bufs.extend(parts)
        return whole.t, parts

    def ps(self, es, name, shape, dt=F32):
        self.cnt += 1
        t = es.enter_context(self.nc.psum_tensor("%s_%d" % (name, self.cnt), list(shape), dt))
        return Buf(t, "%s_%d" % (name, self.cnt))

    def mm(self, ob, o, lb, l, rb, r, start, stop):
        self.S.op("pe", lambda e: e.matmul(o, l, r, start=start, stop=stop), reads=[lb, rb], writes=[ob])

    def tr(self, ob, o, ib, i, idb, idn):
        self.S.op("pe", lambda e: e.transpose(o, i, idn), reads=[ib, idb], writes=[ob])

    def act(self, ob, o, ib, i, func, bias=None, scale=1.0, accum=None, extra=(), eng="act"):
        kw = {}
        if bias is not None:
            kw["bias"] = bias
        if accum is not None:
            kw["accum_out"] = accum[1]
        wr = [ob] + ([accum[0]] if accum is not None else [])
        self.S.op("act", lambda e: e.activation(o, i, func, scale=scale, **kw), reads=[ib] + list(extra), writes=wr)

    def copy(self, eng, ob, o, ib, i):
        if eng == "act":
            self.S.op("act", lambda e: e.copy(o, i), reads=[ib], writes=[ob])
        else:
            self.S.op(eng, lambda e: e.tensor_copy(o, i), reads=[ib], writes=[ob])

    def tt(self, eng, ob, o, ab, a, bb, b_, op):
        self.S.op(eng, lambda e: e.tensor_tensor(o, a, b_, op), reads=[ab, bb], writes=[ob])

    def ts(self, ob, o, ib, i, s1, s2, op0, op1=None, extra=()):
        if op1 is None:
            self.S.op("dve", lambda e: e.tensor_scalar(o, i, s1, None, op0), reads=[ib] + list(extra), writes=[ob])
        else:
            self.S.op("dve", lambda e: e.tensor_scalar(o, i, s1, s2, op0, op1), reads=[ib] + list(extra), writes=[ob])

    def stt(self, ob, o, ab, a, sb_, s, bb, b_, op0, op1):
        rd = [ab, bb] + ([sb_] if sb_ is not None else [])
        self.S.op("dve", lambda e: e.scalar_tensor_tensor(o, a, s, b_, op0, op1), reads=rd, writes=[ob])

    def load(self, dst, o, src, q="sp"):
        self.S.dma_load(q, dst, lambda e, s: e.dma_start(out=o, in_=src).then_inc(s, 16))

    def store(self, srcb, i, dst, q="sp"):
        self.S.dma_store(q, srcb, lambda e, s: e.dma_start(out=dst, in_=i).then_inc(s, 16))

    def setup_consts(self, es):
        c = self.din("consts", [128, 6 * 128])
        self.cf = self.sb(es, "cf", [128, 6, 128], F32)
        self.cb = self.sb(es, "cbf", [128, 6, 128], BF16)
        self.load(self.cf, self.cf[:], c.rearrange("p (a b) -> p a b", a=6))
        self.copy("dve", self.cb, self.cb[:], self.cf, self.cf[:])
        self.fl = self.sb(es, "fl", [128, self.NB * 2], F32)
        self.load(self.fl, self.fl[:], self.din("flags", [128, self.NB * 2]))
        self.epsb = self.sb(es, "epsb", [128, 1], F32)
        self.S.op("pool", lambda e: e.memset(self.epsb[:], EPS), writes=[self.epsb])

    def load_weight(self, es, dst, dcol0, src, K, N, stage, rowscale=None):
        engs = ["act", "dve", "pool"] if rowscale is None else ["act", "dve"]
        i = 0
        for k in range(K // 128):
            for c0 in range(0, N, 1024):
                n = min(1024, N - c0)
                st = stage.next()
                self.load(st, st[:, 0:n], src[k * 128:(k + 1) * 128, c0:c0 + n])
                o = dst[:, k, dcol0 + c0:dcol0 + c0 + n]
                e = engs[i % len(engs)]; i += 1
                if rowscale is None:
                    self.copy(e, dst, o, st, st[:, 0:n])
                else:
                    rb, r = rowscale
                    self.ts(dst, o, st, st[:, 0:n], r[:, k:k + 1], None, ALU.mult, extra=[rb])

    def make_diag(self, es, diag, cw, n):
        for i in range(n):
            if i % 2 == 0:
                self.ts(diag, diag[:, i, :], self.cb, self.cb[:, 0, :], cw[:, i:i + 1], None, ALU.mult, extra=[cw])
            else:
                self.S.op("act", lambda e, i=i: e.activation(diag[:, i, :], self.cb[:, 0, :], AF.Copy, scale=cw[:, i:i + 1]),
                          reads=[self.cb, cw], writes=[diag])

    def rms_tile(self, xb, npart, wbc, hn_tok, small, junk):
        ss = small.next(); lg = small.next(); rs = small.next()
        self.act(junk, junk[0:npart, 0:1024], xb, xb[0:npart, :], AF.Square, accum=(ss, ss[0:npart, 0:1]))
        self.act(lg, lg[0:npart, :], ss, ss[0:npart, :], AF.Ln, bias=self.epsb[0:npart, 0:1], scale=1.0 / D, extra=[self.epsb])
        self.act(rs, rs[0:npart, :], lg, lg[0:npart, :], AF.Exp, scale=-0.5)
        self.stt(hn_tok, hn_tok[0:npart, :], xb, xb[0:npart, :], rs, rs[0:npart, 0:1], wbc, wbc[0:npart, :], ALU.mult, ALU.mult)

    def phase1(self, x_d, w_in, cw_d, cb_d, dtb_d, nw_d, scr):
        S = self.S; NB = self.NB; T = self.T
        with ExitStack() as es:
            W = self.sb(es, "W1", [128, 8, 4160], BF16)
            stage = Rot([self.sb(es, "stg", [128, 1024], F32) for _ in range(3)])
            self.load_weight(es, W, 0, w_in[:, 2048:6208], 1024, 4160, stage)
            cw = self.sb(es, "cw", [128, 96], F32); self.load(cw, cw[:], cw_d)
            cbias = self.sb(es, "cbias", [128, 32], F32); self.load(cbias, cbias[:], cb_d)
            dtb = self.sb(es, "dtb", [128, 1], F32); self.load(dtb, dtb[0:64, :], dtb_d)
            wbc = self.sb(es, "wbc", [128, 1024], F32); self.load(wbc, wbc[:], nw_d.partition_broadcast(128))
            diag = self.sb(es, "diag", [128, 96, 128], BF16)
            self.make_diag(es, diag, cw, 96)
            xt = Rot([self.sb(es, "xt", [128, 1024], F32) for _ in range(6)])
            hnt = Rot([self.sb(es, "hnt", [128, 1024], BF16) for _ in range(5)])
            junk = self.sb(es, "junk", [128, 1024], BF16)
            small = Rot([self.sb(es, "sm", [128, 1], F32) for _ in range(12)])
            hnr = Rot([self.sb(es, "hn", [128, 8, 514], BF16) for _ in range(2)])
            ur = Rot([self.sb(es, "u", [128, 514], BF16) for _ in range(3)])
            xgr = Rot([self.sb(es, "xg", [128, 8, 512], BF16) for _ in range(2)])
            tokr = Rot([self.sb(es, "tok", [128, 4, 1024], BF16) for _ in range(2)])
            dte = self.sb(es, "dte", [128, 512], F32)
            dtT = self.sb(es, "dtT", [128, 512], F32)
            dtk = Rot([self.sb(es, "dtk", [128, 4, 64], F32) for _ in range(2)])
            pTr = Rot([self.ps(es, "pT", [128, 8, 128], BF16) for _ in range(2)])
            pmr = Rot([self.ps(es, "pm", [128, 512], F32) for _ in range(2)])
            pcr = Rot([self.ps(es, "pc", [128, 512], F32) for _ in range(2)])
            phbs = [self.ps(es, "ph", [128, 512], F32) for _ in range(2)]
            phr = Rot([View(pb[:, 0:2], pb) for pb in phbs])
            pdr = Rot([View(pb[:, 0:64], pb) for pb in pmr.b])
            ev = 0
            hn_next = hnr.next()
            self.norm_block(x_d, 0, 1, wbc, hn_next, xt, hnt, small, junk, pTr)
            for b in range(NB):
                t0 = b * 512
                hn = hn_next
                nxt = None
                us = {}
                xgs = {}

                def projA(m):
                    pm = pmr.next(); ph = phr.next()
                    for k in range(8):
                        self.mm(pm, pm[:], W, W[:, k, 128 * m:128 * (m + 1)], hn, hn[:, k, 1:513], k == 0, k == 7)
                    for k in range(8):
                        self.mm(ph, ph[:], W, W[:, k, 128 * m:128 * (m + 1)], hn, hn[:, k, 0:514:513], k == 0, k == 7)
                    u = ur.next()
                    self.copy("act" if m % 2 else "dve", u, u[:, 1:513], pm, pm[:])
                    self.tt("dve", u, u[:, 0:514:513], ph, ph[:], self.fl, self.fl[:, 2 * b:2 * b + 2], ALU.mult)
                    us[m] = u

                def convB(m):
                    G, i = divmod(m, 8)
                    if i == 0:
                        xgs[G] = xgr.next()
                    xg = xgs[G]; u = us.pop(m)
                    pc = pcr.next()
                    for k in range(3):
                        self.mm(pc, pc[:], diag, diag[:, k * 32 + m, :], u, u[:, k:k + 512], k == 0, k == 2)
                    self.act(xg, xg[:, i, :], pc, pc[:], AF.Silu, bias=cbias[:, m:m + 1], extra=[cbias])
                    if i < 7:
                        return
                    if G < 3:
                        tok = tokr.next()
                        for j in range(4):
                            pT = pTr.next()
                            for ii in range(8):
                                self.tr(pT, pT[:, ii, :], xg, xg[:, ii, 128 * j:128 * (j + 1)], self.cb, self.cb[:, 0, :])
                            self.copy("act" if j % 2 else "dve", tok, tok[:, j, :], pT, pT[:].rearrange("p a b -> p (a b)"))
                        if G < 2:
                            dst = scr["xs"][t0:t0 + 512, G * 1024:(G + 1) * 1024]
                        else:
                            dst = scr["Bt"][t0:t0 + 512, :]
                        self.store(tok, tok[:], dst.rearrange("(j p) c -> p j c", p=128), q="pool")
                    if G == 2:
                        self.store(xg, xg[:], scr["BT"][:, :, t0:t0 + 512], q="pool")
                    if G == 3:
                        self.store(xg, xg[:], scr["CT"][:, :, t0:t0 + 512], q="pool")

                projA(0)
                for m in range(1, 32):
                    projA(m)
                    convB(m - 1)
                    if m == 24 and b + 1 < NB:
                        nxt = self.norm_a(x_d, b + 1, 1, wbc, xt, hnt, small, junk)
                convB(31)
                pm = pmr.next()
                for k in range(8):
                    self.mm(pm, pm[0:64, :], W, W[:, k, 4096:4160], hn, hn[:, k, 1:513], k == 0, k == 7)
                self.act(dte, dte[0:64, :], pm, pm[0:64, :], AF.Exp, bias=dtb[0:64, 0:1], extra=[dtb])
                self.act(dtT, dtT[0:64, :], dte, dte[0:64, :], AF.Ln, bias=1.0)
                dk = dtk.next()
                for j in range(4):
                    pd = pdr.next()
                    self.tr(pd, pd[:], dtT, dtT[0:64, 128 * j:128 * (j + 1)], self.cf, self.cf[0:64, 0, 0:64])
                    self.copy("dve", dk, dk[:, j, :], pd, pd[:])
                self.store(dk, dk[:], scr["dt"][t0:t0 + 512, :].rearrange("(j p) c -> p j c", p=128), q="pool")
                if nxt is not None:
                    hn_next = hnr.next()
                    self.norm_b(nxt, 1, hn_next, pTr)
        self.end_phase()

    def sweep(self, final, scr, alog_d, x_d=None, xout_d=None, w_in=None, w_out=None, d_d=None, gnw_d=None,
              npre_d=None, npost_d=None):
        S = self.S; NB = self.NB; T = self.T; NCH = self.NCH
        d0 = 0 if final else 32
        qi, ei = (1, 2) if final else (3, 4)
        mk_u = 1 if final else 3
        mk_l = 2 if final else 4
        mk_c = 1 if final else 2
        with ExitStack() as es:
            if final:
                Wo = self.sb(es, "Wo", [128, 16, 1024], BF16)
                gnw = self.sb(es, "gnw", [128, 16], F32); self.load(gnw, gnw[:], gnw_d)
                with ExitStack() as es2:
                    stage = Rot([self.sb(es2, "stg", [128, 1024], F32) for _ in range(3)])
                    self.load_weight(es2, Wo, 0, w_out, 2048, 1024, stage, rowscale=(gnw, gnw))
                    S.barrier()
                wpo = self.sb(es, "wpo", [128, 1024], F32); self.load(wpo, wpo[:], npost_d.partition_broadcast(128))
            else:
                Dbc = self.sb(es, "Dbc", [128, 32], F32); self.load(Dbc, Dbc[:], d_d.partition_broadcast(128))
                dx = self.sb(es, "dx", [128, 2048], F32)
                Wz = self.sb(es, "Wz", [128, 8, 2048], BF16)
                with ExitStack() as es2:
                    stage = Rot([self.sb(es2, "stg", [128, 1024], F32) for _ in range(3)])
                    self.load_weight(es2, Wz, 0, w_in[:, 0:2048], 1024, 2048, stage)
                    S.barrier()
                wbc = self.sb(es, "wbc", [128, 1024], F32); self.load(wbc, wbc[:], npre_d.partition_broadcast(128))
            Abc = self.sb(es, "Abc", [128, 32], F32)
            self.load(Abc, Abc[:], alog_d.partition_broadcast(128))
            self.act(Abc, Abc[:], Abc, Abc[:], AF.Exp)
            self.ts(Abc, Abc[:], Abc, Abc[:], -1.0, None, ALU.mult)
            nl = 2
            xsr = Rot([self.sb(es, "xs", [128, 2048], BF16) for _ in range(nl)])
            btr = Rot([self.sb(es, "bt", [128, 1024], BF16) for _ in range(nl)])
            BTr = Rot([self.sb(es, "BT", [128, 8, 128], BF16) for _ in range(nl)])
            CTr = Rot([self.sb(es, "CT", [128, 8, 128], BF16) for _ in range(nl)])
            dtr = Rot([self.sb(es, "dt", [128, 64], F32) for _ in range(nl)])
            x0r = Rot([self.sb(es, "x0", [128, 1024], F32) for _ in range(nl)])
            sm32 = Rot([self.sb(es, "s32", [128, 32], F32) for _ in range(9)])
            scr_ = Rot([self.sb(es, "sc", [128, 96], F32) for _ in range(3)])
            rhsUr = Rot([self.sb_split(es, "rhsU", 4096, BF16, 2) for _ in range(1)])
            MTar = Rot([self.sb_split(es, "MTall", 4096, BF16, 8) for _ in range(2)])
            ydr = Rot([self.sb_split(es, "yd", 2048, F32, 4) for _ in range(2)])
            cbmr = Rot([self.sb_split(es, "cbm", 1024, BF16, 2) for _ in range(2)])
            xdtr = Rot([self.sb(es, "xdt", [128, 2048], BF16) for _ in range(2)])
            xdter = Rot([self.sb(es, "xdte", [128, 2048], BF16) for _ in range(2)])
            Er = Rot([self.sb(es, "E", [128, 512], BF16) for _ in range(2)])
            h, hP = self.sb_split(es, "h", 2048, F32, 4)
            hbf, hbP = self.sb_split(es, "hbf", 2048, BF16, 4)
            S.op("pool", lambda e: e.memset(h[:], 0.0), writes=hP)
            S.op("pool", lambda e: e.memset(hbf[:], 0.0), writes=hbP)
            tmpr = Rot([self.sb(es, "tmp", [128, 512], F32) for _ in range(2)])
            br = Rot([self.sb(es, "b", [128, 512], F32) for _ in range(2)])
            yaccr = Rot([self.sb_split(es, "yacc", 2048, F32, 4) for _ in range(1 if final else 2)])
            junk = self.sb(es, "junk", [128, 1024], BF16)
            small = Rot([self.sb(es, "sm", [128, 1], F32) for _ in range(12)])
            if final:
                ybr = Rot([self.sb(es, "yb", [128, 2048], BF16) for _ in range(nl)])
                zsr = Rot([self.sb(es, "zs", [128, 2048], BF16) for _ in range(nl)])
                yn = self.sb(es, "yn", [128, 2048], BF16)
                ynT = self.sb(es, "ynT", [128, 16, 128], BF16)
                ho = self.sb(es, "ho", [128, 1024], F32)
                xo = Rot([self.sb(es, "xo", [128, 1024], F32) for _ in range(2)])
                sm8 = Rot([self.sb(es, "s8", [128, 8], F32) for _ in range(6)])
            else:
                ybsr = Rot([self.sb(es, "ybs", [128, 2048], BF16) for _ in range(2)])
                hnt = self.sb(es, "hnt", [128, 1024], BF16)
                hnT = self.sb(es, "hnT", [128, 8, 128], BF16)
                zsr = Rot([self.sb(es, "zso", [128, 2048], BF16) for _ in range(2)])
            pg = Rot([self.ps(es, "pg", [128, 512], F32) for _ in range(6)])
            pT = self.ps(es, "pT", [128, 8, 128], BF16)
            psm = self.ps(es, "psm", [128, 512], F32)
            psr = Rot([View(psm[:, 96 * i:96 * (i + 1)], psm) for i in range(4)])
            ones1 = self.cf[:, 5, 0:1]
            order = list(range(NCH)) if final else list(range(NCH - 1, -1, -1))
            v3 = lambda ap: ap.rearrange("p (h d) -> p h d", h=8)

            def prep(c, P):
                b = c // 4; tk = c * 128
                xs = xsr.next(); bt = btr.next(); BT = BTr.next(); CT = CTr.next(); dt = dtr.next(); x0 = x0r.next()
                self.load(xs, xs[:], scr["xs"][tk:tk + 128, :])
                self.load(bt, bt[:], scr["Bt"][tk:tk + 128, :])
                self.load(BT, BT[:], scr["BT"][:, :, tk:tk + 128])
                self.load(CT, CT[:], scr["CT"][:, :, tk:tk + 128])
                self.load(dt, dt[:], scr["dt"][tk:tk + 128, :])
                self.load(x0, x0[:], x_d[tk:tk + 128, :])
                if final:
                    yb = ybr.next(); self.load(yb, yb[:], scr["yb"][tk:tk + 128, :])
                    zs = zsr.next(); self.load(zs, zs[:], scr["zs"][tk:tk + 128, :])
                    P["yb"] = yb; P["zs"] = zs
                    flag = self.fl[:, 2 * b + 1:2 * b + 2] if c % 4 == 3 else ones1
                else:
                    flag = self.fl[:, 2 * b:2 * b + 1] if c % 4 == 0 else ones1
                yield
                dtd = dt[:, d0:d0 + 32]
                dA = sm32.next()
                self.tt("dve", dA, dA[:], dt, dtd, Abc, Abc[:], ALU.mult)
                ps_ = psr.next()
                self.mm(ps_, ps_[:, 0:32], self.cf, self.cf[:, qi, :], dA, dA[:], True, True)
                self.mm(ps_, ps_[:, 32:64], self.cf, self.cf[:, ei, :], dA, dA[:], True, True)
                self.mm(ps_, ps_[:, 64:96], self.cf, self.cf[:, 5, :], dA, dA[:], True, True)
                sc = scr_.next()
                self.act(sc, sc[:], ps_, ps_[:], AF.Exp)
                w2 = sm32.next(); cdf = sm32.next()
                self.tt("dve", w2, w2[:], dt, dtd, sc, sc[:, 32:64], ALU.mult)
                self.ts(cdf, cdf[:], sc, sc[:, 64:96], flag, None, ALU.mult, extra=[self.fl, self.cf])
                yield
                xdt = xdtr.next(); xdte = xdter.next(); rhsU, rhsUP = rhsUr.next(); cbm, cbmP = cbmr.next()
                xs3 = xs[:].rearrange("p (h d) -> p h d", h=32)
                self.tt("pool", xdt, xdt[:].rearrange("p (h d) -> p h d", h=32), xs, xs3, dt,
                        dtd.unsqueeze(2).broadcast_to([128, 32, 64]), ALU.mult)
                self.tt("dve", xdte, xdte[:].rearrange("p (h d) -> p h d", h=32), xs, xs3, w2,
                        w2[:].unsqueeze(2).broadcast_to([128, 32, 64]), ALU.mult)
                yield
                for hf in range(2):
                    self.tt("pool", rhsUP[hf], rhsU[:, hf * 2048:(hf + 1) * 2048].rearrange("p (h t) -> p h t", h=16),
                            self.cf, self.cf[:, mk_u, :].unsqueeze(1).broadcast_to([128, 16, 128]),
                            dA, dA[:, hf * 16:(hf + 1) * 16].unsqueeze(2).broadcast_to([128, 16, 128]), ALU.mult)
                yield
                for hf in range(2):
                    pcb = pg.next()
                    for gg in range(4):
                        g = hf * 4 + gg
                        self.mm(pcb, pcb[:, gg * 128:(gg + 1) * 128], BT, BT[:, g, :], CT, CT[:, g, :], True, True)
                    self.tt("dve", cbmP[hf], cbm[:, hf * 512:(hf + 1) * 512].rearrange("p (g t) -> p g t", g=4),
                            pcb, pcb[:].rearrange("p (g t) -> p g t", g=4),
                            self.cb, self.cb[:, mk_c, :].unsqueeze(1).broadcast_to([128, 4, 128]), ALU.mult)
                yield
                MTa, MTP = MTar.next(); yd, ydP = ydr.next()

                def segA(g):
                    pseg = pg.next()
                    self.mm(pseg, pseg[:], self.cb, self.cb[:, mk_l, :], rhsUP[g // 4], rhsU[:, g * 512:(g + 1) * 512], True, True)
                    E = Er.next()
                    self.act(E, E[:], pseg, pseg[:], AF.Exp)
                    self.tt("dve", MTP[g], MTa[:, g * 512:(g + 1) * 512].rearrange("p (h t) -> p h t", h=4), E, E[:].rearrange("p (h t) -> p h t", h=4),
                            cbmP[g // 4], cbm[:, g * 128:(g + 1) * 128].unsqueeze(1).broadcast_to([128, 4, 128]), ALU.mult)

                pys = {}

                def ydB(g):
                    pr, gg = divmod(g, 2)
                    if gg == 0:
                        pys[pr] = pg.next()
                    py = pys[pr]
                    for hh in range(4):
                        hd = 4 * g + hh
                        self.mm(py, py[:, (gg * 4 + hh) * 64:(gg * 4 + hh + 1) * 64], MTP[g], MTa[:, hd * 128:(hd + 1) * 128],
                                xdt, xdt[:, hd * 64:(hd + 1) * 64], True, True)
                    if gg == 0:
                        return
                    c0 = pr * 512; h0 = pr * 8
                    if final:
                        self.tt("dve", ydP[pr], yd[:, c0:c0 + 512], py, py[:], yb, yb[:, c0:c0 + 512], ALU.add)
                    else:
                        self.copy("act", ydP[pr], yd[:, c0:c0 + 512], py, py[:])

                segA(0); segA(1)
                yield
                for g in range(2, 8):
                    segA(g)
                    ydB(g - 2)
                    yield
                ydB(6); ydB(7)
                P.update(xs=xs, bt=bt, CT=CT, sc=sc, cdf=cdf, flag=flag, xdte=xdte, yd=yd, ydP=ydP, x0=x0, tk=tk)

            def main(c, P):
                bt = P["bt"]; CT = P["CT"]; sc = P["sc"]; cdf = P["cdf"]; flag = P["flag"]
                xdte = P["xdte"]; yd = P["yd"]; ydP = P["ydP"]; x0 = P["x0"]; tk = P["tk"]
                yacc, yaP = yaccr.next()
                for pr in range(4):
                    pyo = pg.next(); pS = pg.next()
                    for gg in range(2):
                        g = 2 * pr + gg
                        self.mm(pyo, pyo[:, gg * 256:(gg + 1) * 256], CT, CT[:, g, :], hbP[pr], hbf[:, g * 256:(g + 1) * 256], True, True)
                    for gg in range(2):
                        g = 2 * pr + gg
                        self.mm(pS, pS[:, gg * 256:(gg + 1) * 256], bt, bt[:, g * 128:(g + 1) * 128],
                                xdte, xdte[:, g * 256:(g + 1) * 256], True, True)
                    c0 = pr * 512; h0 = pr * 8
                    tmp = tmpr.next()
                    self.tt("dve", tmp, v3(tmp[:]), hP[pr], v3(h[:, c0:c0 + 512]), cdf, cdf[:, h0:h0 + 8].unsqueeze(2).broadcast_to([128, 8, 64]), ALU.mult)
                    self.stt(hP[pr], h[:, c0:c0 + 512], pS, pS[:], None, flag, tmp, tmp[:], ALU.mult, ALU.add)
                    S.ops["dve"][-1].deps.extend(self.S._deps("dve", [self.fl, self.cf], []))
                    self.copy("act", hbP[pr], hbf[:, c0:c0 + 512], hP[pr], h[:, c0:c0 + 512])
                    bq = br.next()
                    self.tt("dve", bq, v3(bq[:]), pyo, v3(pyo[:]), sc, sc[:, h0:h0 + 8].unsqueeze(2).broadcast_to([128, 8, 64]), ALU.mult)
                    self.tt("pool", yaP[pr], yacc[:, c0:c0 + 512], ydP[pr], yd[:, c0:c0 + 512], bq, bq[:], ALU.add)
                    yield
                if not final:
                    ybs = ybsr.next(); xs = P["xs"]
                    S.op("pool", lambda e, xs=xs: e.tensor_tensor(dx[:].rearrange("p (h d) -> p h d", h=32), xs[:].rearrange("p (h d) -> p h d", h=32),
                         Dbc[:].unsqueeze(2).broadcast_to([128, 32, 64]), ALU.mult), reads=[xs, Dbc], writes=[dx])
                    S.op("pool", lambda e, ybs=ybs, yacc=yacc: e.tensor_tensor(ybs[:], yacc[:], dx[:], ALU.add), reads=yaP + [dx], writes=[ybs])
                    self.store(ybs, ybs[:], scr["yb"][tk:tk + 128, :], q="pool")
                    yield
                    self.rms_tile(x0, 128, wbc, hnt, small, junk)
                    for k in range(8):
                        self.tr(pT, pT[:, k, :], hnt, hnt[:, k * 128:(k + 1) * 128], self.cb, self.cb[:, 0, :])
                    self.copy("dve", hnT, hnT[:], pT, pT[:])
                    yield
                    zs = zsr.next()
                    for q in range(4):
                        pz = pg.next()
                        for k in range(8):
                            self.mm(pz, pz[:], hnT, hnT[:, k, :], Wz, Wz[:, k, 512 * q:512 * (q + 1)], k == 0, k == 7)
                        self.act(zs, zs[:, 512 * q:512 * (q + 1)], pz, pz[:], AF.Silu)
                        yield
                    self.store(zs, zs[:], scr["zs"][tk:tk + 128, :], q="pool")
                    return
                zs = P["zs"]
                S.op("dve", lambda e, yacc=yacc, zs=zs: e.tensor_tensor(yacc[:], yacc[:], zs[:], ALU.mult), reads=yaP + [zs], writes=yaP)
                yield
                ssg = sm8.next(); lg = sm8.next(); rg = sm8.next()
                for g in range(8):
                    self.act(yn, yn[:, g * 256:(g + 1) * 256], yaP[g // 2], yacc[:, g * 256:(g + 1) * 256], AF.Square,
                             accum=(ssg, ssg[:, g:g + 1]))
                self.act(lg, lg[:], ssg, ssg[:], AF.Ln, bias=self.epsb[:, 0:1], scale=1.0 / 256, extra=[self.epsb])
                self.act(rg, rg[:], lg, lg[:], AF.Exp, scale=-0.5)
                yield
                S.op("dve", lambda e, yacc=yacc, rg=rg: e.tensor_tensor(yn[:].rearrange("p (g d) -> p g d", g=8), yacc[:].rearrange("p (g d) -> p g d", g=8),
                     rg[:].unsqueeze(2).broadcast_to([128, 8, 256]), ALU.mult), reads=yaP + [rg], writes=[yn])
                for hf in range(2):
                    for k in range(8):
                        kk = hf * 8 + k
                        self.tr(pT, pT[:, k, :], yn, yn[:, kk * 128:(kk + 1) * 128], self.cb, self.cb[:, 0, :])
                    self.copy("act" if hf else "dve", ynT, ynT[:, hf * 8:(hf + 1) * 8, :], pT, pT[:])
                    yield
                for q in range(2):
                    po = pg.next()
                    for kk in range(16):
                        self.mm(po, po[:], ynT, ynT[:, kk, :], Wo, Wo[:, kk, 512 * q:512 * (q + 1)], kk == 0, kk == 15)
                    self.copy("act" if q else "dve", ho, ho[:, 512 * q:512 * (q + 1)], po, po[:])
                    yield
                self.post_norm_residual(ho, x0, wpo, xo.next(), junk, small, xout_d[tk:tk + 128, :])

            P = {}
            for _ in prep(order[0], P):
                pass
            for idx, c in enumerate(order):
                Pn = {}
                gens = [main(c, P)]
                if idx + 1 < len(order):
                    gens.append(prep(order[idx + 1], Pn))
                while gens:
                    for gnr in list(gens):
                        try:
                            next(gnr)
                        except StopIteration:
                            gens.remove(gnr)
                P = Pn
        self.end_phase()

    def post_norm_residual(self, hb, x0, wpo, xo, junk, small, dst):
        ss = small.next(); lg = small.next(); rs = small.next()
        self.act(junk, junk[:, 0:1024], hb, hb[:], AF.Square, accum=(ss, ss[:, 0:1]))
        self.act(lg, lg[:], ss, ss[:], AF.Ln, bias=self.epsb[:, 0:1], scale=1.0 / D, extra=[self.epsb])
        self.act(rs, rs[:], lg, lg[:], AF.Exp, scale=-0.5)
        self.stt(xo, xo[:], hb, hb[:], rs, rs[:, 0:1], wpo, wpo[:], ALU.mult, ALU.mult)
        self.tt("pool", xo, xo[:], xo, xo[:], x0, x0[:], ALU.add)
        self.store(xo, xo[:], dst, q="pool")

    def norm_a(self, x_d, b, halo, wbc, xt, hnt, small, junk):
        S = self.S; T = self.T; t0 = b * 512
        tiles = []
        for j in range(5):
            x = xt.next()
            if j < 4:
                self.load(x, x[:], x_d[t0 + 128 * j:t0 + 128 * (j + 1), :]); npart = 128
            else:
                lo = t0 - halo if t0 - halo >= 0 else 0
                hi = t0 + 512 if t0 + 512 + halo <= T else T - halo
                S.dma_load("sp", x, lambda e, s, x=x, lo=lo, hi=hi: (
                    e.dma_start(out=x[0:halo, :], in_=x_d[lo:lo + halo, :]).then_inc(s, 16),
                    e.dma_start(out=x[halo:2 * halo, :], in_=x_d[hi:hi + halo, :]).then_inc(s, 16)), n=2)
                npart = 2 * halo
            ht = hnt.next()
            self.rms_tile(x, npart, wbc, ht, small, junk)
            tiles.append((ht, npart))
        return tiles

    def norm_b(self, tiles, halo, hn, pTr):
        for j, (ht, npart) in enumerate(tiles):
            pT = pTr.next()
            if j < 4:
                for k in range(8):
                    self.tr(pT, pT[:, k, :], ht, ht[:, k * 128:(k + 1) * 128], self.cb, self.cb[:, 0, :])
                self.copy("act" if j % 2 else "dve", hn, hn[:, :, halo + 128 * j:halo + 128 * (j + 1)], pT, pT[:])
            else:
                for k in range(8):
                    self.tr(pT, pT[:, k, 0:npart], ht, ht[0:npart, k * 128:(k + 1) * 128], self.cb, self.cb[0:npart, 0, 0:npart])
                self.copy("dve", hn, hn[:, :, 0:halo], pT, pT[:, :, 0:halo])
                self.copy("dve", hn, hn[:, :, 512 + halo:512 + 2 * halo], pT, pT[:, :, halo:2 * halo])

    def norm_block(self, x_d, b, halo, wbc, hn, xt, hnt, small, junk, pTr):
        self.norm_b(self.norm_a(x_d, b, halo, wbc, xt, hnt, small, junk), halo, hn, pTr)

    def ffn(self, xin_d, xout_d, up_w, cw_d, cb_d, down_w, npre_d, npost_d, wdbf):
        S = self.S; NB = self.NB
        with ExitStack() as es:
            Wu = self.sb(es, "Wu", [128, 8, 5632], BF16)
            cw = self.sb(es, "cw", [128, 132], F32); self.load(cw, cw[:], cw_d)
            cbias = self.sb(es, "cbias", [128, 44], F32); self.load(cbias, cbias[:], cb_d)
            wbc = self.sb(es, "wbc", [128, 1024], F32); self.load(wbc, wbc[:], npre_d.partition_broadcast(128))
            wpo = self.sb(es, "wpo", [128, 1024], F32); self.load(wpo, wpo[:], npost_d.partition_broadcast(128))
            with ExitStack() as es2:
                stage = Rot([self.sb(es2, "stg", [128, 1024], F32) for _ in range(3)])
                stb = Rot([self.sb(es2, "stb", [128, 1024], BF16) for _ in range(2)])
                self.load_weight(es2, Wu, 0, up_w, 1024, 5632, stage)
                for kk in range(22):
                    st = stage.next(); sb_ = stb.next()
                    self.load(st, st[:], down_w[kk * 128:(kk + 1) * 128, :])
                    self.copy(("act", "dve")[kk % 2], sb_, sb_[:], st, st[:])
                    self.store(sb_, sb_[:], wdbf[kk], q="pool")
                S.barrier()
            xt = Rot([self.sb(es, "xt", [128, 1024], F32) for _ in range(3)])
            xrr = Rot([self.sb(es, "xr", [128, 1024], F32) for _ in range(2)])
            hnt = Rot([self.sb(es, "hnt", [128, 1024], BF16) for _ in range(5)])
            junk = self.sb(es, "junk", [128, 1024], BF16)
            small = Rot([self.sb(es, "sm", [128, 1], F32) for _ in range(12)])
            hnr = Rot([self.sb(es, "hn", [128, 8, 514], BF16) for _ in range(2)])
            ur = Rot([self.sb(es, "u", [128, 514], BF16) for _ in range(3)])
            ar = Rot([self.sb(es, "a", [128, 512], BF16) for _ in range(2)])
            dgr = Rot([self.sb(es, "dg", [128, 128], BF16) for _ in range(12)])
            wdr = Rot([self.sb(es, "wd", [128, 512], BF16) for _ in range(4)])
            hact = self.sb(es, "hact", [128, 22, 512], BF16)
            hos = [self.sb(es, "ho", [128, 1024], F32) for _ in range(4)]
            xo = Rot([self.sb(es, "xo", [128, 1024], F32) for _ in range(1)])
            pg = Rot([self.ps(es, "pg", [128, 512], F32) for _ in range(5)])
            pTr = Rot([self.ps(es, "pT", [128, 8, 128], BF16)])
            phbs = [self.ps(es, "ph", [128, 512], F32) for _ in range(2)]
            phr = Rot([View(pb[:, 0:2], pb) for pb in phbs])
            hn_next = hnr.next()
            self.norm_block(xin_d, 0, 1, wbc, hn_next, xt, hnt, small, junk, pTr)
            pending = []
            for b in range(NB):
                t0 = b * 512
                hn = hn_next
                us = {}
                av = {}

                def projA(kk, part):
                    m = kk + 22 * part
                    pm = pg.next(); ph = phr.next()
                    for k in range(8):
                        self.mm(pm, pm[:], Wu, Wu[:, k, 128 * m:128 * (m + 1)], hn, hn[:, k, 1:513], k == 0, k == 7)
                    for k in range(8):
                        self.mm(ph, ph[:], Wu, Wu[:, k, 128 * m:128 * (m + 1)], hn, hn[:, k, 0:514:513], k == 0, k == 7)
                    u = ur.next()
                    self.copy("act" if part else "dve", u, u[:, 1:513], pm, pm[:])
                    self.tt("dve", u, u[:, 0:514:513], ph, ph[:], self.fl, self.fl[:, 2 * b:2 * b + 2], ALU.mult)
                    us[(kk, part)] = u

                def convB(kk, part):
                    m = kk + 22 * part
                    u = us.pop((kk, part))
                    pc = pg.next()
                    for k in range(3):
                        dg = dgr.next()
                        self.tt("pool", dg, dg[:], self.cb, self.cb[:, 0, :], cw, cw[:, k * 44 + m:k * 44 + m + 1].broadcast_to([128, 128]), ALU.mult)
                        self.mm(pc, pc[:], dg, dg[:], u, u[:, k:k + 512], k == 0, k == 2)
                    if part == 0:
                        a = ar.next(); av[kk] = a
                        self.act(a, a[:], pc, pc[:], AF.Silu, bias=cbias[:, m:m + 1], extra=[cbias])
                    else:
                        a = av.pop(kk)
                        self.stt(hact, hact[:, kk, :], pc, pc[:], cbias, cbias[:, m:m + 1], a, a[:], ALU.add, ALU.mult)

                tiles = [(kk, part) for kk in range(22) for part in range(2)]
                projA(*tiles[0])
                for ti in range(1, len(tiles)):
                    projA(*tiles[ti])
                    convB(*tiles[ti - 1])
                    if pending and ti in (6, 16, 26, 36):
                        pending.pop(0)()
                convB(*tiles[-1])
                nxt = self.norm_a(xin_d, b + 1, 1, wbc, xt, hnt, small, junk) if b + 1 < NB else None
                for q in range(2):
                    pos = [pg.next() for _ in range(4)]
                    for kk in range(22):
                        wd = wdr.next()
                        self.load(wd, wd[:], wdbf[kk][:, 512 * q:512 * (q + 1)])
                        for j in range(4):
                            self.mm(pos[j], pos[j][:], hact, hact[:, kk, 128 * j:128 * (j + 1)], wd, wd[:], kk == 0, kk == 21)
                    for j in range(4):
                        self.copy("act" if j % 2 else "dve", hos[j], hos[j][:, 512 * q:512 * (q + 1)], pos[j], pos[j][:])
                if nxt is not None:
                    hn_next = hnr.next()
                    self.norm_b(nxt, 1, hn_next, pTr)
                def mk_post(j, t0=t0):
                    def run():
                        x0 = xrr.next()
                        self.load(x0, x0[:], xin_d[t0 + 128 * j:t0 + 128 * (j + 1), :])
                        self.post_norm_residual(hos[j], x0, wpo, xo.next(), junk, small, xout_d[t0 + 128 * j:t0 + 128 * (j + 1), :])
                    return run
                pending = [mk_post(j) for j in range(4)]
            for run in pending:
                run()
        self.end_phase()

    def conformer(self, xin_d, xout_d, pw1_w, pw1b_d, dw_d, dwb_d, lnw_d, lnb_d, pw2_w, pw2b_d, npre_d, npost_d):
        S = self.S; NB = self.NB; H = 15
        with ExitStack() as es:
            W1 = self.sb(es, "W1c", [128, 8, 2048], BF16)
            W2 = self.sb(es, "W2c", [128, 8, 1024], BF16)
            with ExitStack() as es2:
                stage = Rot([self.sb(es2, "stg", [128, 1024], F32) for _ in range(3)])
                self.load_weight(es2, W1, 0, pw1_w, 1024, 2048, stage)
                self.load_weight(es2, W2, 0, pw2_w, 1024, 1024, stage)
                S.barrier()
            b1 = self.sb(es, "b1", [128, 16], F32); self.load(b1, b1[:], pw1b_d)
            dw = self.sb(es, "dw", [128, 248], F32); self.load(dw, dw[:], dw_d)
            dwb = self.sb(es, "dwb", [128, 8], F32); self.load(dwb, dwb[:], dwb_d)
            lnw = self.sb(es, "lnw", [128, 8], F32); self.load(lnw, lnw[:], lnw_d)
            lnb = self.sb(es, "lnb", [128, 8], F32); self.load(lnb, lnb[:], lnb_d)
            wbc = self.sb(es, "wbc", [128, 1024], F32); self.load(wbc, wbc[:], npre_d.partition_broadcast(128))
            wpo = self.sb(es, "wpo", [128, 1024], F32); self.load(wpo, wpo[:], npost_d.partition_broadcast(128))
            b2bc = self.sb(es, "b2bc", [128, 1024], F32); self.load(b2bc, b2bc[:], pw2b_d.partition_broadcast(128))
            onesN = self.sb(es, "onesN", [128, 128], BF16)
            self.ts(onesN, onesN[:], self.cf, self.cf[:, 5, :], 1.0 / 1024, None, ALU.mult)
            xt = Rot([self.sb(es, "xt", [128, 1024], F32) for _ in range(3)])
            xrr = Rot([self.sb(es, "xr", [128, 1024], F32) for _ in range(2)])
            hnt = Rot([self.sb(es, "hnt", [128, 1024], BF16) for _ in range(5)])
            junk = self.sb(es, "junk", [128, 1024], BF16)
            small = Rot([self.sb(es, "sm", [128, 1], F32) for _ in range(12)])
            hnr = Rot([self.sb(es, "hn", [128, 8, 542], BF16) for _ in range(2)])
            glu = self.sb(es, "glu", [128, 8, 542], BF16)
            sgr = Rot([self.sb(es, "sg", [128, 512], F32) for _ in range(2)])
            s30 = Rot([self.sb(es, "s30", [128, 30], F32) for _ in range(4)])
            dgr = Rot([self.sb(es, "dg", [128, 128], BF16) for _ in range(16)])
            vf = self.sb(es, "vf", [128, 8, 512], F32)
            vb = self.sb(es, "vb", [128, 8, 512], BF16)
            vsq = self.sb(es, "vsq", [128, 8, 512], BF16)
            mean = self.sb(es, "mean", [128, 512], F32)
            var = self.sb(es, "var", [128, 512], F32)
            t1r = Rot([self.sb(es, "t1", [128, 512], F32) for _ in range(2)])
            sact = self.sb(es, "sact", [128, 8, 512], BF16)
            hos = [self.sb(es, "ho", [128, 1024], F32) for _ in range(4)]
            xo = Rot([self.sb(es, "xo", [128, 1024], F32) for _ in range(2)])
            pg = Rot([self.ps(es, "pg", [128, 512], F32) for _ in range(5)])
            pTr = Rot([self.ps(es, "pT", [128, 8, 128], BF16)])
            phbs = [self.ps(es, "ph", [128, 512], F32) for _ in range(2)]
            phr = Rot([View(pb[:, 32 * i:32 * i + 30], pb) for pb in phbs for i in range(2)])
            for b in range(NB):
                t0 = b * 512
                hn = hnr.next()
                self.norm_block(xin_d, b, H, wbc, hn, xt, hnt, small, junk, pTr)
                for m in range(8):
                    pa = pg.next(); pgt = pg.next(); pha = phr.next(); phg = phr.next()
                    for (pmain, phal, c0) in ((pa, pha, 128 * m), (pgt, phg, 1024 + 128 * m)):
                        for k in range(8):
                            self.mm(pmain, pmain[:], W1, W1[:, k, c0:c0 + 128], hn, hn[:, k, H:H + 512], k == 0, k == 7)
                        for k in range(8):
                            self.mm(phal, phal[:, 0:H], W1, W1[:, k, c0:c0 + 128], hn, hn[:, k, 0:H], k == 0, k == 7)
                        for k in range(8):
                            self.mm(phal, phal[:, H:2 * H], W1, W1[:, k, c0:c0 + 128], hn, hn[:, k, 512 + H:512 + 2 * H], k == 0, k == 7)
                    sg = sgr.next()
                    self.act(sg, sg[:], pgt, pgt[:], AF.Sigmoid, bias=b1[:, 8 + m:9 + m], extra=[b1])
                    self.stt(glu, glu[:, m, H:H + 512], pa, pa[:], b1, b1[:, m:m + 1], sg, sg[:], ALU.add, ALU.mult)
                    sh = s30.next(); th = s30.next()
                    self.act(sh, sh[:], phg, phg[:], AF.Sigmoid, bias=b1[:, 8 + m:9 + m], extra=[b1])
                    self.stt(th, th[:], pha, pha[:], b1, b1[:, m:m + 1], sh, sh[:], ALU.add, ALU.mult)
                    self.tt("dve", glu, glu[:, m, 0:H], th, th[:, 0:H], self.fl, self.fl[:, 2 * b:2 * b + 1].broadcast_to([128, H]), ALU.mult)
                    self.tt("dve", glu, glu[:, m, 512 + H:512 + 2 * H], th, th[:, H:2 * H], self.fl,
                            self.fl[:, 2 * b + 1:2 * b + 2].broadcast_to([128, H]), ALU.mult)
                for m in range(8):
                    pc = pg.next()
                    for k in range(31):
                        dg = dgr.next()
                        col = dw[:, k * 8 + m:k * 8 + m + 1]
                        if k % 3 == 0:
                            self.tt("pool", dg, dg[:], self.cb, self.cb[:, 0, :], dw, col.broadcast_to([128, 128]), ALU.mult)
                        elif k % 3 == 1:
                            self.ts(dg, dg[:], self.cb, self.cb[:, 0, :], col, None, ALU.mult, extra=[dw])
                        else:
                            self.S.op("act", lambda e, dg=dg, col=col: e.activation(dg[:], self.cb[:, 0, :], AF.Copy, scale=col),
                                      reads=[self.cb, dw], writes=[dg])
                        self.mm(pc, pc[:], dg, dg[:], glu, glu[:, m, k:k + 512], k == 0, k == 30)
                    self.act(vf, vf[:, m, :], pc, pc[:], AF.Identity, bias=dwb[:, m:m + 1], extra=[dwb])
                    self.act(vsq, vsq[:, m, :], pc, pc[:], AF.Square, bias=dwb[:, m:m + 1], extra=[dwb])
                    self.copy("dve", vb, vb[:, m, :], vf, vf[:, m, :])
                pmean = pg.next(); pmsq = pg.next()
                for m in range(8):
                    self.mm(pmean, pmean[:], onesN, onesN[:], vb, vb[:, m, :], m == 0, m == 7)
                for m in range(8):
                    self.mm(pmsq, pmsq[:], onesN, onesN[:], vsq, vsq[:, m, :], m == 0, m == 7)
                self.copy("dve", mean, mean[:], pmean, pmean[:])
                self.tt("pool", var, var[:], mean, mean[:], mean, mean[:], ALU.mult)
                self.tt("dve", var, var[:], pmsq, pmsq[:], var, var[:], ALU.subtract)
                self.ts(var, var[:], var, var[:], 0.0, None, ALU.max)
                self.act(var, var[:], var, var[:], AF.Ln, bias=self.epsb[:, 0:1], extra=[self.epsb])
                self.act(var, var[:], var, var[:], AF.Exp, scale=-0.5)
                for m in range(8):
                    t1 = t1r.next()
                    self.tt("pool", t1, t1[:], vf, vf[:, m, :], mean, mean[:], ALU.subtract)
                    self.tt("dve", t1, t1[:], t1, t1[:], var, var[:], ALU.mult)
                    self.S.op("act", lambda e, t1=t1, m=m: e.activation(sact[:, m, :], t1[:], AF.Silu, bias=lnb[:, m:m + 1], scale=lnw[:, m:m + 1]),
                              reads=[t1, lnb, lnw], writes=[sact])
                for j in range(4):
                    for q in range(2):
                        po = pg.next()
                        for m in range(8):
                            self.mm(po, po[:], sact, sact[:, m, 128 * j:128 * (j + 1)], W2, W2[:, m, 512 * q:512 * (q + 1)], m == 0, m == 7)
                        self.tt("dve", hos[j], hos[j][:, 512 * q:512 * (q + 1)], po, po[:], b2bc, b2bc[:, 512 * q:512 * (q + 1)], ALU.add)
                    x0 = xrr.next()
                    self.load(x0, x0[:], xin_d[t0 + 128 * j:t0 + 128 * (j + 1), :])
                    self.post_norm_residual(hos[j], x0, wpo, xo.next(), junk, small, xout_d[t0 + 128 * j:t0 + 128 * (j + 1), :])
        self.end_phase()


def build(NB, dbg=(), upto=6):
    B = Builder(NB, dbg)
    T = B.T
    x_d = B.din("x", [T, D])
    w_in = B.din("ssm_in_w", [D, 6208])
    cw1 = B.din("ssm_cw", [128, 96]); cb1 = B.din("ssm_cb", [128, 32]); dtb = B.din("ssm_dtb", [64, 1])
    alog = B.din("ssm_a_log", [2, 32]); dsk = B.din("ssm_d", [1, 32]); gnw = B.din("ssm_gnw", [128, 16])
    w_out = B.din("ssm_out_w", [DI, D])
    npm = B.din("norm_pre_mix", [2, D]); npo = B.din("norm_post_mix", [2, D])
    npf = B.din("norm_pre_ffn", [2, D]); npof = B.din("norm_post_ffn", [2, D])
    up_w = B.din("ffn_up_w", [2, D, 2 * FF]); dn_w = B.din("ffn_down_w", [2, FF, D])
    fcw = B.din("ffn_cw", [2, 128, 132]); fcb = B.din("ffn_cb", [2, 128, 44])
    pw1 = B.din("cf_pw1_w", [D, 2 * D]); pw1b = B.din("cf_pw1_b", [128, 16])
    dww = B.din("cf_dw", [128, 248]); dwb = B.din("cf_dwb", [128, 8])
    lnw = B.din("cf_lnw", [128, 8]); lnb = B.din("cf_lnb", [128, 8])
    pw2 = B.din("cf_pw2_w", [D, D]); pw2b = B.din("cf_pw2_b", [1, D])
    scr = {
        "xs": B.dscr("xs", [T, DI], BF16), "Bt": B.dscr("Bt", [T, 1024], BF16),
        "BT": B.dscr("BT", [128, 8, T], BF16), "CT": B.dscr("CT", [128, 8, T], BF16),
        "dt": B.dscr("dt", [T, 64], F32), "yb": B.dscr("yb", [T, DI], BF16), "zs": B.dscr("zs", [T, DI], BF16),
    }
    x1 = B.dscr("x1", [T, D], F32); x2 = B.dscr("x2", [T, D], F32); x3 = B.dscr("x3", [T, D], F32)
    wdbf = B.dscr("wdbf", [22, 128, D], BF16)
    y_d = B.nc.dram_tensor("y", [T, D], F32, kind="ExternalOutput").ap()
    with ExitStack() as es:
        B.setup_consts(es)
        B.cur_bufs = []
        B.phase1(x_d, w_in, cw1, cb1, dtb, npm[0:1, :], scr)
        if upto >= 2:
            B.sweep(False, scr, alog[1:2, :], x_d=x_d, w_in=w_in, npre_d=npm[0:1, :], d_d=dsk)
        if upto >= 3:
            B.sweep(True, scr, alog[0:1, :], x_d=x_d, xout_d=x1, w_in=w_in, w_out=w_out, d_d=dsk, gnw_d=gnw,
                    npre_d=npm[0:1, :], npost_d=npo[0:1, :])
        if upto >= 4:
            B.ffn(x1, x2, up_w[0], fcw[0], fcb[0], dn_w[0], npf[0:1, :], npof[0:1, :], wdbf)
        if upto >= 5:
            B.conformer(x2, x3, pw1, pw1b, dww, dwb, lnw, lnb, pw2, pw2b, npm[1:2, :], npo[1:2, :])
        if upto >= 6:
            B.ffn(x3, y_d, up_w[1], fcw[1], fcb[1], dn_w[1], npf[1:2, :], npof[1:2, :], wdbf)
        B.S.emit()
    return B


def make_consts():
    r = np.arange(128)
    ident = np.eye(128, dtype=np.float32)
    LE = (r[:, None] <= r[None, :]).astype(np.float32)
    GT = (r[:, None] > r[None, :]).astype(np.float32)
    GE = (r[:, None] >= r[None, :]).astype(np.float32)
    LT = (r[:, None] < r[None, :]).astype(np.float32)
    ones = np.ones((128, 128), np.float32)
    return np.ascontiguousarray(np.concatenate([ident, LE, GT, GE, LT, ones], axis=1))


def colmajor(v, ncol):
    return np.ascontiguousarray(np.asarray(v, np.float32).reshape(ncol, 128).T)


SLOT = 4096
NB_FULL = 24
_CACHE = {}


def _layout_weights(w):
    f = lambda a: np.ascontiguousarray(np.asarray(a, np.float32))
    out = {
        "ssm_in_w": f(w["ssm_in_w"][0]),
        "ssm_cw": colmajor(f(w["ssm_conv_w"][0]).reshape(-1), 96),
        "ssm_cb": colmajor(w["ssm_conv_b"][0], 32),
        "ssm_dtb": f(w["ssm_dt_bias"][0]).reshape(64, 1),
        "ssm_a_log": f(w["ssm_a_log"][0]),
        "ssm_d": f(w["ssm_d"][0]).reshape(1, 32),
        "ssm_gnw": colmajor(w["ssm_norm_w"][0], 16),
        "ssm_out_w": f(w["ssm_out_w"][0]),
        "norm_pre_mix": f(w["norm_pre_mix"]), "norm_post_mix": f(w["norm_post_mix"]),
        "norm_pre_ffn": f(w["norm_pre_ffn"]), "norm_post_ffn": f(w["norm_post_ffn"]),
        "ffn_up_w": f(w["ffn_up_w"]), "ffn_down_w": f(w["ffn_down_w"]),
        "ffn_cw": np.stack([colmajor(f(w["ffn_conv_w"][l]).reshape(-1), 132) for l in range(2)]),
        "ffn_cb": np.stack([colmajor(w["ffn_conv_b"][l], 44) for l in range(2)]),
        "cf_pw1_w": f(w["cf_pw1_w"][0]), "cf_pw1_b": colmajor(w["cf_pw1_b"][0], 16),
        "cf_dw": colmajor(f(w["cf_dw_w"][0]).reshape(-1), 248),
        "cf_dwb": colmajor(w["cf_dw_b"][0], 8), "cf_lnw": colmajor(w["cf_ln_w"][0], 8), "cf_lnb": colmajor(w["cf_ln_b"][0], 8),
        "cf_pw2_w": f(w["cf_pw2_w"][0]), "cf_pw2_b": f(w["cf_pw2_b"][0]).reshape(1, D),
    }
    return out


def kernel(x_prompt, x_sample, **w):
    x_prompt = np.asarray(x_prompt, np.float32); x_sample = np.asarray(x_sample, np.float32)
    plan = []
    for c in range(4):
        plan.append([("p", 3 * c), ("p", 3 * c + 1), ("p", 3 * c + 2)])
    for c in range(2):
        plan.append([("p", 12 + 2 * c), ("p", 13 + 2 * c), None])
    for c in range(2):
        plan.append([("s", c, 0), ("s", c, 1), None])
    wl = _layout_weights(w)
    consts = make_consts()
    in_maps = []
    for c in range(NCORES):
        xs = np.zeros((3 * SLOT, D), np.float32)
        fl = np.ones((NB_FULL, 2), np.float32)
        for si, ent in enumerate(plan[c]):
            b0 = si * 8
            fl[b0, 0] = 0.0; fl[b0 + 7, 1] = 0.0
            if ent is None:
                continue
            if ent[0] == "p":
                xs[si * SLOT:(si + 1) * SLOT] = x_prompt[ent[1]]
            else:
                xs[si * SLOT:(si + 1) * SLOT] = x_sample[ent[1], ent[2] * SLOT:(ent[2] + 1) * SLOT]
                if ent[2] == 0:
                    fl[b0 + 7, 1] = 1.0
                else:
                    fl[b0, 0] = 1.0
        m = dict(wl)
        m["x"] = xs
        m["flags"] = np.ascontiguousarray(np.broadcast_to(fl.reshape(1, -1), (128, NB_FULL * 2)))
        m["consts"] = consts
        in_maps.append(m)
    if "prog" not in _CACHE:
        _CACHE["prog"] = build(NB_FULL)
    res = run_bass_kernel_spmd(_CACHE["prog"].nc, in_maps, core_ids=list(range(NCORES)))
    y_prompt = np.empty_like(x_prompt); y_sample = np.empty_like(x_sample)
    for c in range(NCORES):
        y = np.asarray(res.results[c]["y"], np.float32)
        for si, ent in enumerate(plan[c]):
            if ent is None:
                continue
            if ent[0] == "p":
                y_prompt[ent[1]] = y[si * SLOT:(si + 1) * SLOT]
            else:
                y_sample[ent[1], ent[2] * SLOT:(ent[2] + 1) * SLOT] = y[si * SLOT:(si + 1) * SLOT]
    return (y_prompt, y_sample)
```

```python
from contextlib import ExitStack
import numpy as np
import concourse.bass as bass
import concourse.mybir as mybir
from concourse.bass_utils import run_bass_kernel_spmd

F32 = mybir.dt.float32
BF16 = mybir.dt.bfloat16
AF = mybir.ActivationFunctionType
ALU = mybir.AluOpType
AX = mybir.AxisListType

ENGS = ("pe", "act", "dve", "pool", "sp")
D = 1024; DI = 2048; NH = 32; NG = 8; HP = 64; NS = 128; CD = 4096; FF = 2816
EPS = 1e-6
NCORES = 8


class Buf:
    __slots__ = ("t", "name", "lw", "rd", "dsem", "dcount")

    def __init__(self, t, name):
        self.t = t; self.name = name; self.lw = None; self.rd = []; self.dsem = None; self.dcount = 0

    def __getitem__(self, k):
        return self.t[k]


class View:
    __slots__ = ("t", "root")

    def __init__(self, ap, root):
        self.t = ap; self.root = root

    def __getitem__(self, k):
        return self.t[k]


def _roots(bufs):
    return [getattr(b, "root", b) for b in bufs]


class Op:
    __slots__ = ("eng", "fn", "deps", "semval", "needed", "dma")

    def __init__(self, eng, fn, deps, dma=None):
        self.eng = eng; self.fn = fn; self.deps = deps; self.semval = None; self.needed = False; self.dma = dma


class Sched:
    def __init__(self, nc):
        self.nc = nc
        self.ops = {e: [] for e in ENGS}
        self.all = []
        self.semfinal = {}
        self.sem_pool = {"hw": [], "sw": []}
        self.last = {e: None for e in ENGS}

    def _deps(self, eng, reads, writes):
        reads = _roots(reads); writes = _roots(writes)
        deps = []
        for b in reads:
            if b.lw is not None:
                deps.append((b.lw, "raw"))
        for b in writes:
            if b.lw is not None:
                deps.append((b.lw, "waw"))
            deps.extend((r, "war") for r in b.rd)
        out = []
        for d, kind in deps:
            if d[0] == "op" and d[1].eng == eng:
                if eng == "pe":
                    continue
                if kind != "raw":
                    continue
            out.append(d)
        return out

    def op(self, eng, fn, reads=(), writes=()):
        reads = _roots(reads); writes = _roots(writes)
        o = Op(eng, fn, self._deps(eng, reads, writes))
        self.all.append(o); self.ops[eng].append(o); self.last[eng] = o
        tok = ("op", o)
        for b in reads:
            b.rd.append(tok)
        for b in writes:
            b.lw = tok; b.rd = []
        return o

    def _dsem(self, b, q):
        qc = "sw" if q == "pool" else "hw"
        if b.dsem is None:
            b.dsem = {}
        if qc not in b.dsem:
            if self.sem_pool[qc]:
                b.dsem[qc] = list(self.sem_pool[qc].pop())
            else:
                b.dsem[qc] = [self.nc.alloc_semaphore("d%s_%s" % (qc, b.name)), 0]
        return b.dsem[qc]

    def release(self, bufs):
        for b in bufs:
            if b.dsem:
                for qc, (sm, cnt) in b.dsem.items():
                    self.sem_pool[qc].append((sm, cnt))
                b.dsem = None

    def dma_load(self, q, dst, fn, n=1):
        deps = self._deps(q, [], [dst])
        ent = self._dsem(dst, q)
        ent[1] += 16 * n
        sem, cnt = ent
        self.semfinal[sem.num] = (sem, cnt)
        o = Op(q, fn, deps, dma=sem)
        self.all.append(o); self.ops[q].append(o)
        dst.lw = ("dma", sem, cnt); dst.rd = []
        return o

    def dma_store(self, q, src, fn, n=1):
        deps = self._deps(q, [src], [])
        ent = self._dsem(src, q)
        ent[1] += 16 * n
        sem, cnt = ent
        self.semfinal[sem.num] = (sem, cnt)
        o = Op(q, fn, deps, dma=sem)
        self.all.append(o); self.ops[q].append(o)
        src.rd.append(("dma", sem, cnt))
        return o

    def barrier(self):
        toks = [("op", self.last[e]) for e in ENGS if self.last[e] is not None and self.last[e].dma is None]
        toks += [("dma", sm, v) for (sm, v) in self.semfinal.values()]
        for e in ENGS:
            o = Op(e, None, list(toks))
            self.all.append(o); self.ops[e].append(o)

    def emit(self):
        nc = self.nc
        for o in self.all:
            for d in o.deps:
                if d[0] == "op":
                    d[1].needed = True
        esem = {e: nc.alloc_semaphore("e_" + e) for e in ENGS}
        for e in ENGS:
            c = 0
            for o in self.ops[e]:
                if o.needed:
                    c += 1; o.semval = c
        finals = list(self.semfinal.values())

        def run(e, engine):
            waited = {}
            for o in self.ops[e]:
                need = {}
                for d in o.deps:
                    if d[0] == "op":
                        if d[1].eng == e and e == "pe":
                            continue
                        s, v = esem[d[1].eng], d[1].semval
                    else:
                        s, v = d[1], d[2]
                    if need.get(s.num, (None, 0))[1] < v:
                        need[s.num] = (s, v)
                for k, (s, v) in need.items():
                    if waited.get(k, 0) < v:
                        engine.wait_ge(s, v); waited[k] = v
                if o.fn is None:
                    continue
                if o.dma is not None:
                    o.fn(engine, o.dma)
                else:
                    ins = o.fn(engine)
                    if o.needed:
                        ins.then_inc(esem[e], 1)
            if e == "sp":
                for (s, v) in finals:
                    if waited.get(s.num, 0) < v:
                        engine.wait_ge(s, v)

        with nc.Block() as block:
            block.tensor(lambda eng: run("pe", eng))
            block.scalar(lambda eng: run("act", eng))
            block.vector(lambda eng: run("dve", eng))
            block.gpsimd(lambda eng: run("pool", eng))
            block.sync(lambda eng: run("sp", eng))


class Rot:
    def __init__(self, bufs):
        self.b = bufs; self.i = 0

    def next(self):
        r = self.b[self.i % len(self.b)]; self.i += 1
        return r


class Builder:
    def __init__(self, NB, dbg=()):
        self.NB = NB; self.T = NB * 512; self.NCH = self.T // 128
        self.dbg = dbg
        self.nc = bass.Bass("TRN2", target_bir_lowering=False)
        self.S = Sched(self.nc)
        self.ein = {}
        self.cnt = 0
        self.cur_bufs = []

    def end_phase(self):
        self.S.barrier()
        self.S.release(self.cur_bufs)
        self.cur_bufs = []

    def din(self, name, shape, dt=F32):
        self.ein[name] = self.nc.dram_tensor(name, list(shape), dt, kind="ExternalInput").ap()
        return self.ein[name]

    def dscr(self, name, shape, dt):
        kind = "ExternalOutput" if name in self.dbg else "Internal"
        return self.nc.dram_tensor(name, list(shape), dt, kind=kind).ap()

    def sb(self, es, name, shape, dt):
        self.cnt += 1
        t = es.enter_context(self.nc.sbuf_tensor("%s_%d" % (name, self.cnt), list(shape), dt))
        b = Buf(t, "%s_%d" % (name, self.cnt))
        self.cur_bufs.append(b)
        return b

    def sb_split(self, es, name, ncols, dt, nparts):
        whole = self.sb(es, name, [128, ncols], dt)
        w = ncols // nparts
        parts = [Buf(whole.t[:, i * w:(i + 1) * w], "%s_p%d" % (whole.name, i)) for i in range(nparts)]
        self.cur_bufs.extend(parts)
        return whole.t, parts

    def ps(self, es, name, shape, dt=F32):
        self.cnt += 1
        t = es.enter_context(self.nc.psum_tensor("%s_%d" % (name, self.cnt), list(shape), dt))
        return Buf(t, "%s_%d" % (name, self.cnt))

    def mm(self, ob, o, lb, l, rb, r, start, stop):
        self.S.op("pe", lambda e: e.matmul(o, l, r, start=start, stop=stop), reads=[lb, rb], writes=[ob])

    def tr(self, ob, o, ib, i, idb, idn):
        self.S.op("pe", lambda e: e.transpose(o, i, idn), reads=[ib, idb], writes=[ob])

    def act(self, ob, o, ib, i, func, bias=None, scale=1.0, accum=None, extra=(), eng="act"):
        kw = {}
        if bias is not None:
            kw["bias"] = bias
        if accum is not None:
            kw["accum_out"] = accum[1]
        wr = [ob] + ([accum[0]] if accum is not None else [])
        self.S.op("act", lambda e: e.activation(o, i, func, scale=scale, **kw), reads=[ib] + list(extra), writes=wr)

    def copy(self, eng, ob, o, ib, i):
        if eng == "act":
            self.S.op("act", lambda e: e.copy(o, i), reads=[ib], writes=[ob])
        else:
            self.S.op(eng, lambda e: e.tensor_copy(o, i), reads=[ib], writes=[ob])

    def tt(self, eng, ob, o, ab, a, bb, b_, op):
        self.S.op(eng, lambda e: e.tensor_tensor(o, a, b_, op), reads=[ab, bb], writes=[ob])

    def ts(self, ob, o, ib, i, s1, s2, op0, op1=None, extra=()):
        if op1 is None:
            self.S.op("dve", lambda e: e.tensor_scalar(o, i, s1, None, op0), reads=[ib] + list(extra), writes=[ob])
        else:
            self.S.op("dve", lambda e: e.tensor_scalar(o, i, s1, s2, op0, op1), reads=[ib] + list(extra), writes=[ob])

    def stt(self, ob, o, ab, a, sb_, s, bb, b_, op0, op1):
        rd = [ab, bb] + ([sb_] if sb_ is not None else [])
        self.S.op("dve", lambda e: e.scalar_tensor_tensor(o, a, s, b_, op0, op1), reads=rd, writes=[ob])

    def load(self, dst, o, src, q="sp"):
        self.S.dma_load(q, dst, lambda e, s: e.dma_start(out=o, in_=src).then_inc(s, 16))

    def store(self, srcb, i, dst, q="sp"):
        self.S.dma_store(q, srcb, lambda e, s: e.dma_start(out=dst, in_=i).then_inc(s, 16))

    def setup_consts(self, es):
        c = self.din("consts", [128, 6 * 128])
        self.cf = self.sb(es, "cf", [128, 6, 128], F32)
        self.cb = self.sb(es, "cbf", [128, 6, 128], BF16)
        self.load(self.cf, self.cf[:], c.rearrange("p (a b) -> p a b", a=6))
        self.copy("dve", self.cb, self.cb[:], self.cf, self.cf[:])
        self.fl = self.sb(es, "fl", [128, self.NB * 2], F32)
        self.load(self.fl, self.fl[:], self.din("flags", [128, self.NB * 2]))
        self.epsb = self.sb(es, "epsb", [128, 1], F32)
        self.S.op("pool", lambda e: e.memset(self.epsb[:], EPS), writes=[self.epsb])

    def load_weight(self, es, dst, dcol0, src, K, N, stage, rowscale=None):
        engs = ["act", "dve", "pool"] if rowscale is None else ["act", "dve"]
        i = 0
        for k in range(K // 128):
            for c0 in range(0, N, 1024):
                n = min(1024, N - c0)
                st = stage.next()
                self.load(st, st[:, 0:n], src[k * 128:(k + 1) * 128, c0:c0 + n])
                o = dst[:, k, dcol0 + c0:dcol0 + c0 + n]
                e = engs[i % len(engs)]; i += 1
                if rowscale is None:
                    self.copy(e, dst, o, st, st[:, 0:n])
                else:
                    rb, r = rowscale
                    self.ts(dst, o, st, st[:, 0:n], r[:, k:k + 1], None, ALU.mult, extra=[rb])

    def make_diag(self, es, diag, cw, n):
        for i in range(n):
            if i % 2 == 0:
                self.ts(diag, diag[:, i, :], self.cb, self.cb[:, 0, :], cw[:, i:i + 1], None, ALU.mult, extra=[cw])
            else:
                self.S.op("act", lambda e, i=i: e.activation(diag[:, i, :], self.cb[:, 0, :], AF.Copy, scale=cw[:, i:i + 1]),
                          reads=[self.cb, cw], writes=[diag])

    def rms_tile(self, xb, npart, wbc, hn_tok, small, junk):
        ss = small.next(); lg = small.next(); rs = small.next()
        self.act(junk, junk[0:npart, 0:1024], xb, xb[0:npart, :], AF.Square, accum=(ss, ss[0:npart, 0:1]))
        self.act(lg, lg[0:npart, :], ss, ss[0:npart, :], AF.Ln, bias=self.epsb[0:npart, 0:1], scale=1.0 / D, extra=[self.epsb])
        self.act(rs, rs[0:npart, :], lg, lg[0:npart, :], AF.Exp, scale=-0.5)
        self.stt(hn_tok, hn_tok[0:npart, :], xb, xb[0:npart, :], rs, rs[0:npart, 0:1], wbc, wbc[0:npart, :], ALU.mult, ALU.mult)

    def phase1(self, x_d, w_in, cw_d, cb_d, dtb_d, nw_d, scr):
        S = self.S; NB = self.NB; T = self.T
        with ExitStack() as es:
            W = self.sb(es, "W1", [128, 8, 4160], BF16)
            stage = Rot([self.sb(es, "stg", [128, 1024], F32) for _ in range(3)])
            self.load_weight(es, W, 0, w_in[:, 2048:6208], 1024, 4160, stage)
            cw = self.sb(es, "cw", [128, 96], F32); self.load(cw, cw[:], cw_d)
            cbias = self.sb(es, "cbias", [128, 32], F32); self.load(cbias, cbias[:], cb_d)
            dtb = self.sb(es, "dtb", [128, 1], F32); self.load(dtb, dtb[0:64, :], dtb_d)
            wbc = self.sb(es, "wbc", [128, 1024], F32); self.load(wbc, wbc[:], nw_d.partition_broadcast(128))
            diag = self.sb(es, "diag", [128, 96, 128], BF16)
            self.make_diag(es, diag, cw, 96)
            xt = Rot([self.sb(es, "xt", [128, 1024], F32) for _ in range(6)])
            hnt = Rot([self.sb(es, "hnt", [128, 1024], BF16) for _ in range(5)])
            junk = self.sb(es, "junk", [128, 1024], BF16)
            small = Rot([self.sb(es, "sm", [128, 1], F32) for _ in range(12)])
            hnr = Rot([self.sb(es, "hn", [128, 8, 514], BF16) for _ in range(2)])
            ur = Rot([self.sb(es, "u", [128, 514], BF16) for _ in range(3)])
            xgr = Rot([self.sb(es, "xg", [128, 8, 512], BF16) for _ in range(2)])
            tokr = Rot([self.sb(es, "tok", [128, 4, 1024], BF16) for _ in range(2)])
            dte = self.sb(es, "dte", [128, 512], F32)
            dtT = self.sb(es, "dtT", [128, 512], F32)
            dtk = Rot([self.sb(es, "dtk", [128, 4, 64], F32) for _ in range(2)])
            pTr = Rot([self.ps(es, "pT", [128, 8, 128], BF16) for _ in range(2)])
            pmr = Rot([self.ps(es, "pm", [128, 512], F32) for _ in range(2)])
            pcr = Rot([self.ps(es, "pc", [128, 512], F32) for _ in range(2)])
            phbs = [self.ps(es, "ph", [128, 512], F32) for _ in range(2)]
            phr = Rot([View(pb[:, 0:2], pb) for pb in phbs])
            pdr = Rot([View(pb[:, 0:64], pb) for pb in pmr.b])
            ev = 0
            hn_next = hnr.next()
            self.norm_block(x_d, 0, 1, wbc, hn_next, xt, hnt, small, junk, pTr)
            for b in range(NB):
                t0 = b * 512
                hn = hn_next
                nxt = None
                us = {}
                xgs = {}

                def projA(m):
                    pm = pmr.next(); ph = phr.next()
                    for k in range(8):
                        self.mm(pm, pm[:], W, W[:, k, 128 * m:128 * (m + 1)], hn, hn[:, k, 1:513], k == 0, k == 7)
                    for k in range(8):
                        self.mm(ph, ph[:], W, W[:, k, 128 * m:128 * (m + 1)], hn, hn[:, k, 0:514:513], k == 0, k == 7)
                    u = ur.next()
                    self.copy("act" if m % 2 else "dve", u, u[:, 1:513], pm, pm[:])
                    self.tt("dve", u, u[:, 0:514:513], ph, ph[:], self.fl, self.fl[:, 2 * b:2 * b + 2], ALU.mult)
                    us[m] = u

                def convB(m):
                    G, i = divmod(m, 8)
                    if i == 0:
                        xgs[G] = xgr.next()
                    xg = xgs[G]; u = us.pop(m)
                    pc = pcr.next()
                    for k in range(3):
                        self.mm(pc, pc[:], diag, diag[:, k * 32 + m, :], u, u[:, k:k + 512], k == 0, k == 2)
                    self.act(xg, xg[:, i, :], pc, pc[:], AF.Silu, bias=cbias[:, m:m + 1], extra=[cbias])
                    if i < 7:
                        return
                    if G < 3:
                        tok = tokr.next()
                        for j in range(4):
                            pT = pTr.next()
                            for ii in range(8):
                                self.tr(pT, pT[:, ii, :], xg, xg[:, ii, 128 * j:128 * (j + 1)], self.cb, self.cb[:, 0, :])
                            self.copy("act" if j % 2 else "dve", tok, tok[:, j, :], pT, pT[:].rearrange("p a b -> p (a b)"))
                        if G < 2:
                            dst = scr["xs"][t0:t0 + 512, G * 1024:(G + 1) * 1024]
                        else:
                            dst = scr["Bt"][t0:t0 + 512, :]
                        self.store(tok, tok[:], dst.rearrange("(j p) c -> p j c", p=128), q="pool")
                    if G == 2:
                        self.store(xg, xg[:], scr["BT"][:, :, t0:t0 + 512], q="pool")
                    if G == 3:
                        self.store(xg, xg[:], scr["CT"][:, :, t0:t0 + 512], q="pool")

                projA(0)
                for m in range(1, 32):
                    projA(m)
                    convB(m - 1)
                    if m == 24 and b + 1 < NB:
                        nxt = self.norm_a(x_d, b + 1, 1, wbc, xt, hnt, small, junk)
                convB(31)
                pm = pmr.next()
                for k in range(8):
                    self.mm(pm, pm[0:64, :], W, W[:, k, 4096:4160], hn, hn[:, k, 1:513], k == 0, k == 7)
                self.act(dte, dte[0:64, :], pm, pm[0:64, :], AF.Exp, bias=dtb[0:64, 0:1], extra=[dtb])
                self.act(dtT, dtT[0:64, :], dte, dte[0:64, :], AF.Ln, bias=1.0)
                dk = dtk.next()
                for j in range(4):
                    pd = pdr.next()
                    self.tr(pd, pd[:], dtT, dtT[0:64, 128 * j:128 * (j + 1)], self.cf, self.cf[0:64, 0, 0:64])
                    self.copy("dve", dk, dk[:, j, :], pd, pd[:])
                self.store(dk, dk[:], scr["dt"][t0:t0 + 512, :].rearrange("(j p) c -> p j c", p=128), q="pool")
                if nxt is not None:
                    hn_next = hnr.next()
                    self.norm_b(nxt, 1, hn_next, pTr)
        self.end_phase()

    def sweep(self, final, scr, alog_d, x_d=None, xout_d=None, w_in=None, w_out=None, d_d=None, gnw_d=None,
              npre_d=None, npost_d=None):
        S = self.S; NB = self.NB; T = self.T; NCH = self.NCH
        d0 = 0 if final else 32
        qi, ei = (1, 2) if final else (3, 4)
        mk_u = 1 if final else 3
        mk_l = 2 if final else 4
        mk_c = 1 if final else 2
        with ExitStack() as es:
            if final:
                Wo = self.sb(es, "Wo", [128, 16, 1024], BF16)
                gnw = self.sb(es, "gnw", [128, 16], F32); self.load(gnw, gnw[:], gnw_d)
                with ExitStack() as es2:
                    stage = Rot([self.sb(es2, "stg", [128, 1024], F32) for _ in range(3)])
                    self.load_weight(es2, Wo, 0, w_out, 2048, 1024, stage, rowscale=(gnw, gnw))
                    S.barrier()
                wpo = self.sb(es, "wpo", [128, 1024], F32); self.load(wpo, wpo[:], npost_d.partition_broadcast(128))
            else:
                Dbc = self.sb(es, "Dbc", [128, 32], F32); self.load(Dbc, Dbc[:], d_d.partition_broadcast(128))
                dx = self.sb(es, "dx", [128, 2048], F32)
                Wz = self.sb(es, "Wz", [128, 8, 2048], BF16)
                with ExitStack() as es2:
                    stage = Rot([self.sb(es2, "stg", [128, 1024], F32) for _ in range(3)])
                    self.load_weight(es2, Wz, 0, w_in[:, 0:2048], 1024, 2048, stage)
                    S.barrier()
                wbc = self.sb(es, "wbc", [128, 1024], F32); self.load(wbc, wbc[:], npre_d.partition_broadcast(128))
            Abc = self.sb(es, "Abc", [128, 32], F32)
            self.load(Abc, Abc[:], alog_d.partition_broadcast(128))
            self.act(Abc, Abc[:], Abc, Abc[:], AF.Exp)
            self.ts(Abc, Abc[:], Abc, Abc[:], -1.0, None, ALU.mult)
            nl = 2
            xsr = Rot([self.sb(es, "xs", [128, 2048], BF16) for _ in range(nl)])
            btr = Rot([self.sb(es, "bt", [128, 1024], BF16) for _ in range(nl)])
            BTr = Rot([self.sb(es, "BT", [128, 8, 128], BF16) for _ in range(nl)])
            CTr = Rot([self.sb(es, "CT", [128, 8, 128], BF16) for _ in range(nl)])
            dtr = Rot([self.sb(es, "dt", [128, 64], F32) for _ in range(nl)])
            x0r = Rot([self.sb(es, "x0", [128, 1024], F32) for _ in range(nl)])
            sm32 = Rot([self.sb(es, "s32", [128, 32], F32) for _ in range(9)])
            scr_ = Rot([self.sb(es, "sc", [128, 96], F32) for _ in range(3)])
            rhsUr = Rot([self.sb_split(es, "rhsU", 4096, BF16, 2) for _ in range(1)])
            MTar = Rot([self.sb_split(es, "MTall", 4096, BF16, 8) for _ in range(2)])
            ydr = Rot([self.sb_split(es, "yd", 2048, F32, 4) for _ in range(2)])
            cbmr = Rot([self.sb_split(es, "cbm", 1024, BF16, 2) for _ in range(2)])
            xdtr = Rot([self.sb(es, "xdt", [128, 2048], BF16) for _ in range(2)])
            xdter = Rot([self.sb(es, "xdte", [128, 2048], BF16) for _ in range(2)])
            Er = Rot([self.sb(es, "E", [128, 512], BF16) for _ in range(2)])
            h, hP = self.sb_split(es, "h", 2048, F32, 4)
            hbf, hbP = self.sb_split(es, "hbf", 2048, BF16, 4)
            S.op("pool", lambda e: e.memset(h[:], 0.0), writes=hP)
            S.op("pool", lambda e: e.memset(hbf[:], 0.0), writes=hbP)
            tmpr = Rot([self.sb(es, "tmp", [128, 512], F32) for _ in range(2)])
            br = Rot([self.sb(es, "b", [128, 512], F32) for _ in range(2)])
            yaccr = Rot([self.sb_split(es, "yacc", 2048, F32, 4) for _ in range(1 if final else 2)])
            junk = self.sb(es, "junk", [128, 1024], BF16)
            small = Rot([self.sb(es, "sm", [128, 1], F32) for _ in range(12)])
            if final:
                ybr = Rot([self.sb(es, "yb", [128, 2048], BF16) for _ in range(nl)])
                zsr = Rot([self.sb(es, "zs", [128, 2048], BF16) for _ in range(nl)])
                yn = self.sb(es, "yn", [128, 2048], BF16)
                ynT = self.sb(es, "ynT", [128, 16, 128], BF16)
                ho = self.sb(es, "ho", [128, 1024], F32)
                xo = Rot([self.sb(es, "xo", [128, 1024], F32) for _ in range(2)])
                sm8 = Rot([self.sb(es, "s8", [128, 8], F32) for _ in range(6)])
            else:
                ybsr = Rot([self.sb(es, "ybs", [128, 2048], BF16) for _ in range(2)])
                hnt = self.sb(es, "hnt", [128, 1024], BF16)
                hnT = self.sb(es, "hnT", [128, 8, 128], BF16)
                zsr = Rot([self.sb(es, "zso", [128, 2048], BF16) for _ in range(2)])
            pg = Rot([self.ps(es, "pg", [128, 512], F32) for _ in range(6)])
            pT = self.ps(es, "pT", [128, 8, 128], BF16)
            psm = self.ps(es, "psm", [128, 512], F32)
            psr = Rot([View(psm[:, 96 * i:96 * (i + 1)], psm) for i in range(4)])
            ones1 = self.cf[:, 5, 0:1]
            order = list(range(NCH)) if final else list(range(NCH - 1, -1, -1))
            v3 = lambda ap: ap.rearrange("p (h d) -> p h d", h=8)

            def prep(c, P):
                b = c // 4; tk = c * 128
                xs = xsr.next(); bt = btr.next(); BT = BTr.next(); CT = CTr.next(); dt = dtr.next(); x0 = x0r.next()
                self.load(xs, xs[:], scr["xs"][tk:tk + 128, :])
                self.load(bt, bt[:], scr["Bt"][tk:tk + 128, :])
                self.load(BT, BT[:], scr["BT"][:, :, tk:tk + 128])
                self.load(CT, CT[:], scr["CT"][:, :, tk:tk + 128])
                self.load(dt, dt[:], scr["dt"][tk:tk + 128, :])
                self.load(x0, x0[:], x_d[tk:tk + 128, :])
                if final:
                    yb = ybr.next(); self.load(yb, yb[:], scr["yb"][tk:tk + 128, :])
                    zs = zsr.next(); self.load(zs, zs[:], scr["zs"][tk:tk + 128, :])
                    P["yb"] = yb; P["zs"] = zs
                    flag = self.fl[:, 2 * b + 1:2 * b + 2] if c % 4 == 3 else ones1
                else:
                    flag = self.fl[:, 2 * b:2 * b + 1] if c % 4 == 0 else ones1
                yield
                dtd = dt[:, d0:d0 + 32]
                dA = sm32.next()
                self.tt("dve", dA, dA[:], dt, dtd, Abc, Abc[:], ALU.mult)
                ps_ = psr.next()
                self.mm(ps_, ps_[:, 0:32], self.cf, self.cf[:, qi, :], dA, dA[:], True, True)
                self.mm(ps_, ps_[:, 32:64], self.cf, self.cf[:, ei, :], dA, dA[:], True, True)
                self.mm(ps_, ps_[:, 64:96], self.cf, self.cf[:, 5, :], dA, dA[:], True, True)
                sc = scr_.next()
                self.act(sc, sc[:], ps_, ps_[:], AF.Exp)
                w2 = sm32.next(); cdf = sm32.next()
                self.tt("dve", w2, w2[:], dt, dtd, sc, sc[:, 32:64], ALU.mult)
                self.ts(cdf, cdf[:], sc, sc[:, 64:96], flag, None, ALU.mult, extra=[self.fl, self.cf])
                yield
                xdt = xdtr.next(); xdte = xdter.next(); rhsU, rhsUP = rhsUr.next(); cbm, cbmP = cbmr.next()
                xs3 = xs[:].rearrange("p (h d) -> p h d", h=32)
                self.tt("pool", xdt, xdt[:].rearrange("p (h d) -> p h d", h=32), xs, xs3, dt,
                        dtd.unsqueeze(2).broadcast_to([128, 32, 64]), ALU.mult)
                self.tt("dve", xdte, xdte[:].rearrange("p (h d) -> p h d", h=32), xs, xs3, w2,
                        w2[:].unsqueeze(2).broadcast_to([128, 32, 64]), ALU.mult)
                yield
                for hf in range(2):
                    self.tt("pool", rhsUP[hf], rhsU[:, hf * 2048:(hf + 1) * 2048].rearrange("p (h t) -> p h t", h=16),
                            self.cf, self.cf[:, mk_u, :].unsqueeze(1).broadcast_to([128, 16, 128]),
                            dA, dA[:, hf * 16:(hf + 1) * 16].unsqueeze(2).broadcast_to([128, 16, 128]), ALU.mult)
                yield
                for hf in range(2):
                    pcb = pg.next()
                    for gg in range(4):
                        g = hf * 4 + gg
                        self.mm(pcb, pcb[:, gg * 128:(gg + 1) * 128], BT, BT[:, g, :], CT, CT[:, g, :], True, True)
                    self.tt("dve", cbmP[hf], cbm[:, hf * 512:(hf + 1) * 512].rearrange("p (g t) -> p g t", g=4),
                            pcb, pcb[:].rearrange("p (g t) -> p g t", g=4),
                            self.cb, self.cb[:, mk_c, :].unsqueeze(1).broadcast_to([128, 4, 128]), ALU.mult)
                yield
                MTa, MTP = MTar.next(); yd, ydP = ydr.next()

                def segA(g):
                    pseg = pg.next()
                    self.mm(pseg, pseg[:], self.cb, self.cb[:, mk_l, :], rhsUP[g // 4], rhsU[:, g * 512:(g + 1) * 512], True, True)
                    E = Er.next()
                    self.act(E, E[:], pseg, pseg[:], AF.Exp)
                    self.tt("dve", MTP[g], MTa[:, g * 512:(g + 1) * 512].rearrange("p (h t) -> p h t", h=4), E, E[:].rearrange("p (h t) -> p h t", h=4),
                            cbmP[g // 4], cbm[:, g * 128:(g + 1) * 128].unsqueeze(1).broadcast_to([128, 4, 128]), ALU.mult)

                pys = {}

                def ydB(g):
                    pr, gg = divmod(g, 2)
                    if gg == 0:
                        pys[pr] = pg.next()
                    py = pys[pr]
                    for hh in range(4):
                        hd = 4 * g + hh
                        self.mm(py, py[:, (gg * 4 + hh) * 64:(gg * 4 + hh + 1) * 64], MTP[g], MTa[:, hd * 128:(hd + 1) * 128],
                                xdt, xdt[:, hd * 64:(hd + 1) * 64], True, True)
                    if gg == 0:
                        return
                    c0 = pr * 512; h0 = pr * 8
                    if final:
                        self.tt("dve", ydP[pr], yd[:, c0:c0 + 512], py, py[:], yb, yb[:, c0:c0 + 512], ALU.add)
                    else:
                        self.copy("act", ydP[pr], yd[:, c0:c0 + 512], py, py[:])

                segA(0); segA(1)
                yield
                for g in range(2, 8):
                    segA(g)
                    ydB(g - 2)
                    yield
                ydB(6); ydB(7)
                P.update(xs=xs, bt=bt, CT=CT, sc=sc, cdf=cdf, flag=flag, xdte=xdte, yd=yd, ydP=ydP, x0=x0, tk=tk)

            def main(c, P):
                bt = P["bt"]; CT = P["CT"]; sc = P["sc"]; cdf = P["cdf"]; flag = P["flag"]
                xdte = P["xdte"]; yd = P["yd"]; ydP = P["ydP"]; x0 = P["x0"]; tk = P["tk"]
                yacc, yaP = yaccr.next()
                for pr in range(4):
                    pyo = pg.next(); pS = pg.next()
                    for gg in range(2):
                        g = 2 * pr + gg
                        self.mm(pyo, pyo[:, gg * 256:(gg + 1) * 256], CT, CT[:, g, :], hbP[pr], hbf[:, g * 256:(g + 1) * 256], True, True)
                    for gg in range(2):
                        g = 2 * pr + gg
                        self.mm(pS, pS[:, gg * 256:(gg + 1) * 256], bt, bt[:, g * 128:(g + 1) * 128],
                                xdte, xdte[:, g * 256:(g + 1) * 256], True, True)
                    c0 = pr * 512; h0 = pr * 8
                    tmp = tmpr.next()
                    self.tt("dve", tmp, v3(tmp[:]), hP[pr], v3(h[:, c0:c0 + 512]), cdf, cdf[:, h0:h0 + 8].unsqueeze(2).broadcast_to([128, 8, 64]), ALU.mult)
                    self.stt(hP[pr], h[:, c0:c0 + 512], pS, pS[:], None, flag, tmp, tmp[:], ALU.mult, ALU.add)
                    S.ops["dve"][-1].deps.extend(self.S._deps("dve", [self.fl, self.cf], []))
                    self.copy("act", hbP[pr], hbf[:, c0:c0 + 512], hP[pr], h[:, c0:c0 + 512])
                    bq = br.next()
                    self.tt("dve", bq, v3(bq[:]), pyo, v3(pyo[:]), sc, sc[:, h0:h0 + 8].unsqueeze(2).broadcast_to([128, 8, 64]), ALU.mult)
                    self.tt("pool", yaP[pr], yacc[:, c0:c0 + 512], ydP[pr], yd[:, c0:c0 + 512], bq, bq[:], ALU.add)
                    yield
                if not final:
                    ybs = ybsr.next(); xs = P["xs"]
                    S.op("pool", lambda e, xs=xs: e.tensor_tensor(dx[:].rearrange("p (h d) -> p h d", h=32), xs[:].rearrange("p (h d) -> p h d", h=32),
                         Dbc[:].unsqueeze(2).broadcast_to([128, 32, 64]), ALU.mult), reads=[xs, Dbc], writes=[dx])
                    S.op("pool", lambda e, ybs=ybs, yacc=yacc: e.tensor_tensor(ybs[:], yacc[:], dx[:], ALU.add), reads=yaP + [dx], writes=[ybs])
                    self.store(ybs, ybs[:], scr["yb"][tk:tk + 128, :], q="pool")
                    yield
                    self.rms_tile(x0, 128, wbc, hnt, small, junk)
                    for k in range(8):
                        self.tr(pT, pT[:, k, :], hnt, hnt[:, k * 128:(k + 1) * 128], self.cb, self.cb[:, 0, :])
                    self.copy("dve", hnT, hnT[:], pT, pT[:])
                    yield
                    zs = zsr.next()
                    for q in range(4):
                        pz = pg.next()
                        for k in range(8):
                            self.mm(pz, pz[:], hnT, hnT[:, k, :], Wz, Wz[:, k, 512 * q:512 * (q + 1)], k == 0, k == 7)
                        self.act(zs, zs[:, 512 * q:512 * (q + 1)], pz, pz[:], AF.Silu)
                        yield
                    self.store(zs, zs[:], scr["zs"][tk:tk + 128, :], q="pool")
                    return
                zs = P["zs"]
                S.op("dve", lambda e, yacc=yacc, zs=zs: e.tensor_tensor(yacc[:], yacc[:], zs[:], ALU.mult), reads=yaP + [zs], writes=yaP)
                yield
                ssg = sm8.next(); lg = sm8.next(); rg = sm8.next()
                for g in range(8):
                    self.act(yn, yn[:, g * 256:(g + 1) * 256], yaP[g // 2], yacc[:, g * 256:(g + 1) * 256], AF.Square,
                             accum=(ssg, ssg[:, g:g + 1]))
                self.act(lg, lg[:], ssg, ssg[:], AF.Ln, bias=self.epsb[:, 0:1], scale=1.0 / 256, extra=[self.epsb])
                self.act(rg, rg[:], lg, lg[:], AF.Exp, scale=-0.5)
                yield
                S.op("dve", lambda e, yacc=yacc, rg=rg: e.tensor_tensor(yn[:].rearrange("p (g d) -> p g d", g=8), yacc[:].rearrange("p (g d) -> p g d", g=8),
                     rg[:].unsqueeze(2).broadcast_to([128, 8, 256]), ALU.mult), reads=yaP + [rg], writes=[yn])
                for hf in range(2):
                    for k in range(8):
                        kk = hf * 8 + k
                        self.tr(pT, pT[:, k, :], yn, yn[:, kk * 128:(kk + 1) * 128], self.cb, self.cb[:, 0, :])
                    self.copy("act" if hf else "dve", ynT, ynT[:, hf * 8:(hf + 1) * 8, :], pT, pT[:])
                    yield
                for q in range(2):
                    po = pg.next()
                    for kk in range(16):
                        self.mm(po, po[:], ynT, ynT[:, kk, :], Wo, Wo[:, kk, 512 * q:512 * (q + 1)], kk == 0, kk == 15)
                    self.copy("act" if q else "dve", ho, ho[:, 512 * q:512 * (q + 1)], po, po[:])
                    yield
                self.post_norm_residual(ho, x0, wpo, xo.next(), junk, small, xout_d[tk:tk + 128, :])

            P = {}
            for _ in prep(order[0], P):
                pass
            for idx, c in enumerate(order):
                Pn = {}
                gens = [main(c, P)]
                if idx + 1 < len(order):
                    gens.append(prep(order[idx + 1], Pn))
                while gens:
                    for gnr in list(gens):
                        try:
                            next(gnr)
                        except StopIteration:
                            gens.remove(gnr)
                P = Pn
        self.end_phase()

    def post_norm_residual(self, hb, x0, wpo, xo, junk, small, dst):
        ss = small.next(); lg = small.next(); rs = small.next()
        self.act(junk, junk[:, 0:1024], hb, hb[:], AF.Square, accum=(ss, ss[:, 0:1]))
        self.act(lg, lg[:], ss, ss[:], AF.Ln, bias=self.epsb[:, 0:1], scale=1.0 / D, extra=[self.epsb])
        self.act(rs, rs[:], lg, lg[:], AF.Exp, scale=-0.5)
        self.stt(xo, xo[:], hb, hb[:], rs, rs[:, 0:1], wpo, wpo[:], ALU.mult, ALU.mult)
        self.tt("pool", xo, xo[:], xo, xo[:], x0, x0[:], ALU.add)
        self.store(xo, xo[:], dst, q="pool")

    def norm_a(self, x_d, b, halo, wbc, xt, hnt, small, junk):
        S = self.S; T = self.T; t0 = b * 512
        tiles = []
        for j in range(5):
            x = xt.next()
            if j < 4:
                self.load(x, x[:], x_d[t0 + 128 * j:t0 + 128 * (j + 1), :]); npart = 128
            else:
                lo = t0 - halo if t0 - halo >= 0 else 0
                hi = t0 + 512 if t0 + 512 + halo <= T else T - halo
                S.dma_load("sp", x, lambda e, s, x=x, lo=lo, hi=hi: (
                    e.dma_start(out=x[0:halo, :], in_=x_d[lo:lo + halo, :]).then_inc(s, 16),
                    e.dma_start(out=x[halo:2 * halo, :], in_=x_d[hi:hi + halo, :]).then_inc(s, 16)), n=2)
                npart = 2 * halo
            ht = hnt.next()
            self.rms_tile(x, npart, wbc, ht, small, junk)
            tiles.append((ht, npart))
        return tiles

    def norm_b(self, tiles, halo, hn, pTr):
        for j, (ht, npart) in enumerate(tiles):
            pT = pTr.next()
            if j < 4:
                for k in range(8):
                    self.tr(pT, pT[:, k, :], ht, ht[:, k * 128:(k + 1) * 128], self.cb, self.cb[:, 0, :])
                self.copy("act" if j % 2 else "dve", hn, hn[:, :, halo + 128 * j:halo + 128 * (j + 1)], pT, pT[:])
            else:
                for k in range(8):
                    self.tr(pT, pT[:, k, 0:npart], ht, ht[0:npart, k * 128:(k + 1) * 128], self.cb, self.cb[0:npart, 0, 0:npart])
                self.copy("dve", hn, hn[:, :, 0:halo], pT, pT[:, :, 0:halo])
                self.copy("dve", hn, hn[:, :, 512 + halo:512 + 2 * halo], pT, pT[:, :, halo:2 * halo])

    def norm_block(self, x_d, b, halo, wbc, hn, xt, hnt, small, junk, pTr):
        self.norm_b(self.norm_a(x_d, b, halo, wbc, xt, hnt, small, junk), halo, hn, pTr)

    def ffn(self, xin_d, xout_d, up_w, cw_d, cb_d, down_w, npre_d, npost_d, wdbf):
        S = self.S; NB = self.NB
        with ExitStack() as es:
            Wu = self.sb(es, "Wu", [128, 8, 5632], BF16)
            cw = self.sb(es, "cw", [128, 132], F32); self.load(cw, cw[:], cw_d)
            cbias = self.sb(es, "cbias", [128, 44], F32); self.load(cbias, cbias[:], cb_d)
            wbc = self.sb(es, "wbc", [128, 1024], F32); self.load(wbc, wbc[:], npre_d.partition_broadcast(128))
            wpo = self.sb(es, "wpo", [128, 1024], F32); self.load(wpo, wpo[:], npost_d.partition_broadcast(128))
            with ExitStack() as es2:
                stage = Rot([self.sb(es2, "stg", [128, 1024], F32) for _ in range(3)])
                stb = Rot([self.sb(es2, "stb", [128, 1024], BF16) for _ in range(2)])
                self.load_weight(es2, Wu, 0, up_w, 1024, 5632, stage)
                for kk in range(22):
                    st = stage.next(); sb_ = stb.next()
                    self.load(st, st[:], down_w[kk * 128:(kk + 1) * 128, :])
                    self.copy(("act", "dve")[kk % 2], sb_, sb_[:], st, st[:])
                    self.store(sb_, sb_[:], wdbf[kk], q="pool")
                S.barrier()
            xt = Rot([self.sb(es, "xt", [128, 1024], F32) for _ in range(3)])
            xrr = Rot([self.sb(es, "xr", [128, 1024], F32) for _ in range(3)])
            hnt = Rot([self.sb(es, "hnt", [128, 1024], BF16) for _ in range(5)])
            junk = self.sb(es, "junk", [128, 1024], BF16)
            small = Rot([self.sb(es, "sm", [128, 1], F32) for _ in range(12)])
            hnr = Rot([self.sb(es, "hn", [128, 8, 514], BF16) for _ in range(2)])
            ur = Rot([self.sb(es, "u", [128, 514], BF16) for _ in range(3)])
            ar = Rot([self.sb(es, "a", [128, 512], BF16) for _ in range(2)])
            dgr = Rot([self.sb(es, "dg", [128, 128], BF16) for _ in range(12)])
            wdr = Rot([self.sb(es, "wd", [128, 512], BF16) for _ in range(4)])
            hact = self.sb(es, "hact", [128, 22, 512], BF16)
            hos = [self.sb(es, "ho", [128, 1024], F32) for _ in range(4)]
            pg = Rot([self.ps(es, "pg", [128, 512], F32) for _ in range(5)])
            pTr = Rot([self.ps(es, "pT", [128, 8, 128], BF16)])
            phbs = [self.ps(es, "ph", [128, 512], F32) for _ in range(2)]
            phr = Rot([View(pb[:, 0:2], pb) for pb in phbs])
            hn_next = hnr.next()
            self.norm_block(xin_d, 0, 1, wbc, hn_next, xt, hnt, small, junk, pTr)
            for b in range(NB):
                t0 = b * 512
                hn = hn_next
                us = {}
                av = {}

                def projA(kk, part):
                    m = kk + 22 * part
                    pm = pg.next(); ph = phr.next()
                    for k in range(8):
                        self.mm(pm, pm[:], Wu, Wu[:, k, 128 * m:128 * (m + 1)], hn, hn[:, k, 1:513], k == 0, k == 7)
                    for k in range(8):
                        self.mm(ph, ph[:], Wu, Wu[:, k, 128 * m:128 * (m + 1)], hn, hn[:, k, 0:514:513], k == 0, k == 7)
                    u = ur.next()
                    self.copy("act" if part else "dve", u, u[:, 1:513], pm, pm[:])
                    self.tt("dve", u, u[:, 0:514:513], ph, ph[:], self.fl, self.fl[:, 2 * b:2 * b + 2], ALU.mult)
                    us[(kk, part)] = u

                def convB(kk, part):
                    m = kk + 22 * part
                    u = us.pop((kk, part))
                    pc = pg.next()
                    for k in range(3):
                        dg = dgr.next()
                        self.tt("pool", dg, dg[:], self.cb, self.cb[:, 0, :], cw, cw[:, k * 44 + m:k * 44 + m + 1].broadcast_to([128, 128]), ALU.mult)
                        self.mm(pc, pc[:], dg, dg[:], u, u[:, k:k + 512], k == 0, k == 2)
                    if part == 0:
                        a = ar.next(); av[kk] = a
                        self.act(a, a[:], pc, pc[:], AF.Silu, bias=cbias[:, m:m + 1], extra=[cbias])
                    else:
                        a = av.pop(kk)
                        self.stt(hact, hact[:, kk, :], pc, pc[:], cbias, cbias[:, m:m + 1], a, a[:], ALU.add, ALU.mult)

                tiles = [(kk, part) for kk in range(22) for part in range(2)]
                projA(*tiles[0])
                for ti in range(1, len(tiles)):
                    projA(*tiles[ti])
                    convB(*tiles[ti - 1])
                convB(*tiles[-1])
                nxt = self.norm_a(xin_d, b + 1, 1, wbc, xt, hnt, small, junk) if b + 1 < NB else None
                for q in range(2):
                    pos = [pg.next() for _ in range(4)]
                    for kk in range(22):
                        wd = wdr.next()
                        self.load(wd, wd[:], wdbf[kk][:, 512 * q:512 * (q + 1)])
                        for j in range(4):
                            self.mm(pos[j], pos[j][:], hact, hact[:, kk, 128 * j:128 * (j + 1)], wd, wd[:], kk == 0, kk == 21)
                    for j in range(4):
                        self.copy("act" if j % 2 else "dve", hos[j], hos[j][:, 512 * q:512 * (q + 1)], pos[j], pos[j][:])
                if nxt is not None:
                    hn_next = hnr.next()
                    self.norm_b(nxt, 1, hn_next, pTr)
                sss = [small.next() for _ in range(4)]
                for j in range(4):
                    self.act(hact, hact[:, 2 * j:2 * j + 2, :].rearrange("p a b -> p (a b)"), hos[j], hos[j][:], AF.Square, accum=(sss[j], sss[j][:, 0:1]))
                for j in range(4):
                    self.act(sss[j], sss[j][:], sss[j], sss[j][:], AF.Ln, bias=self.epsb[:, 0:1], scale=1.0 / D, extra=[self.epsb])
                for j in range(4):
                    self.act(sss[j], sss[j][:], sss[j], sss[j][:], AF.Exp, scale=-0.5)
                for j in range(4):
                    x0 = xrr.next()
                    self.load(x0, x0[:], xin_d[t0 + 128 * j:t0 + 128 * (j + 1), :])
                    self.stt(hos[j], hos[j][:], hos[j], hos[j][:], sss[j], sss[j][:, 0:1], wpo, wpo[:], ALU.mult, ALU.mult)
                    self.tt("pool", hos[j], hos[j][:], hos[j], hos[j][:], x0, x0[:], ALU.add)
                    self.store(hos[j], hos[j][:], xout_d[t0 + 128 * j:t0 + 128 * (j + 1), :], q="pool")
        self.end_phase()

    def conformer(self, xin_d, xout_d, pw1_w, pw1b_d, dw_d, dwb_d, lnw_d, lnb_d, pw2_w, pw2b_d, npre_d, npost_d):
        S = self.S; NB = self.NB; H = 15
        with ExitStack() as es:
            W1 = self.sb(es, "W1c", [128, 8, 2048], BF16)
            W2 = self.sb(es, "W2c", [128, 8, 1024], BF16)
            with ExitStack() as es2:
                stage = Rot([self.sb(es2, "stg", [128, 1024], F32) for _ in range(3)])
                self.load_weight(es2, W1, 0, pw1_w, 1024, 2048, stage)
                self.load_weight(es2, W2, 0, pw2_w, 1024, 1024, stage)
                S.barrier()
            b1 = self.sb(es, "b1", [128, 16], F32); self.load(b1, b1[:], pw1b_d)
            dw = self.sb(es, "dw", [128, 248], F32); self.load(dw, dw[:], dw_d)
            dwb = self.sb(es, "dwb", [128, 8], F32); self.load(dwb, dwb[:], dwb_d)
            lnw = self.sb(es, "lnw", [128, 8], F32); self.load(lnw, lnw[:], lnw_d)
            lnb = self.sb(es, "lnb", [128, 8], F32); self.load(lnb, lnb[:], lnb_d)
            wbc = self.sb(es, "wbc", [128, 1024], F32); self.load(wbc, wbc[:], npre_d.partition_broadcast(128))
            wpo = self.sb(es, "wpo", [128, 1024], F32); self.load(wpo, wpo[:], npost_d.partition_broadcast(128))
            b2bc = self.sb(es, "b2bc", [128, 1024], F32); self.load(b2bc, b2bc[:], pw2b_d.partition_broadcast(128))
            onesN = self.sb(es, "onesN", [128, 128], BF16)
            self.ts(onesN, onesN[:], self.cf, self.cf[:, 5, :], 1.0 / 1024, None, ALU.mult)
            xt = Rot([self.sb(es, "xt", [128, 1024], F32) for _ in range(3)])
            xrr = Rot([self.sb(es, "xr", [128, 1024], F32) for _ in range(2)])
            hnt = Rot([self.sb(es, "hnt", [128, 1024], BF16) for _ in range(5)])
            junk = self.sb(es, "junk", [128, 1024], BF16)
            small = Rot([self.sb(es, "sm", [128, 1], F32) for _ in range(12)])
            hnr = Rot([self.sb(es, "hn", [128, 8, 542], BF16) for _ in range(2)])
            glu = self.sb(es, "glu", [128, 8, 542], BF16)
            sgr = Rot([self.sb(es, "sg", [128, 512], F32) for _ in range(2)])
            s30 = Rot([self.sb(es, "s30", [128, 30], F32) for _ in range(4)])
            dgr = Rot([self.sb(es, "dg", [128, 128], BF16) for _ in range(16)])
            vf = self.sb(es, "vf", [128, 8, 512], F32)
            vb = self.sb(es, "vb", [128, 8, 512], BF16)
            vsq = self.sb(es, "vsq", [128, 8, 512], BF16)
            mean = self.sb(es, "mean", [128, 512], F32)
            var = self.sb(es, "var", [128, 512], F32)
            t1r = Rot([self.sb(es, "t1", [128, 512], F32) for _ in range(2)])
            sact = self.sb(es, "sact", [128, 8, 512], BF16)
            hos = [self.sb(es, "ho", [128, 1024], F32) for _ in range(4)]
            xo = Rot([self.sb(es, "xo", [128, 1024], F32) for _ in range(2)])
            pg = Rot([self.ps(es, "pg", [128, 512], F32) for _ in range(5)])
            pTr = Rot([self.ps(es, "pT", [128, 8, 128], BF16)])
            phbs = [self.ps(es, "ph", [128, 512], F32) for _ in range(2)]
            phr = Rot([View(pb[:, 32 * i:32 * i + 30], pb) for pb in phbs for i in range(2)])
            for b in range(NB):
                t0 = b * 512
                hn = hnr.next()
                self.norm_block(xin_d, b, H, wbc, hn, xt, hnt, small, junk, pTr)
                for m in range(8):
                    pa = pg.next(); pgt = pg.next(); pha = phr.next(); phg = phr.next()
                    for (pmain, phal, c0) in ((pa, pha, 128 * m), (pgt, phg, 1024 + 128 * m)):
                        for k in range(8):
                            self.mm(pmain, pmain[:], W1, W1[:, k, c0:c0 + 128], hn, hn[:, k, H:H + 512], k == 0, k == 7)
                        for k in range(8):
                            self.mm(phal, phal[:, 0:H], W1, W1[:, k, c0:c0 + 128], hn, hn[:, k, 0:H], k == 0, k == 7)
                        for k in range(8):
                            self.mm(phal, phal[:, H:2 * H], W1, W1[:, k, c0:c0 + 128], hn, hn[:, k, 512 + H:512 + 2 * H], k == 0, k == 7)
                    sg = sgr.next()
                    self.act(sg, sg[:], pgt, pgt[:], AF.Sigmoid, bias=b1[:, 8 + m:9 + m], extra=[b1])
                    self.stt(glu, glu[:, m, H:H + 512], pa, pa[:], b1, b1[:, m:m + 1], sg, sg[:], ALU.add, ALU.mult)
                    sh = s30.next(); th = s30.next()
                    self.act(sh, sh[:], phg, phg[:], AF.Sigmoid, bias=b1[:, 8 + m:9 + m], extra=[b1])
                    self.stt(th, th[:], pha, pha[:], b1, b1[:, m:m + 1], sh, sh[:], ALU.add, ALU.mult)
                    self.tt("dve", glu, glu[:, m, 0:H], th, th[:, 0:H], self.fl, self.fl[:, 2 * b:2 * b + 1].broadcast_to([128, H]), ALU.mult)
                    self.tt("dve", glu, glu[:, m, 512 + H:512 + 2 * H], th, th[:, H:2 * H], self.fl,
                            self.fl[:, 2 * b + 1:2 * b + 2].broadcast_to([128, H]), ALU.mult)
                for m in range(8):
                    pc = pg.next()
                    for k in range(31):
                        dg = dgr.next()
                        col = dw[:, k * 8 + m:k * 8 + m + 1]
                        if k % 3 == 0:
                            self.tt("pool", dg, dg[:], self.cb, self.cb[:, 0, :], dw, col.broadcast_to([128, 128]), ALU.mult)
                        elif k % 3 == 1:
                            self.ts(dg, dg[:], self.cb, self.cb[:, 0, :], col, None, ALU.mult, extra=[dw])
                        else:
                            self.S.op("act", lambda e, dg=dg, col=col: e.activation(dg[:], self.cb[:, 0, :], AF.Copy, scale=col),
                                      reads=[self.cb, dw], writes=[dg])
                        self.mm(pc, pc[:], dg, dg[:], glu, glu[:, m, k:k + 512], k == 0, k == 30)
                    self.act(vf, vf[:, m, :], pc, pc[:], AF.Identity, bias=dwb[:, m:m + 1], extra=[dwb])
                    self.act(vsq, vsq[:, m, :], pc, pc[:], AF.Square, bias=dwb[:, m:m + 1], extra=[dwb])
                    self.copy("dve", vb, vb[:, m, :], vf, vf[:, m, :])
                pmean = pg.next(); pmsq = pg.next()
                for m in range(8):
                    self.mm(pmean, pmean[:], onesN, onesN[:], vb, vb[:, m, :], m == 0, m == 7)
                for m in range(8):
                    self.mm(pmsq, pmsq[:], onesN, onesN[:], vsq, vsq[:, m, :], m == 0, m == 7)
                self.copy("dve", mean, mean[:], pmean, pmean[:])
                self.tt("pool", var, var[:], mean, mean[:], mean, mean[:], ALU.mult)
                self.tt("dve", var, var[:], pmsq, pmsq[:], var, var[:], ALU.subtract)
                self.ts(var, var[:], var, var[:], 0.0, None, ALU.max)
                self.act(var, var[:], var, var[:], AF.Ln, bias=self.epsb[:, 0:1], extra=[self.epsb])
                self.act(var, var[:], var, var[:], AF.Exp, scale=-0.5)
                for m in range(8):
                    t1 = t1r.next()
                    self.tt("pool", t1, t1[:], vf, vf[:, m, :], mean, mean[:], ALU.subtract)
                    self.tt("dve", t1, t1[:], t1, t1[:], var, var[:], ALU.mult)
                    self.S.op("act", lambda e, t1=t1, m=m: e.activation(sact[:, m, :], t1[:], AF.Silu, bias=lnb[:, m:m + 1], scale=lnw[:, m:m + 1]),
                              reads=[t1, lnb, lnw], writes=[sact])
                for j in range(4):
                    for q in range(2):
                        po = pg.next()
                        for m in range(8):
                            self.mm(po, po[:], sact, sact[:, m, 128 * j:128 * (j + 1)], W2, W2[:, m, 512 * q:512 * (q + 1)], m == 0, m == 7)
                        self.tt("dve", hos[j], hos[j][:, 512 * q:512 * (q + 1)], po, po[:], b2bc, b2bc[:, 512 * q:512 * (q + 1)], ALU.add)
                    x0 = xrr.next()
                    self.load(x0, x0[:], xin_d[t0 + 128 * j:t0 + 128 * (j + 1), :])
                    self.post_norm_residual(hos[j], x0, wpo, xo.next(), junk, small, xout_d[t0 + 128 * j:t0 + 128 * (j + 1), :])
        self.end_phase()


def build(NB, dbg=(), upto=6):
    B = Builder(NB, dbg)
    T = B.T
    x_d = B.din("x", [T, D])
    w_in = B.din("ssm_in_w", [D, 6208])
    cw1 = B.din("ssm_cw", [128, 96]); cb1 = B.din("ssm_cb", [128, 32]); dtb = B.din("ssm_dtb", [64, 1])
    alog = B.din("ssm_a_log", [2, 32]); dsk = B.din("ssm_d", [1, 32]); gnw = B.din("ssm_gnw", [128, 16])
    w_out = B.din("ssm_out_w", [DI, D])
    npm = B.din("norm_pre_mix", [2, D]); npo = B.din("norm_post_mix", [2, D])
    npf = B.din("norm_pre_ffn", [2, D]); npof = B.din("norm_post_ffn", [2, D])
    up_w = B.din("ffn_up_w", [2, D, 2 * FF]); dn_w = B.din("ffn_down_w", [2, FF, D])
    fcw = B.din("ffn_cw", [2, 128, 132]); fcb = B.din("ffn_cb", [2, 128, 44])
    pw1 = B.din("cf_pw1_w", [D, 2 * D]); pw1b = B.din("cf_pw1_b", [128, 16])
    dww = B.din("cf_dw", [128, 248]); dwb = B.din("cf_dwb", [128, 8])
    lnw = B.din("cf_lnw", [128, 8]); lnb = B.din("cf_lnb", [128, 8])
    pw2 = B.din("cf_pw2_w", [D, D]); pw2b = B.din("cf_pw2_b", [1, D])
    scr = {
        "xs": B.dscr("xs", [T, DI], BF16), "Bt": B.dscr("Bt", [T, 1024], BF16),
        "BT": B.dscr("BT", [128, 8, T], BF16), "CT": B.dscr("CT", [128, 8, T], BF16),
        "dt": B.dscr("dt", [T, 64], F32), "yb": B.dscr("yb", [T, DI], BF16), "zs": B.dscr("zs", [T, DI], BF16),
    }
    x1 = B.dscr("x1", [T, D], F32); x2 = B.dscr("x2", [T, D], F32); x3 = B.dscr("x3", [T, D], F32)
    wdbf = B.dscr("wdbf", [22, 128, D], BF16)
    y_d = B.nc.dram_tensor("y", [T, D], F32, kind="ExternalOutput").ap()
    with ExitStack() as es:
        B.setup_consts(es)
        B.cur_bufs = []
        B.phase1(x_d, w_in, cw1, cb1, dtb, npm[0:1, :], scr)
        if upto >= 2:
            B.sweep(False, scr, alog[1:2, :], x_d=x_d, w_in=w_in, npre_d=npm[0:1, :], d_d=dsk)
        if upto >= 3:
            B.sweep(True, scr, alog[0:1, :], x_d=x_d, xout_d=x1, w_in=w_in, w_out=w_out, d_d=dsk, gnw_d=gnw,
                    npre_d=npm[0:1, :], npost_d=npo[0:1, :])
        if upto >= 4:
            B.ffn(x1, x2, up_w[0], fcw[0], fcb[0], dn_w[0], npf[0:1, :], npof[0:1, :], wdbf)
        if upto >= 5:
            B.conformer(x2, x3, pw1, pw1b, dww, dwb, lnw, lnb, pw2, pw2b, npm[1:2, :], npo[1:2, :])
        if upto >= 6:
            B.ffn(x3, y_d, up_w[1], fcw[1], fcb[1], dn_w[1], npf[1:2, :], npof[1:2, :], wdbf)
        B.S.emit()
    return B


def make_consts():
    r = np.arange(128)
    ident = np.eye(128, dtype=np.float32)
    LE = (r[:, None] <= r[None, :]).astype(np.float32)
    GT = (r[:, None] > r[None, :]).astype(np.float32)
    GE = (r[:, None] >= r[None, :]).astype(np.float32)
    LT = (r[:, None] < r[None, :]).astype(np.float32)
    ones = np.ones((128, 128), np.float32)
    return np.ascontiguousarray(np.concatenate([ident, LE, GT, GE, LT, ones], axis=1))


def colmajor(v, ncol):
    return np.ascontiguousarray(np.asarray(v, np.float32).reshape(ncol, 128).T)


SLOT = 4096
NB_FULL = 24
_CACHE = {}


def _layout_weights(w):
    f = lambda a: np.ascontiguousarray(np.asarray(a, np.float32))
    out = {
        "ssm_in_w": f(w["ssm_in_w"][0]),
        "ssm_cw": colmajor(f(w["ssm_conv_w"][0]).reshape(-1), 96),
        "ssm_cb": colmajor(w["ssm_conv_b"][0], 32),
        "ssm_dtb": f(w["ssm_dt_bias"][0]).reshape(64, 1),
        "ssm_a_log": f(w["ssm_a_log"][0]),
        "ssm_d": f(w["ssm_d"][0]).reshape(1, 32),
        "ssm_gnw": colmajor(w["ssm_norm_w"][0], 16),
        "ssm_out_w": f(w["ssm_out_w"][0]),
        "norm_pre_mix": f(w["norm_pre_mix"]), "norm_post_mix": f(w["norm_post_mix"]),
        "norm_pre_ffn": f(w["norm_pre_ffn"]), "norm_post_ffn": f(w["norm_post_ffn"]),
        "ffn_up_w": f(w["ffn_up_w"]), "ffn_down_w": f(w["ffn_down_w"]),
        "ffn_cw": np.stack([colmajor(f(w["ffn_conv_w"][l]).reshape(-1), 132) for l in range(2)]),
        "ffn_cb": np.stack([colmajor(w["ffn_conv_b"][l], 44) for l in range(2)]),
        "cf_pw1_w": f(w["cf_pw1_w"][0]), "cf_pw1_b": colmajor(w["cf_pw1_b"][0], 16),
        "cf_dw": colmajor(f(w["cf_dw_w"][0]).reshape(-1), 248),
        "cf_dwb": colmajor(w["cf_dw_b"][0], 8), "cf_lnw": colmajor(w["cf_ln_w"][0], 8), "cf_lnb": colmajor(w["cf_ln_b"][0], 8),
        "cf_pw2_w": f(w["cf_pw2_w"][0]), "cf_pw2_b": f(w["cf_pw2_b"][0]).reshape(1, D),
    }
    return out


def kernel(x_prompt, x_sample, **w):
    x_prompt = np.asarray(x_prompt, np.float32); x_sample = np.asarray(x_sample, np.float32)
    plan = []
    for c in range(4):
        plan.append([("p", 3 * c), ("p", 3 * c + 1), ("p", 3 * c + 2)])
    for c in range(2):
        plan.append([("p", 12 + 2 * c), ("p", 13 + 2 * c), None])
    for c in range(2):
        plan.append([("s", c, 0), ("s", c, 1), None])
    wl = _layout_weights(w)
    consts = make_consts()
    in_maps = []
    for c in range(NCORES):
        xs = np.zeros((3 * SLOT, D), np.float32)
        fl = np.ones((NB_FULL, 2), np.float32)
        for si, ent in enumerate(plan[c]):
            b0 = si * 8
            fl[b0, 0] = 0.0; fl[b0 + 7, 1] = 0.0
            if ent is None:
                continue
            if ent[0] == "p":
                xs[si * SLOT:(si + 1) * SLOT] = x_prompt[ent[1]]
            else:
                xs[si * SLOT:(si + 1) * SLOT] = x_sample[ent[1], ent[2] * SLOT:(ent[2] + 1) * SLOT]
                if ent[2] == 0:
                    fl[b0 + 7, 1] = 1.0
                else:
                    fl[b0, 0] = 1.0
        m = dict(wl)
        m["x"] = xs
        m["flags"] = np.ascontiguousarray(np.broadcast_to(fl.reshape(1, -1), (128, NB_FULL * 2)))
        m["consts"] = consts
        in_maps.append(m)
    if "prog" not in _CACHE:
        _CACHE["prog"] = build(NB_FULL)
    res = run_bass_kernel_spmd(_CACHE["prog"].nc, in_maps, core_ids=list(range(NCORES)))
    y_prompt = np.empty_like(x_prompt); y_sample = np.empty_like(x_sample)
    for c in range(NCORES):
        y = np.asarray(res.results[c]["y"], np.float32)
        for si, ent in enumerate(plan[c]):
            if ent is None:
                continue
            if ent[0] == "p":
                y_prompt[ent[1]] = y[si * SLOT:(si + 1) * SLOT]
            else:
                y_sample[ent[1], ent[2] * SLOT:(ent[2] + 1) * SLOT] = y[si * SLOT:(si + 1) * SLOT]
    return (y_prompt, y_sample)
```
